# Optimizing a Trainium2 kernel written in Bass

```python
import jax
import jax.numpy as jnp
from jax import lax
import numpy as np

D_MODEL = 1024
BATCH = 16
SEQ = 4096
DEPTH = 4

CTX_LEN = 256
GRID_W = 64
N_MOD = 6
NORM_EPS = 1e-6
LN_EPS = 1e-5
RW_N = 64
RW_W = D_MODEL // 2
RW_H = RW_W // RW_N
LORA_DECAY = 64
LORA_ICLR = 64
LORA_GATE = 128
RW_GN_EPS = 64e-5
CONV_W = D_MODEL // 2
CONV_K = 31
ML_H = 4
ML_W = D_MODEL // 2
ML_DH = ML_W // ML_H
ML_CHUNK = 64
N_BRANCH = 3
D_FF = 4 * D_MODEL
RW_COLS = 3 * RW_W + 2 * LORA_DECAY + 2 * LORA_ICLR + LORA_GATE
CV_COLS = 2 * CONV_W
ML_COLS = 4 * ML_W + 4 * ML_H
GATE_COLS = N_BRANCH * D_MODEL
OFF_CV = RW_COLS
OFF_ML = OFF_CV + CV_COLS
OFF_GATE = OFF_ML + ML_COLS
P_TOTAL = OFF_GATE + GATE_COLS

kernel_name = 'hybrid_rwkv7_conformer_mlstm_block'


def _split_at(u, sizes):
    return jnp.split(u, [int(s) for s in np.cumsum(sizes)[:-1]], axis=-1)


def _rmsnorm(x, g):
    xf = x.astype(jnp.float32)
    y = xf * lax.rsqrt(jnp.mean(xf * xf, axis=-1, keepdims=True) + NORM_EPS)
    return (y * g.astype(jnp.float32)).astype(x.dtype)


def _layernorm(x, w, b, eps):
    xf = x.astype(jnp.float32)
    xc = xf - jnp.mean(xf, axis=-1, keepdims=True)
    y = xc * lax.rsqrt(jnp.mean(xc * xc, axis=-1, keepdims=True) + eps) * w.astype(jnp.float32)
    if b is not None:
        y = y + b.astype(jnp.float32)
    return y.astype(x.dtype)


def _token_shift(u, mu_prev, mu_next):
    prev = jnp.pad(u, ((0, 0), (1, 0), (0, 0)))[:, :-1]
    nxt = jnp.pad(u, ((0, 0), (0, 1), (0, 0)))[:, 1:]
    return u + mu_prev * (prev - u) + mu_next * (nxt - u)


def _rwkv_feats(u, w0, w2, a0, a2, k_k, k_a):
    B, T, _ = u.shape
    r, k, v, wd_f, wd_b, ad_f, ad_b, gd = _split_at(
        u, (RW_W, RW_W, RW_W, LORA_DECAY, LORA_DECAY, LORA_ICLR, LORA_ICLR, LORA_GATE))

    def heads(t):
        return t.astype(jnp.float32).reshape(B, T, RW_H, RW_N)

    kk = heads(k * k_k)
    kk = kk / jnp.maximum(jnp.sqrt(jnp.sum(kk * kk, axis=-1, keepdims=True)), 1e-12)
    dirs = []
    for d, (wd, ad) in enumerate(((wd_f, ad_f), (wd_b, ad_b))):
        w_log = -jax.nn.softplus(-(w0[d] + jnp.tanh(wd) @ w2[d]).astype(jnp.float32)) - 0.5
        decay = jnp.exp(-jnp.exp(w_log))
        a = jax.nn.sigmoid((a0[d] + ad @ a2[d]).astype(jnp.float32))
        k_d = k.astype(jnp.float32) * (1.0 + (a - 1.0) * k_a)
        dirs.append((heads(decay), heads(k_d), heads(a)))
    return heads(r), heads(k), heads(v), kk, dirs, gd


def _rwkv7_scan(s0, w, k, v, kk, a, r, reverse, emit):
    xs = tuple(jnp.moveaxis(t, 1, 0) for t in (w, k, v, kk, a))
    if emit:
        xs = xs + (jnp.moveaxis(r, 1, 0),)

    def step(S, inp):
        w_t, k_t, v_t, kk_t, a_t = inp[:5]
        S = (S * w_t[:, :, None, :]
             - jnp.einsum('bhvk,bhk->bhv', S, kk_t)[..., None] * (kk_t * a_t)[:, :, None, :]
             + v_t[..., None] * k_t[:, :, None, :])
        y = jnp.einsum('bhvk,bhk->bhv', S, inp[5]) if emit else None
        return S, y

    S, y = lax.scan(step, s0, xs, reverse=reverse)
    return S, (jnp.moveaxis(y, 0, 1) if emit else None)


def _rwkv_out(f, y, r_k, g2, gn_w, gn_b):
    r, k, v, _, _, gd = f
    B, T = r.shape[:2]
    yn = _layernorm(y, gn_w.reshape(RW_H, RW_N), gn_b.reshape(RW_H, RW_N), RW_GN_EPS)
    bonus = jnp.sum(r * k * r_k, axis=-1, keepdims=True) * v
    gate = jax.nn.sigmoid(gd) @ g2
    return (yn + bonus).reshape(B, T, RW_W) * gate


def _rwkv_mixer(u_lat, u_ctx, mu_p, mu_n, w0, w2, a0, a2, k_k, k_a, r_k, g2, gn_w, gn_b, emit_ctx):
    fl = _rwkv_feats(_token_shift(u_lat, mu_p, mu_n), w0, w2, a0, a2, k_k, k_a)
    fc = _rwkv_feats(_token_shift(u_ctx, mu_p, mu_n), w0, w2, a0, a2, k_k, k_a)
    s0 = jnp.zeros((u_lat.shape[0], RW_H, RW_N, RW_N), jnp.float32)
    y_lat, y_ctx = 0.0, 0.0
    for d in range(2):
        rev = d == 1
        w_c, k_c, a_c = fc[4][d]
        s_ctx, yc = _rwkv7_scan(s0, w_c, k_c, fc[2], fc[3], a_c, fc[0], rev, emit_ctx)
        w_l, k_l, a_l = fl[4][d]
        _, yl = _rwkv7_scan(s_ctx, w_l, k_l, fl[2], fl[3], a_l, fl[0], rev, True)
        y_lat = y_lat + yl
        if emit_ctx:
            y_ctx = y_ctx + yc
    out_lat = _rwkv_out(fl, y_lat, r_k, g2, gn_w, gn_b).astype(u_lat.dtype)
    out_ctx = _rwkv_out(fc, y_ctx, r_k, g2, gn_w, gn_b).astype(u_ctx.dtype) if emit_ctx else None
    return out_lat, out_ctx


def _depthwise_conv(z, w, b):
    y = lax.conv_general_dilated(z, w[:, None, :].astype(z.dtype), window_strides=(1,), padding='SAME',
                                 dimension_numbers=('NWC', 'WIO', 'NWC'), feature_group_count=z.shape[-1])
    return y + b


def _conv_branch(u, dw, db, ln_w, ln_b, rows):
    B, T, _ = u.shape
    a, g = jnp.split(u, 2, axis=-1)
    z = a * jax.nn.sigmoid(g)
    if rows is not None:
        z = _depthwise_conv(z.reshape(B * rows, GRID_W, CONV_W), dw, db).reshape(B, T, CONV_W)
    else:
        z = _depthwise_conv(z, dw, db)
    return jax.nn.silu(_layernorm(z, ln_w, ln_b, LN_EPS))


def _mlstm_feats(u, ib, fb):
    B, T, _ = u.shape
    q, k, v, o, gates = _split_at(u, (ML_W, ML_W, ML_W, ML_W, 4 * ML_H))

    def heads(t):
        return jnp.moveaxis(t.astype(jnp.float32).reshape(B, T, ML_H, ML_DH), 2, 1)

    gates = gates.astype(jnp.float32).reshape(B, T, 2, 2, ML_H)
    ig = jnp.moveaxis(gates[:, :, :, 0] + ib, 1, -1)
    lf = jnp.moveaxis(jax.nn.log_sigmoid(gates[:, :, :, 1] + fb), 1, -1)
    return heads(q), heads(k) * ML_DH ** -0.5, heads(v), o, ig, lf


def _mlstm_scan(state, k, v, ig, lf, q, emit):
    B, H, T, dh = k.shape
    nc = T // ML_CHUNK

    def chunks(t):
        return jnp.moveaxis(t.reshape((B, H, nc, ML_CHUNK) + t.shape[3:]), 2, 0)

    tri = jnp.tril(jnp.ones((ML_CHUNK, ML_CHUNK), dtype=bool))
    xs = (chunks(k), chunks(v), chunks(ig), chunks(lf)) + ((chunks(q),) if emit else ())

    def step(carry, inp):
        C, n, m = carry
        kc, vc, ic, fc = inp[:4]
        b = jnp.cumsum(fc, axis=-1)
        h = None
        if emit:
            qc = inp[4]
            dlog = jnp.where(tri, b[..., :, None] - b[..., None, :] + ic[..., None, :], -jnp.inf)
            inter = b + m[..., None]
            m_t = jnp.maximum(inter, jnp.max(dlog, axis=-1))
            s = jnp.einsum('bhtd,bhsd->bhts', qc, kc) * jnp.exp(dlog - m_t[..., None])
            w_inter = jnp.exp(inter - m_t)
            num = (w_inter[..., None] * jnp.einsum('bhvd,bhtd->bhtv', C, qc)
                   + jnp.einsum('bhts,bhsv->bhtv', s, vc))
            den = w_inter * jnp.einsum('bhd,bhtd->bht', n, qc) + jnp.sum(s, axis=-1)
            h = num / jnp.maximum(jnp.abs(den), jnp.exp(-m_t))[..., None]
        b_last = b[..., -1]
        wlog = b_last[..., None] - b + ic
        m_new = jnp.maximum(b_last + m, jnp.max(wlog, axis=-1))
        carry_w = jnp.exp(b_last + m - m_new)
        wk = jnp.exp(wlog - m_new[..., None])[..., None] * kc
        C_new = carry_w[..., None, None] * C + jnp.einsum('bhsv,bhsd->bhvd', vc, wk)
        n_new = carry_w[..., None] * n + jnp.sum(wk, axis=2)
        return (C_new, n_new, m_new), h

    state, h = lax.scan(step, state, xs)
    if emit:
        h = jnp.moveaxis(h, 0, 2).reshape(B, H, T, dh)
    return state, h


def _flip_time(t, rev):
    return jnp.flip(t, axis=2) if rev else t


def _mlstm_out(h, o, nw):
    B, H, T, dh = h.shape
    hn = _layernorm(jnp.moveaxis(h, 1, 2), nw.reshape(ML_H, ML_DH), None, LN_EPS)
    return hn.reshape(B, T, ML_W) * jax.nn.sigmoid(o.astype(jnp.float32))


def _mlstm_mixer(u_lat, u_ctx, ib, fb, nw, emit_ctx):
    ql, kl, vl, ol, il, fl = _mlstm_feats(u_lat, ib, fb)
    qc, kc, vc, oc, ic, fc = _mlstm_feats(u_ctx, ib, fb)
    B = u_lat.shape[0]
    init = (jnp.zeros((B, ML_H, ML_DH, ML_DH), jnp.float32), jnp.zeros((B, ML_H, ML_DH), jnp.float32),
            jnp.zeros((B, ML_H), jnp.float32))
    h_lat, h_ctx = 0.0, 0.0
    for d in range(2):
        rev = d == 1
        st, hc = _mlstm_scan(init, _flip_time(kc, rev), _flip_time(vc, rev), _flip_time(ic[:, d], rev),
                             _flip_time(fc[:, d], rev), _flip_time(qc, rev) if emit_ctx else None, emit_ctx)
        _, hl = _mlstm_scan(st, _flip_time(kl, rev), _flip_time(vl, rev), _flip_time(il[:, d], rev),
                            _flip_time(fl[:, d], rev), _flip_time(ql, rev), True)
        h_lat = h_lat + _flip_time(hl, rev)
        if emit_ctx:
            h_ctx = h_ctx + _flip_time(hc, rev)
    out_lat = _mlstm_out(h_lat, ol, nw).astype(u_lat.dtype)
    out_ctx = _mlstm_out(h_ctx, oc, nw).astype(u_ctx.dtype) if emit_ctx else None
    return out_lat, out_ctx


def _merge(u_gate, ya, yb, yc, pa, pb, pc, wo):
    ga, gb, gc = jnp.split(jax.nn.sigmoid(u_gate), N_BRANCH, axis=-1)
    m = ga * (ya @ pa) + gb * (yb @ pb) + gc * (yc @ pc)
    return m @ wo


def _mlp(h, w1, w2):
    return jnp.square(jax.nn.relu(h @ w1)) @ w2


def setup_inputs(seed: int = 0) -> dict:
    key = jax.random.key(seed)
    ks = iter(jax.random.split(key, 40))
    f32 = jnp.float32
    L, D = DEPTH, D_MODEL

    def nrm(shape, scale):
        return scale * jax.random.normal(next(ks), shape, f32)

    def unif(shape, lo, hi):
        return jax.random.uniform(next(ks), shape, f32, lo, hi)

    return {
        'x': nrm((BATCH, SEQ, D), 1.0),
        'c': nrm((BATCH, D), 1.0),
        'ctx': nrm((BATCH, CTX_LEN, D), 1.0),
        'c_ctx': nrm((D,), 1.0),
        'w_ada': nrm((L, D, N_MOD * D), 0.5 * D ** -0.5),
        'b_ada': nrm((L, N_MOD * D), 0.02),
        'g_norm1': 1.0 + nrm((L, D), 0.02),
        'g_norm2': 1.0 + nrm((L, D), 0.02),
        'w_in': nrm((L, D, P_TOTAL), D ** -0.5),
        'mu_prev': unif((L, RW_COLS), 0.0, 0.5),
        'mu_next': unif((L, RW_COLS), 0.0, 0.5),
        'rw_w0': unif((L, 2, RW_W), -6.0, 1.0),
        'rw_w2': nrm((L, 2, LORA_DECAY, RW_W), 0.1),
        'rw_a0': nrm((L, 2, RW_W), 0.5),
        'rw_a2': nrm((L, 2, LORA_ICLR, RW_W), LORA_ICLR ** -0.5),
        'rw_kk': 0.85 + nrm((L, RW_W), 0.05),
        'rw_ka': 1.0 + nrm((L, RW_W), 0.05),
        'rw_rk': nrm((L, RW_H, RW_N), 0.1),
        'rw_g2': nrm((L, LORA_GATE, RW_W), LORA_GATE ** -0.5),
        'rw_lnw': 1.0 + nrm((L, RW_W), 0.02),
        'rw_lnb': nrm((L, RW_W), 0.02),
        'cv_dw': nrm((L, CONV_K, CONV_W), CONV_K ** -0.5),
        'cv_db': nrm((L, CONV_W), 0.02),
        'cv_lnw': 1.0 + nrm((L, CONV_W), 0.02),
        'cv_lnb': nrm((L, CONV_W), 0.02),
        'ml_ib': nrm((L, 2, ML_H), 0.1),
        'ml_fb': jnp.linspace(3.0, 6.0, ML_H, dtype=f32) + nrm((L, 2, ML_H), 0.1),
        'ml_nw': 1.0 + nrm((L, ML_W), 0.02),
        'p_a': nrm((L, RW_W, D), RW_W ** -0.5),
        'p_b': nrm((L, CONV_W, D), CONV_W ** -0.5),
        'p_c': nrm((L, ML_W, D), ML_W ** -0.5),
        'w_out': nrm((L, D, D), D ** -0.5),
        'w_mlp1': nrm((L, D, D_FF), D ** -0.5),
        'w_mlp2': nrm((L, D_FF, D), D_FF ** -0.5),
        'g_final': 1.0 + nrm((D,), 0.02),
    }


def reference(x, c, ctx, c_ctx, w_ada, b_ada, g_norm1, g_norm2, w_in, mu_prev, mu_next,
              rw_w0, rw_w2, rw_a0, rw_a2, rw_kk, rw_ka, rw_rk, rw_g2, rw_lnw, rw_lnb,
              cv_dw, cv_db, cv_lnw, cv_lnb, ml_ib, ml_fb, ml_nw,
              p_a, p_b, p_c, w_out, w_mlp1, w_mlp2, g_final):
    rows = x.shape[1] // GRID_W
    for l in range(DEPTH):
        last = l == DEPTH - 1
        mod_x = jnp.split((jax.nn.silu(c) @ w_ada[l] + b_ada[l])[:, None, :], N_MOD, axis=-1)
        mod_c = jnp.split(jax.nn.silu(c_ctx) @ w_ada[l] + b_ada[l], N_MOD, axis=-1)
        h = _rmsnorm(x, g_norm1[l]) * (1.0 + mod_x[1]) + mod_x[0]
        hc = _rmsnorm(ctx, g_norm1[l]) * (1.0 + mod_c[1]) + mod_c[0]
        u_rw, u_cv, u_ml, u_gate = jnp.split(h @ w_in[l], [OFF_CV, OFF_ML, OFF_GATE], axis=-1)
        if last:
            uc_rw = hc @ w_in[l, :, :OFF_CV]
            uc_ml = hc @ w_in[l, :, OFF_ML:OFF_GATE]
        else:
            uc_rw, uc_cv, uc_ml, uc_gate = jnp.split(hc @ w_in[l], [OFF_CV, OFF_ML, OFF_GATE], axis=-1)
        ya, ya_c = _rwkv_mixer(u_rw, uc_rw, mu_prev[l], mu_next[l], rw_w0[l], rw_w2[l], rw_a0[l], rw_a2[l],
                               rw_kk[l], rw_ka[l], rw_rk[l], rw_g2[l], rw_lnw[l], rw_lnb[l], not last)
        yc, yc_c = _mlstm_mixer(u_ml, uc_ml, ml_ib[l], ml_fb[l], ml_nw[l], not last)
        yb = _conv_branch(u_cv, cv_dw[l], cv_db[l], cv_lnw[l], cv_lnb[l], rows)
        x = x + mod_x[2] * _merge(u_gate, ya, yb, yc, p_a[l], p_b[l], p_c[l], w_out[l])
        if not last:
            yb_c = _conv_branch(uc_cv, cv_dw[l], cv_db[l], cv_lnw[l], cv_lnb[l], None)
            ctx = ctx + mod_c[2] * _merge(uc_gate, ya_c, yb_c, yc_c, p_a[l], p_b[l], p_c[l], w_out[l])
        h = _rmsnorm(x, g_norm2[l]) * (1.0 + mod_x[4]) + mod_x[3]
        x = x + mod_x[5] * _mlp(h, w_mlp1[l], w_mlp2[l])
        if not last:
            hc = _rmsnorm(ctx, g_norm2[l]) * (1.0 + mod_c[4]) + mod_c[3]
            ctx = ctx + mod_c[5] * _mlp(hc, w_mlp1[l], w_mlp2[l])
    return _rmsnorm(x, g_final)
```

```python
import numpy as np
import concourse.bass as bass
import concourse.mybir as mybir
from concourse.bass_utils import run_bass_kernel_spmd

F32 = mybir.dt.float32
BF16 = mybir.dt.bfloat16
ALU = mybir.AluOpType
AF = mybir.ActivationFunctionType
AX = mybir.AxisListType

D = 1024
KC = 8
NCH = 64
PW = NCH * 128
DFF = 4096
L_CH = 64
NORM_EPS = 1e-6
LN_EPS = 1e-5
RW_GN_EPS = 64e-5

SELF_WAIT = True


class LT:
    __slots__ = ("name", "w", "r")

    def __init__(self, name=""):
        self.name = name
        self.w = None
        self.r = {}


class Buf:
    def __init__(self, t, lt, psum=False):
        self.t = t
        self.lt = lt
        self.psum = psum

    def __getitem__(self, k):
        return self.t[k]


class Prog:
    def __init__(self, nc, n_dma_sems=6):
        self.nc = nc
        self.engs = {"pe": nc.tensor, "act": nc.scalar, "dve": nc.vector,
                     "pool": nc.gpsimd, "sp": nc.sync}
        self.sems = {}
        self.cnt = {}
        for k in self.engs:
            self.sems[k] = nc.alloc_semaphore(name="s_" + k)
            self.cnt[k] = 0
        self.dq = {}
        for q in ("sp", "pool", "act"):
            lst = []
            for i in range(n_dma_sems):
                key = "d_%s%d" % (q, i)
                self.sems[key] = nc.alloc_semaphore(name=key)
                self.cnt[key] = 0
                lst.append(key)
            self.dq[q] = [lst, 0]
        self.obs = {k: {} for k in self.engs}
        self.ninst = 0
        self.nbuf = 0

    def _need(self, e, deps):
        for (k, v) in deps:
            if k == e and (e == "pe" or not SELF_WAIT):
                continue
            if self.obs[e].get(k, 0) < v:
                self.engs[e].wait_ge(self.sems[k], v)
                self.obs[e][k] = v
                self.ninst += 1

    @staticmethod
    def _lts(bufs):
        out = []
        for b in bufs:
            if isinstance(b, LT):
                out.append(b)
            elif isinstance(b, Buf):
                out.append(b.lt)
            else:
                raise TypeError(b)
        return out

    def _deps(self, reads, writes):
        deps = []
        for t in reads:
            if t.w is not None:
                deps.append(t.w)
        for t in writes:
            if t.w is not None:
                deps.append(t.w)
            for k, v in t.r.items():
                deps.append((k, v))
        return deps

    def op(self, e, fn, rd=(), wr=()):
        if e != "pe":
            pr = [b for b in rd if isinstance(b, Buf) and b.psum]
            if pr:
                rd = [b for b in rd if not (isinstance(b, Buf) and b.psum)]
                wr = list(wr) + [b for b in pr if b not in wr]
        reads, writes = self._lts(rd), self._lts(wr)
        self._need(e, self._deps(reads, writes))
        ins = fn(self.engs[e])
        self.cnt[e] += 1
        c = self.cnt[e]
        ins.then_inc(self.sems[e], 1)
        self.ninst += 1
        for t in reads:
            t.r[e] = c
        for t in writes:
            t.w = (e, c)
            t.r = {}
        return ins

    def dma(self, q, out, in_, rd=(), wr=(), **kw):
        reads, writes = self._lts(rd), self._lts(wr)
        lst, idx = self.dq[q]
        key = lst[idx % len(lst)]
        self.dq[q][1] = idx + 1
        deps = self._deps(reads, writes)
        if self.cnt[key] > 0:
            deps.append((key, self.cnt[key]))
        self._need(q, deps)
        ins = self.engs[q].dma_start(out=out, in_=in_, **kw)
        self.cnt[key] += 16
        c = self.cnt[key]
        ins.then_inc(self.sems[key], 16)
        self.ninst += 1
        for t in reads:
            t.r[key] = c
        for t in writes:
            t.w = (key, c)
            t.r = {}
        return ins

    def finish(self, bufs, e="sp"):
        deps = []
        for t in self._lts(bufs):
            if t.w is not None:
                deps.append(t.w)
        self._need(e, deps)

    def barrier(self):
        allc = [(k, v) for k, v in self.cnt.items() if v > 0]
        for e in self.engs:
            self._need(e, allc)

    def scope_begin(self):
        return (self.nc.sbuf_base, self.nc.sbuf_top)

    def scope_end(self, mark):
        self.barrier()
        self.nc.sbuf_base, self.nc.sbuf_top = mark

    def sb(self, shape, dt, name=None):
        self.nbuf += 1
        nm = "%s_%d" % (name or "b", self.nbuf)
        return Buf(self.nc.alloc_sbuf_tensor(nm, list(shape), dt), LT(nm))

    def ps(self, shape, dt, name=None):
        self.nbuf += 1
        nm = "%s_%d" % (name or "p", self.nbuf)
        return Buf(self.nc.alloc_psum_tensor(nm, list(shape), dt), LT(nm), psum=True)


class Rot:
    def __init__(self, bufs):
        self.bufs = bufs
        self.i = 0

    def next(self):
        b = self.bufs[self.i % len(self.bufs)]
        self.i += 1
        return b


PCOLS = {}
_off = 0
for _nm, _n in (("mu_p", 15), ("mu_n", 15), ("w0", 8), ("a0", 8), ("kk", 4), ("ka", 4), ("rk", 4),
                ("rlnw", 4), ("rlnb", 4), ("cdw", 4 * 31), ("cdb", 4), ("clnw", 4), ("clnb", 4),
                ("mnw", 4), ("g1", 8), ("g2", 8)):
    PCOLS[_nm] = (_off, _n)
    _off += _n
NPCOL = _off


def _cm(vec, nchunks):
    return np.ascontiguousarray(np.asarray(vec, np.float32).reshape(nchunks, 128).T)


def pack_pcols(inp, l):
    out = np.zeros((128, NPCOL), np.float32)

    def put(nm, arr):
        o, n = PCOLS[nm]
        assert arr.shape == (128, n), (nm, arr.shape)
        out[:, o:o + n] = arr
    put("mu_p", _cm(inp["mu_prev"][l], 15))
    put("mu_n", _cm(inp["mu_next"][l], 15))
    put("w0", np.concatenate([_cm(inp["rw_w0"][l, d], 4) for d in range(2)], 1))
    put("a0", np.concatenate([_cm(inp["rw_a0"][l, d], 4) for d in range(2)], 1))
    put("kk", _cm(inp["rw_kk"][l], 4))
    put("ka", _cm(inp["rw_ka"][l], 4))
    put("rk", _cm(inp["rw_rk"][l].reshape(-1), 4))
    put("rlnw", _cm(inp["rw_lnw"][l], 4))
    put("rlnb", _cm(inp["rw_lnb"][l], 4))
    dw = np.asarray(inp["cv_dw"][l], np.float32)
    put("cdw", np.concatenate([np.ascontiguousarray(dw[:, j * 128:(j + 1) * 128].T) for j in range(4)], 1))
    put("cdb", _cm(inp["cv_db"][l], 4))
    put("clnw", _cm(inp["cv_lnw"][l], 4))
    put("clnb", _cm(inp["cv_lnb"][l], 4))
    put("mnw", _cm(inp["ml_nw"][l], 4))
    put("g1", _cm(inp["g_norm1"][l], 8))
    put("g2", _cm(inp["g_norm2"][l], 8))
    return out


def pad_w_in(w):
    Ln = w.shape[0]
    out = np.zeros((Ln, D, PW), np.float32)
    out[:, :, 0:5008] = w[:, :, 0:5008]
    out[:, :, 5120:8192] = w[:, :, 5008:8080]
    return out


class Cfg:
    def __init__(self, NS=2, T=4096, CTX=256, DEPTH=4, debug=False, stop_after=None, c_stop=9):
        self.c_stop = c_stop
        self.NS, self.T, self.CTX, self.DEPTH = NS, T, CTX, DEPTH
        self.TC = T + CTX
        self.debug = debug
        self.stop_after = stop_after
        self.tiles = []
        e = 0
        while e < CTX:
            n = min(512, CTX - e)
            self.tiles.append((e, n)); e += n
        while e < self.TC:
            n = min(512, self.TC - e)
            self.tiles.append((e, n)); e += n


def build(cfg):
    NS, T, CTX, DEPTH, TC = cfg.NS, cfg.T, cfg.CTX, cfg.DEPTH, cfg.TC
    nc = bass.Bass("TRN2", target_bir_lowering=False)
    P = Prog(nc)
    kind_dbg = "ExternalOutput" if cfg.debug else "Internal"

    def din(name, shape, dt=F32):
        return nc.dram_tensor(name, list(shape), dt, kind="ExternalInput").ap()

    def dscr(name, shape, dt, dbg=False):
        return nc.dram_tensor(name, list(shape), dt, kind=(kind_dbg if dbg else "Internal")).ap()

    x_in = din("x", [NS, T, D])
    ctx_in = din("ctx", [NS, CTX, D])
    cvec = din("cvec", [3, D])
    w_ada = din("w_ada", [DEPTH, D, 6 * D])
    b_ada = din("b_ada", [DEPTH, 6 * D])
    w_in = din("w_in", [DEPTH, D, PW])
    pcols_d = din("pcols", [DEPTH, 128, NPCOL])
    rw_w2 = din("rw_w2", [DEPTH, 128, 512])
    rw_a2 = din("rw_a2", [DEPTH, 128, 512])
    rw_g2 = din("rw_g2", [DEPTH, 128, 512])
    mlb = din("mlb", [DEPTH, 16, 2])
    p_abc = din("p_abc", [DEPTH, 3, 512, D])
    w_out = din("w_out", [DEPTH, D, D])
    w_mlp1 = din("w_mlp1", [DEPTH, D, DFF])
    w_mlp2 = din("w_mlp2", [DEPTH, DFF, D])
    g_final = din("g_final", [1, D])
    out = nc.dram_tensor("out", [NS, T, D], F32, kind="ExternalOutput").ap()

    xc = dscr("xc", [NS, CTX, D], F32)
    U = dscr("U", [NS, NCH, 128, TC], BF16, dbg=True)
    MOD = dscr("MOD", [DEPTH, 3, 6 * D], F32, dbg=True)
    WB_in = dscr("WB_in", [DEPTH, D, PW], BF16)
    WB_p = dscr("WB_p", [DEPTH, 3, 512, D], BF16)
    WB_out = dscr("WB_out", [DEPTH, D, D], BF16)
    WB_1 = dscr("WB_1", [DEPTH, D, DFF], BF16)
    WB_2 = dscr("WB_2", [DEPTH, DFF, D], BF16)
    YBR = dscr("YBR", [NS, 3, 4, 128, TC], BF16, dbg=True)

    L_U = [[LT("U%d_%d" % (s, i)) for i in range(len(cfg.tiles))] for s in range(NS)]
    L_X = [[LT("X%d_%d" % (s, i)) for i in range(len(cfg.tiles))] for s in range(NS)]
    L_WB = [LT("WB%d" % l) for l in range(DEPTH)]
    L_MOD = LT("MOD")
    L_Y = [[[LT() for i in range(len(cfg.tiles))] for b in range(3)] for s in range(NS)]

    psf = Rot([P.ps([128, 512], F32, "psf") for _ in range(6)])
    psb = Rot([P.ps([128, 1024], BF16, "psb") for _ in range(2)])

    ident_bf = P.sb([128, 128], BF16, "identb")
    ident_f = P.sb([128, 128], F32, "identf")
    ones_bf = P.sb([128, 128], BF16, "onesb")
    cst = {}

    def const_setup():
        pass

    ident_d = din("ident", [128, 128])
    blk_d = din("blkones", [128, 128])
    masks_d = din("masks", [4, 64, 64])
    lvmask_d = din("lvmask", [6, 64, 64])
    P.dma("sp", ident_f[:], ident_d, wr=[ident_f])
    P.op("dve", lambda e: e.tensor_copy(ident_bf[:], ident_f[:]), rd=[ident_f], wr=[ident_bf])
    P.op("dve", lambda e: e.memset(ones_bf[:], 1.0), wr=[ones_bf])
    blk_f = P.sb([128, 128], F32, "blkf")
    blk_bf = P.sb([128, 128], BF16, "blkb")
    P.dma("sp", blk_f[:], blk_d, wr=[blk_f])
    P.op("dve", lambda e: e.tensor_copy(blk_bf[:], blk_f[:]), rd=[blk_f], wr=[blk_bf])

    cast_i = [0]
    castbuf = {}

    def cast_dram(src, dst, R, C, lt_dst):
        for r0 in range(0, R, 128):
            for c0 in range(0, C, 2048):
                cw = min(2048, C - c0)
                a, b = castbuf["cin"].next(), castbuf["cout"].next()
                P.dma("sp", a[:, :cw], src[r0:r0 + 128, c0:c0 + cw], wr=[a])
                eng = ("dve", "act", "pool")[cast_i[0] % 3]
                cast_i[0] += 1
                if eng == "act":
                    P.op("act", lambda e: e.copy(b[:, :cw], a[:, :cw]), rd=[a], wr=[b])
                else:
                    P.op(eng, lambda e: e.tensor_copy(b[:, :cw], a[:, :cw]), rd=[a], wr=[b])
                P.dma("pool", dst[r0:r0 + 128, c0:c0 + cw], b[:, :cw], rd=[b], wr=[lt_dst])

    def cast_layer(l):
        mark = P.scope_begin()
        castbuf["cin"] = Rot([P.sb([128, 2048], F32, "cin") for _ in range(3)])
        castbuf["cout"] = Rot([P.sb([128, 2048], BF16, "cout") for _ in range(3)])
        cast_dram(w_in[l], WB_in[l], D, PW, L_WB[l])
        for br in range(3):
            cast_dram(p_abc[l, br], WB_p[l, br], 512, D, L_WB[l])
        cast_dram(w_out[l], WB_out[l], D, D, L_WB[l])
        cast_dram(w_mlp1[l], WB_1[l], D, DFF, L_WB[l])
        cast_dram(w_mlp2[l], WB_2[l], DFF, D, L_WB[l])
        P.scope_end(mark)

    def modulation():
        mark = P.scope_begin()
        cT = P.sb([128, KC, 3], F32, "cT")
        scT = P.sb([128, KC, 3], F32, "scT")
        for v in range(3):
            P.dma("sp", cT[:, :, v], cvec[v].rearrange("(k p) -> p k", p=128), wr=[cT], allow_slow_non_contiguous=True)
        P.op("act", lambda e: e.activation(out=scT[:], in_=cT[:], func=AF.Silu), rd=[cT], wr=[scT])
        wada_t = Rot([P.sb([128, KC, 512], F32, "wada") for _ in range(2)])
        bada_t = Rot([P.sb([3, 512], F32, "bada") for _ in range(2)])
        modrow = Rot([P.sb([3, 512], F32, "modrow") for _ in range(2)])
        for l in range(DEPTH):
            for g in range(12):
                wt, bt, mr = wada_t.next(), bada_t.next(), modrow.next()
                P.dma("sp", bt[:], b_ada[l:l + 1, g * 512:(g + 1) * 512].broadcast_to([3, 512]), wr=[bt])
                P.dma("sp", wt[:], w_ada[l, :, g * 512:(g + 1) * 512].rearrange("(k p) c -> p k c", p=128), wr=[wt])
                ps = psf.next()
                for k in range(KC):
                    P.op("pe", lambda e: e.matmul(ps[0:3, :], scT[:, k, :], wt[:, k, :], start=(k == 0), stop=(k == KC - 1)),
                         rd=[scT, wt], wr=[ps])
                P.op("dve", lambda e: e.tensor_tensor(mr[:], ps[0:3, :], bt[:], ALU.add), rd=[ps, bt], wr=[mr])
                P.dma("sp", MOD[l, :, g * 512:(g + 1) * 512], mr[:], rd=[mr], wr=[L_MOD])
        P.scope_end(mark)

    pc = P.sb([128, NPCOL], F32, "pcols")
    modc = P.sb([128, 48, 3], F32, "modc")
    g1c = P.sb([128, 3, KC], F32, "g1c")
    g2c = P.sb([128, 3, KC], F32, "g2c")
    mgrow = [[P.sb([128, D], F32, "mgrow") for w in range(2)] for v in range(3)]

    def pcol(nm, i=0, n=1):
        o, _ = PCOLS[nm]
        return pc[:, o + i:o + i + n]

    def layer_params(l):
        P.dma("sp", pc[:], pcols_d[l], wr=[pc])
        for v in range(3):
            P.dma("sp", modc[:, :, v], MOD[l, v].rearrange("(c p) -> p c", p=128), rd=[L_MOD], wr=[modc],
                  allow_slow_non_contiguous=True)
        for v in range(3):
            for w, mi in enumerate((2, 5)):
                P.dma("sp", mgrow[v][w][:], MOD[l, v:v + 1, mi * D:(mi + 1) * D].broadcast_to([128, D]),
                      rd=[L_MOD], wr=[mgrow[v][w]])
            for (gc, gname, mi) in ((g1c, "g1", 1), (g2c, "g2", 4)):
                P.op("dve", lambda e: e.scalar_tensor_tensor(gc[:, v, :], modc[:, mi * 8:(mi + 1) * 8, v], 1.0,
                                                              pcol(gname, 0, 8), ALU.add, ALU.mult),
                     rd=[modc, pc], wr=[gc])

    def x_src(l, s, e0, n):
        if e0 < CTX:
            return (ctx_in if l == 0 else xc)[s, e0:e0 + n, :]
        return (x_in if l == 0 else out)[s, e0 - CTX:e0 - CTX + n, :]

    def x_dst(s, e0, n):
        if e0 < CTX:
            return xc[s, e0:e0 + n, :]
        return out[s, e0 - CTX:e0 - CTX + n, :]

    ss_t = Rot([P.sb([128, 8], F32, "ss") for _ in range(2)])
    junk = P.sb([128, D], F32, "junk")

    epsc = P.sb([128, 4], F32, "epsc")
    P.op("dve", lambda e: e.memset(epsc[:, 0:1], NORM_EPS), wr=[epsc])
    P.op("dve", lambda e: e.memset(epsc[:, 1:2], LN_EPS), wr=[epsc])
    P.op("dve", lambda e: e.memset(epsc[:, 2:3], RW_GN_EPS), wr=[epsc])
    P.op("dve", lambda e: e.memset(epsc[:, 3:4], 1e-24), wr=[epsc])
    EPSI = {NORM_EPS: 0, LN_EPS: 1, RW_GN_EPS: 2, 1e-24: 3}

    def rsqrt(out_ap, in_ap, scale, eps, rd, wr):
        i = EPSI[eps]
        npart = out_ap.shape[0]
        P.op("act", lambda e: e.activation(out=out_ap, in_=in_ap, func=AF.Sqrt, scale=scale, bias=epsc[0:npart, i:i + 1]),
             rd=list(rd) + [epsc], wr=wr)
        P.op("dve", lambda e: e.reciprocal(out_ap, out_ap), rd=wr, wr=wr)

    def norm_T(xt, nsub, gcol, shcol, hT, xn):
        ss = ss_t.next()
        for sub in range(nsub):
            P.op("act", lambda e: e.activation(out=junk[:], in_=xt[:, sub, :], func=AF.Square,
                                               accum_out=ss[:, sub:sub + 1]), rd=[xt], wr=[junk, ss])
        rsqrt(ss[:, 4:4 + nsub], ss[:, 0:nsub], 1.0 / D, NORM_EPS, [ss], [ss])
        for sub in range(nsub):
            P.op("dve", lambda e: e.tensor_scalar(xn[:, sub, :], xt[:, sub, :], ss[:, 4 + sub:5 + sub], None, ALU.mult),
                 rd=[xt, ss], wr=[xn])
        for k in range(KC):
            pb = psb.next()
            for sub in range(nsub):
                P.op("pe", lambda e: e.transpose(pb[:, sub * 128:(sub + 1) * 128], xn[:, sub, k * 128:(k + 1) * 128],
                                                 ident_bf[:]), rd=[xn, ident_bf], wr=[pb])
            P.op("act", lambda e: e.activation(out=hT[:, k, 0:nsub * 128], in_=pb[:, 0:nsub * 128], func=AF.Identity,
                                               scale=gcol(k), bias=shcol(k)), rd=[pb, g1c, g2c, modc], wr=[hT])

    evac_i = [0]

    def evac(dst_ap, ps, src_ap, wr):
        evac_i[0] += 1
        if evac_i[0] % 2:
            P.op("act", lambda e: e.copy(dst_ap, src_ap), rd=[ps], wr=wr)
        else:
            P.op("dve", lambda e: e.tensor_copy(dst_ap, src_ap), rd=[ps], wr=wr)

    def phase_A(l):
        mark = P.scope_begin()
        wg_rot = Rot([P.sb([128, KC, 512], BF16, "wg") for _ in range(3)])
        ust_rot = Rot([P.sb([128, 4, 512], BF16, "ust") for _ in range(2)])
        xt_rot = Rot([P.sb([128, 4, D], F32, "xt") for _ in range(2)])
        xn = P.sb([128, 4, D], BF16, "xn")
        hT_rot = Rot([P.sb([128, KC, 512], BF16, "hT") for _ in range(2)])
        for s in range(NS):
            for ti, (e0, n) in enumerate(cfg.tiles):
                nsub = n // 128
                v = 2 if e0 < CTX else s
                xt = xt_rot.next()
                P.dma("sp", xt[:, 0:nsub, :], x_src(l, s, e0, n).rearrange("(a p) d -> p a d", p=128),
                      rd=[L_X[s][ti]], wr=[xt])
                hT = hT_rot.next()
                norm_T(xt, nsub, lambda k: g1c[:, v, k:k + 1], lambda k: modc[:, 0 * 8 + k, v:v + 1], hT, xn)
                for g in range(NCH // 4):
                    wg = wg_rot.next()
                    P.dma("sp" if g % 2 == 0 else "pool", wg[:],
                          WB_in[l, :, g * 512:(g + 1) * 512].rearrange("(k p) c -> p k c", p=128),
                          rd=[L_WB[l]], wr=[wg])
                    ust = ust_rot.next()
                    for c4 in range(4):
                        ps = psf.next()
                        for k in range(KC):
                            P.op("pe", lambda e: e.matmul(ps[:, 0:n], wg[:, k, c4 * 128:(c4 + 1) * 128], hT[:, k, 0:n],
                                                          start=(k == 0), stop=(k == KC - 1)), rd=[wg, hT], wr=[ps])
                        evac(ust[:, c4, 0:n], ps, ps[:, 0:n], [ust])
                    P.dma("sp", U[s, g * 4:(g + 1) * 4, :, e0:e0 + n].rearrange("c p t -> p c t"), ust[:, :, 0:n],
                          rd=[ust], wr=[L_U[s][ti]])
        P.scope_end(mark)

    def ln_stats_bcast(xb, sqb, lhsT_ones, nj, n, eps, tag):
        raise NotImplementedError

    def phase_C(l, last):
        mark = P.scope_begin()
        pw = P.sb([128, 3, 4, D], BF16, "pw")
        wo = P.sb([128, KC, D], BF16, "wo")
        for br in range(3):
            P.dma("sp", pw[:, br, :, :], WB_p[l, br].rearrange("(j p) d -> p j d", p=128), rd=[L_WB[l]], wr=[pw])
        P.dma("sp", wo[:], WB_out[l].rearrange("(k p) d -> p k d", p=128), rd=[L_WB[l]], wr=[wo])
        yt_rot = Rot([P.sb([128, 12, 256], BF16, "yt") for _ in range(1)])
        gt_rot = Rot([P.sb([128, 24, 256], BF16, "gt") for _ in range(1)])
        mT = P.sb([128, KC, 256], F32, "mT")
        mTb = P.sb([128, KC, 256], BF16, "mTb")
        tmpc = Rot([P.sb([128, 512], F32, "tmpc") for _ in range(2)])
        xt2 = Rot([P.sb([128, 2, D], F32, "xt2") for _ in range(1)])
        hid = P.sb([128, 32, 256], BF16, "hid")
        w1_rot = Rot([P.sb([128, KC, 512], BF16, "w1g") for _ in range(2)])
        w2_rot = Rot([P.sb([128, 8, D], BF16, "w2g") for _ in range(2)])
        xt_rot = Rot([P.sb([128, 2, D], F32, "xt") for _ in range(2)])
        xn = P.sb([128, 2, D], BF16, "xn")
        hT_rot = Rot([P.sb([128, KC, 256], BF16, "hT") for _ in range(1)])
        gfin = P.sb([128, D], F32, "gfin")
        if last:
            P.dma("sp", gfin[:], g_final.broadcast_to([128, D]), wr=[gfin])
        for s in range(NS):
            for ti, e0, n in [(ti, e0 + o, min(256, n - o)) for ti, (e0, n) in enumerate(cfg.tiles) for o in range(0, n, 256)]:
                if last and e0 < CTX:
                    continue
                nsub = n // 128
                v = 2 if e0 < CTX else s
                xt = xt_rot.next()
                P.dma("sp", xt[:, 0:nsub, :], x_src(l, s, e0, n).rearrange("(a p) d -> p a d", p=128),
                      rd=[L_X[s][ti]], wr=[xt])
                yt, gt = yt_rot.next(), gt_rot.next()
                P.dma("pool", yt[:, :, 0:n], YBR[s, :, :, :, e0:e0 + n].rearrange("b j p t -> p (b j) t"),
                      rd=L_Y[s][0][ti:ti + 1] + L_Y[s][1][ti:ti + 1] + L_Y[s][2][ti:ti + 1], wr=[yt])
                P.dma("pool", gt[:, :, 0:n], U[s, 40:64, :, e0:e0 + n].rearrange("c p t -> p c t"), rd=[L_U[s][ti]], wr=[gt])
                P.op("act", lambda e: e.activation(out=gt[:, :, 0:n], in_=gt[:, :, 0:n], func=AF.Sigmoid), rd=[gt], wr=[gt])
                for oc in range(KC):
                    for br in range(3):
                        ps = psf.next()
                        for j in range(4):
                            P.op("pe", lambda e: e.matmul(ps[:, 0:n], pw[:, br, j, oc * 128:(oc + 1) * 128], yt[:, br * 4 + j, 0:n],
                                                          start=(j == 0), stop=(j == 3)), rd=[pw, yt], wr=[ps])
                        if br == 0:
                            P.op("dve", lambda e: e.tensor_tensor(mT[:, oc, 0:n], ps[:, 0:n], gt[:, br * 8 + oc, 0:n], ALU.mult),
                                 rd=[ps, gt], wr=[mT])
                        else:
                            tc_ = tmpc.next()
                            P.op("dve", lambda e: e.tensor_tensor(tc_[:, 0:n], ps[:, 0:n], gt[:, br * 8 + oc, 0:n], ALU.mult),
                                 rd=[ps, gt], wr=[tc_])
                            P.op("pool", lambda e: e.tensor_tensor(mT[:, oc, 0:n], mT[:, oc, 0:n], tc_[:, 0:n], ALU.add),
                                 rd=[tc_, mT], wr=[mT])
                    P.op("act", lambda e: e.copy(mTb[:, oc, 0:n], mT[:, oc, 0:n]), rd=[mT], wr=[mTb])
                if cfg.c_stop <= 2:
                    continue
                x2 = xt2.next()
                for sub in range(nsub):
                    for half in range(2):
                        ps = psf.next()
                        for k in range(KC):
                            P.op("pe", lambda e: e.matmul(ps[:, :], mTb[:, k, sub * 128:(sub + 1) * 128],
                                                          wo[:, k, half * 512:(half + 1) * 512], start=(k == 0), stop=(k == KC - 1)),
                                 rd=[mTb, wo], wr=[ps])
                        tc_ = tmpc.next()
                        P.op("dve", lambda e: e.tensor_tensor(tc_[:], ps[:], mgrow[v][0][:, half * 512:(half + 1) * 512], ALU.mult),
                             rd=[ps, mgrow[v][0]], wr=[tc_])
                        P.op("pool", lambda e: e.tensor_tensor(x2[:, sub, half * 512:(half + 1) * 512], tc_[:],
                                                               xt[:, sub, half * 512:(half + 1) * 512], ALU.add),
                             rd=[tc_, xt], wr=[x2])
                if cfg.c_stop <= 3:
                    continue
                hT = hT_rot.next()
                norm_T(x2, nsub, lambda k: g2c[:, v, k:k + 1], lambda k: modc[:, 3 * 8 + k, v:v + 1], hT, xn)
                for g in range(8):
                    w1 = w1_rot.next()
                    P.dma("sp", w1[:], WB_1[l, :, g * 512:(g + 1) * 512].rearrange("(k p) c -> p k c", p=128),
                          rd=[L_WB[l]], wr=[w1])
                    for c4 in range(4):
                        hc = g * 4 + c4
                        ps = psf.next()
                        for k in range(KC):
                            P.op("pe", lambda e: e.matmul(ps[:, 0:n], w1[:, k, c4 * 128:(c4 + 1) * 128], hT[:, k, 0:n],
                                                          start=(k == 0), stop=(k == KC - 1)), rd=[w1, hT], wr=[ps])
                        tc_ = tmpc.next()
                        P.op("act", lambda e: e.activation(out=tc_[:, 0:n], in_=ps[:, 0:n], func=AF.Relu), rd=[ps], wr=[tc_])
                        P.op("dve" if hc % 2 else "pool", lambda e: e.tensor_tensor(hid[:, hc, 0:n], tc_[:, 0:n], tc_[:, 0:n], ALU.mult),
                             rd=[tc_], wr=[hid])
                if cfg.c_stop <= 4:
                    continue
                xo = xt_rot.next()
                pss = [[psf.next() for half in range(2)] for sub in range(2)]
                for sp in range(0, nsub, 2):
                    subs = list(range(sp, min(sp + 2, nsub)))
                    for g in range(4):
                        w2 = w2_rot.next()
                        P.dma("sp", w2[:], WB_2[l, g * 1024:(g + 1) * 1024, :].rearrange("(k p) d -> p k d", p=128),
                              rd=[L_WB[l]], wr=[w2])
                        for kk_ in range(8):
                            hc = g * 8 + kk_
                            for sub in subs:
                                for half in range(2):
                                    ps = pss[sub - sp][half]
                                    P.op("pe", lambda e: e.matmul(ps[:, :], hid[:, hc, sub * 128:(sub + 1) * 128],
                                                                  w2[:, kk_, half * 512:(half + 1) * 512],
                                                                  start=(hc == 0), stop=(hc == 31)), rd=[hid, w2], wr=[ps])
                    for sub in subs:
                        for half in range(2):
                            ps = pss[sub - sp][half]
                            tc_ = tmpc.next()
                            P.op("dve", lambda e: e.tensor_tensor(tc_[:], ps[:], mgrow[v][1][:, half * 512:(half + 1) * 512], ALU.mult),
                                 rd=[ps, mgrow[v][1]], wr=[tc_])
                            P.op("pool", lambda e: e.tensor_tensor(xo[:, sub, half * 512:(half + 1) * 512], tc_[:],
                                                                   x2[:, sub, half * 512:(half + 1) * 512], ALU.add),
                                 rd=[tc_, x2], wr=[xo])
                if cfg.c_stop <= 5:
                    continue
                if last:
                    ss = ss_t.next()
                    for sub in range(nsub):
                        P.op("act", lambda e: e.activation(out=junk[:], in_=xo[:, sub, :], func=AF.Square,
                                                           accum_out=ss[:, sub:sub + 1]), rd=[xo], wr=[junk, ss])
                    rsqrt(ss[:, 4:4 + nsub], ss[:, 0:nsub], 1.0 / D, NORM_EPS, [ss], [ss])
                    for sub in range(nsub):
                        P.op("dve", lambda e: e.scalar_tensor_tensor(xo[:, sub, :], xo[:, sub, :], ss[:, 4 + sub:5 + sub], gfin[:],
                                                                      ALU.mult, ALU.mult), rd=[xo, ss, gfin], wr=[xo])
                if cfg.c_stop <= 6:
                    continue
                for sub in range(nsub):
                    P.dma("sp", x_dst(s, e0 + sub * 128, 128), xo[:, sub, :], rd=[xo], wr=[L_X[s][ti]])
        P.scope_end(mark)

    def stats_rstd(x_f32, nj, n, ones_l, eps, inv_n, scr):
        lhs, full = ones_l
        xb, sq = scr["xb"], scr["sq"]
        for j in range(nj):
            P.op("act", lambda e: e.copy(xb[:, j, 0:n], x_f32(j)), rd=scr["rd"], wr=[xb])
            P.op("act", lambda e: e.activation(out=sq[:, j, 0:n], in_=x_f32(j), func=AF.Square), rd=scr["rd"], wr=[sq])
        groups = [list(range(nj))] if full else [[j] for j in range(nj)]
        for gi, g in enumerate(groups):
            ps1, ps2 = psf.next(), psf.next()
            for a, j in enumerate(g):
                P.op("pe", lambda e: e.matmul(ps1[:, 0:n], lhs[:], xb[:, j, 0:n], start=(a == 0), stop=(a == len(g) - 1)),
                     rd=[lhs, xb], wr=[ps1])
            for a, j in enumerate(g):
                P.op("pe", lambda e: e.matmul(ps2[:, 0:n], lhs[:], sq[:, j, 0:n], start=(a == 0), stop=(a == len(g) - 1)),
                     rd=[lhs, sq], wr=[ps2])
            mean, rstd = scr["mean"], scr["rstd"]
            P.op("act", lambda e: e.mul(mean[:, gi, 0:n], ps1[:, 0:n], inv_n), rd=[ps1], wr=[mean])
            P.op("dve", lambda e: e.tensor_tensor(rstd[:, gi, 0:n], mean[:, gi, 0:n], mean[:, gi, 0:n], ALU.mult),
                 rd=[mean], wr=[rstd])
            P.op("dve", lambda e: e.scalar_tensor_tensor(rstd[:, gi, 0:n], ps2[:, 0:n], inv_n, rstd[:, gi, 0:n],
                                                          ALU.mult, ALU.subtract), rd=[ps2, rstd], wr=[rstd])
            rsqrt(rstd[:, gi, 0:n], rstd[:, gi, 0:n], 1.0, eps, [rstd], [rstd])

    def conv_branch(l, last):
        mark = P.scope_begin()
        ug = Rot([P.sb([128, 8, 512], BF16, "ug") for _ in range(2)])
        zp = P.sb([128, 4, 8 * 94], F32, "zp")
        zc = P.sb([128, 4, 542], F32, "zc")
        sg = P.sb([128, 4, 512], F32, "sg")
        acc = P.sb([128, 4, 512], F32, "acc")
        scr = {"xb": P.sb([128, 4, 512], BF16, "xb"), "sq": P.sb([128, 4, 512], BF16, "sq"),
               "mean": P.sb([128, 1, 512], F32, "mean"), "rstd": P.sb([128, 1, 512], F32, "rstd"), "rd": [acc]}
        yo = Rot([P.sb([128, 4, 512], BF16, "yo") for _ in range(2)])
        P.op("dve", lambda e: e.memset(zp[:], 0.0), wr=[zp])
        P.op("dve", lambda e: e.memset(zc[:], 0.0), wr=[zc])
        o_dw = PCOLS["cdw"][0]
        for s in range(NS):
            for ti, (e0, n) in enumerate(cfg.tiles):
                isctx = e0 < CTX
                if last and isctx:
                    continue
                u = ug.next()
                P.dma("sp", u[:, :, 0:n], U[s, 15:23, :, e0:e0 + n].rearrange("c p t -> p c t"), rd=[L_U[s][ti]], wr=[u])
                P.op("act", lambda e: e.activation(out=sg[:, :, 0:n], in_=u[:, 4:8, 0:n], func=AF.Sigmoid), rd=[u], wr=[sg])
                if isctx:
                    assert CTX <= 512 and n == CTX
                    zbuf = zc
                    zin = zc[:, :, 15:15 + n]
                    P.op("dve", lambda e: e.tensor_tensor(zin, u[:, 0:4, 0:n], sg[:, :, 0:n], ALU.mult), rd=[u, sg], wr=[zc])
                    win = lambda j, tau: zc[:, j, tau:tau + n]
                    av = lambda j: acc[:, j, 0:n]
                else:
                    nr = n // 64
                    zbuf = zp
                    zin = zp[:, :, 0:nr * 94].rearrange("p j (r w) -> p j r w", w=94)[:, :, :, 15:79]
                    P.op("dve", lambda e: e.tensor_tensor(zin, u[:, 0:4, 0:n].rearrange("p j (r w) -> p j r w", w=64),
                                                           sg[:, :, 0:n].rearrange("p j (r w) -> p j r w", w=64), ALU.mult),
                         rd=[u, sg], wr=[zp])
                    win = lambda j, tau: zp[:, j, 0:nr * 94].rearrange("p (r w) -> p r w", w=94)[:, :, tau:tau + 64]
                    av = lambda j: acc[:, j, 0:n].rearrange("p (r w) -> p r w", w=64)
                for j in range(4):
                    eng = "dve"
                    P.op(eng, lambda e: e.tensor_scalar(av(j), win(j, 0), pc[:, o_dw + j * 31:o_dw + j * 31 + 1],
                                                        pcol("cdb", j), ALU.mult, ALU.add), rd=[zbuf, pc], wr=[acc])
                    for tau in range(1, 31):
                        P.op(eng, lambda e: e.scalar_tensor_tensor(av(j), win(j, tau),
                                                                   pc[:, o_dw + j * 31 + tau:o_dw + j * 31 + tau + 1],
                                                                   av(j), ALU.mult, ALU.add), rd=[zbuf, pc, acc], wr=[acc])
                stats_rstd(lambda j: acc[:, j, 0:n], 4, n, (ones_bf, True), LN_EPS, 1.0 / 512, scr)
                y = yo.next()
                for j in range(4):
                    P.op("dve", lambda e: e.tensor_tensor(acc[:, j, 0:n], acc[:, j, 0:n], scr["mean"][:, 0, 0:n], ALU.subtract),
                         rd=[acc, scr["mean"]], wr=[acc])
                    P.op("dve", lambda e: e.tensor_tensor(acc[:, j, 0:n], acc[:, j, 0:n], scr["rstd"][:, 0, 0:n], ALU.mult),
                         rd=[acc, scr["rstd"]], wr=[acc])
                    P.op("act", lambda e: e.activation(out=y[:, j, 0:n], in_=acc[:, j, 0:n], func=AF.Silu,
                                                       scale=pcol("clnw", j), bias=pcol("clnb", j)), rd=[acc, pc], wr=[y])
                P.dma("sp", YBR[s, 1, :, :, e0:e0 + n].rearrange("j p t -> p j t"), y[:, :, 0:n], rd=[y], wr=[L_Y[s][1][ti]])
        P.scope_end(mark)

    HM = dscr("HM", [NS, 2, TC, 512], F32)
    L_HM = [[LT() for d in range(2)] for s in range(NS)]
    selh_d = din("selh", [4, 4 * 128])
    NCK = TC // 64
    DH5 = 128 ** -0.5

    def mlstm_branch(l, last):
        mark = P.scope_begin()
        selh = P.sb([4, 4 * 128], F32, "selh")
        P.dma("sp", selh[:], selh_d, wr=[selh])
        mb = P.sb([4, 4], F32, "mb")
        for d in range(2):
            P.dma("sp", mb[:, 2 * d:2 * d + 2], mlb[l, d * 4:(d + 1) * 4, :], wr=[mb])
        mk = P.sb([64, 2, 64], F32, "mk")
        P.dma("sp", mk[:, 0, :], masks_d[1], wr=[mk])
        P.dma("sp", mk[:, 1, :], masks_d[3], wr=[mk])
        P.op("dve", lambda e: e.tensor_scalar(mk[:], mk[:], DH5, None, ALU.mult), rd=[mk], wr=[mk])
        PAD = 64
        gb = P.sb([4, 2, TC], BF16, "gb")
        IG = P.sb([4, TC], F32, "IG")
        Bc = P.sb([4, TC], F32, "Bc")
        G = P.sb([4, TC], F32, "G")
        MU = P.sb([4, TC + 2 * PAD], F32, "MU")
        Q2_ = P.sb([4, TC], F32, "Q2")
        Q = [G, IG, Q2_, Bc]
        CAR = P.sb([4, NCK], F32, "CAR")
        carb = P.sb([128, 4, NCK], F32, "carb")
        qk_rot = Rot([P.sb([128, 8, 512], BF16, "qk") for _ in range(1)])
        vk_rot = Rot([P.sb([128, 4, 512], BF16, "vk") for _ in range(2)])
        vtm = P.sb([64, 8, 4, 130], BF16, "vtm")
        ktm = P.sb([64, 8, 4, 128], BF16, "ktm")
        cols = Rot([P.sb([64, 16], F32, "cols") for _ in range(2)])
        Sb = Rot([P.sb([64, 64], BF16, "Sb") for _ in range(2)])
        Vp = Rot([P.sb([64, 130], BF16, "Vp") for _ in range(2)])
        tmpn = Rot([P.sb([64, 130], F32, "tmpn") for _ in range(2)])
        ne = Rot([P.sb([64, 130], F32, "ne") for _ in range(2)])
        dn = Rot([P.sb([64, 2], F32, "dn") for _ in range(2)])
        hst = Rot([P.sb([64, 8, 512], F32, "hst") for _ in range(1)])
        Cf = P.sb([128, 4, 130], F32, "Cf")
        Cb = P.sb([128, 4, 130], BF16, "Cb")
        P.op("dve", lambda e: e.memset(vtm[:], 1.0), wr=[vtm])
        for s in range(NS):
            for d in range(2):
                def seg(e_lo, e_hi):
                    if d == 0:
                        return slice(e_lo, e_hi)
                    return slice(e_lo - CTX, e_hi - CTX) if e_lo >= CTX else slice(T + e_lo, T + e_hi)
                for (lo, hi) in ((0, CTX), (CTX, TC)):
                    for gi in range(2):
                        P.dma("sp", gb[:, gi, seg(lo, hi)], U[s, 39, d * 8 + gi * 4:d * 8 + gi * 4 + 4, lo:hi],
                              rd=[t_ for t_ in L_U[s]], wr=[gb])
                P.op("act", lambda e: e.activation(out=IG[:], in_=gb[:, 0, :], func=AF.Identity, bias=mb[:, 2 * d:2 * d + 1]),
                     rd=[gb, mb], wr=[IG])
                P.op("act", lambda e: e.activation(out=Bc[:], in_=gb[:, 1, :], func=AF.Sigmoid, bias=mb[:, 2 * d + 1:2 * d + 2]),
                     rd=[gb, mb], wr=[Bc])
                P.op("act", lambda e: e.activation(out=Bc[:], in_=Bc[:], func=AF.Ln), rd=[Bc], wr=[Bc])
                rv = (lambda ap: ap) if d == 0 else (lambda ap: ap[:, ::-1])
                P.op("dve", lambda e: e.memset(MU[:], 1.0), wr=[MU])
                P.op("dve", lambda e: e.tensor_tensor_scan(rv(G[:]), rv(MU[:, 0:TC]), rv(Bc[:]), 0.0, ALU.mult, ALU.add),
                     rd=[MU, Bc], wr=[G])
                P.op("dve", lambda e: e.memset(MU[:], 0.0), rd=[G], wr=[MU])
                P.op("dve", lambda e: e.tensor_copy(Bc[:], G[:]), rd=[G], wr=[Bc])
                P.op("dve", lambda e: e.tensor_tensor(G[:], IG[:], Bc[:], ALU.subtract), rd=[IG, Bc], wr=[G])
                mu = MU[:, PAD:PAD + TC]
                P.op("dve", lambda e: e.tensor_tensor_scan(rv(mu), rv(G[:]), rv(G[:]), 0.0, ALU.max, ALU.max),
                     rd=[G], wr=[MU])
                mu3 = mu.rearrange("p (c t) -> p c t", t=64)
                if d == 0:
                    mu_last = mu3[:, :, 63:64]
                    mu_ent = MU[:, PAD - 1:PAD - 1 + TC].rearrange("p (c t) -> p c t", t=64)[:, :, 0:1]
                else:
                    mu_last = mu3[:, :, 0:1]
                    mu_ent = MU[:, PAD + 64:PAD + 64 + TC].rearrange("p (c t) -> p c t", t=64)[:, :, 0:1]
                v3 = lambda b: b[:].rearrange("p (c t) -> p c t", t=64)
                bc3 = lambda ap: ap.broadcast_to([4, NCK, 64])
                P.op("dve", lambda e: e.tensor_tensor(v3(Q[0]), v3(G), bc3(mu_last), ALU.subtract), rd=[G, MU], wr=[Q[0]])
                P.op("act", lambda e: e.activation(out=Q[0][:], in_=Q[0][:], func=AF.Exp), rd=[Q[0]], wr=[Q[0]])
                P.op("dve", lambda e: e.tensor_tensor(v3(Q[1]), bc3(mu_last), mu3, ALU.subtract), rd=[MU], wr=[Q[1]])
                P.op("act", lambda e: e.activation(out=Q[1][:], in_=Q[1][:], func=AF.Exp), rd=[Q[1]], wr=[Q[1]])
                P.op("dve", lambda e: e.tensor_tensor(v3(Q[2]), bc3(mu_ent), mu3, ALU.subtract), rd=[MU], wr=[Q[2]])
                P.op("act", lambda e: e.activation(out=Q[2][:], in_=Q[2][:], func=AF.Exp), rd=[Q[2]], wr=[Q[2]])
                P.op("dve", lambda e: e.tensor_tensor(Q[3][:], Bc[:], mu, ALU.add), rd=[Bc, MU], wr=[Q[3]])
                P.op("act", lambda e: e.activation(out=Q[3][:], in_=Q[3][:], func=AF.Exp, scale=-1.0), rd=[Q[3]], wr=[Q[3]])
                P.op("dve", lambda e: e.tensor_tensor(CAR[:].unsqueeze(2), mu_ent, mu_last, ALU.subtract), rd=[MU], wr=[CAR])
                P.op("act", lambda e: e.activation(out=CAR[:], in_=CAR[:], func=AF.Exp), rd=[CAR], wr=[CAR])
                for h in range(4):
                    ps = psf.next()
                    P.op("pe", lambda e: e.matmul(ps[:, 0:NCK], selh[:, h * 128:(h + 1) * 128], CAR[:], start=True, stop=True),
                         rd=[selh, CAR], wr=[ps])
                    P.op("dve", lambda e: e.tensor_copy(carb[:, h, :], ps[:, 0:NCK]), rd=[ps], wr=[carb])
                P.op("dve", lambda e: e.memset(Cf[:], 0.0), wr=[Cf])
                P.op("dve", lambda e: e.memset(Cb[:], 0.0), wr=[Cb])
                order = list(range(len(cfg.tiles)))
                nctx_t = sum(1 for (e0, n) in cfg.tiles if e0 < CTX)
                if d == 1:
                    order = list(range(nctx_t - 1, -1, -1)) + list(range(len(cfg.tiles) - 1, nctx_t - 1, -1))
                for ti in order:
                    e0, n = cfg.tiles[ti]
                    nck = n // 64
                    qk, vk = qk_rot.next(), vk_rot.next()
                    P.dma("sp", qk[:, :, 0:n], U[s, 23:31, :, e0:e0 + n].rearrange("c p t -> p c t"), rd=[L_U[s][ti]], wr=[qk])
                    P.dma("pool", vk[:, 0:4, 0:n], U[s, 31:35, :, e0:e0 + n].rearrange("c p t -> p c t"), rd=[L_U[s][ti]], wr=[vk])
                    for ck in range(nck):
                        for (src, srcbuf, dst, w) in ((lambda h: vk[:, h, ck * 64:(ck + 1) * 64], vk, vtm, 130),
                                                      (lambda h: qk[:, 4 + h, ck * 64:(ck + 1) * 64], qk, ktm, 128)):
                            pb = psb.next()
                            for h in range(4):
                                P.op("pe", lambda e: e.transpose(pb[0:64, h * 128:(h + 1) * 128], src(h), ident_bf[:]),
                                     rd=[srcbuf, ident_bf], wr=[pb])
                            P.op("act", lambda e: e.copy(dst[:, ck, :, 0:128], pb[0:64, 0:512].rearrange("p (h c) -> p h c", c=128)),
                                 rd=[pb], wr=[dst])
                    hs = hst.next()
                    ckorder = range(nck) if d == 0 else range(nck - 1, -1, -1)
                    for ck in ckorder:
                        ec = (e0 + ck * 64)
                        mc = seg(ec, ec + 64)
                        cidx = mc.start // 64
                        cl = cols.next()
                        psq = psf.next()
                        for q in range(4):
                            P.op("pe", lambda e: e.matmul(psq[0:64, q * 4:(q + 1) * 4], Q[q][:, mc], ident_f[0:4, 0:4],
                                                          start=True, stop=True), rd=[Q[q], ident_f], wr=[psq])
                        P.op("dve", lambda e: e.tensor_copy(cl[:], psq[0:64, 0:16]), rd=[psq], wr=[cl])
                        for h in range(4):
                            om, rho, win, em = (cl[:, q * 4 + h:q * 4 + h + 1] for q in range(4))
                            psS = psf.next()
                            P.op("pe", lambda e: e.matmul(psS[0:64, 0:64], qk[:, 4 + h, ck * 64:(ck + 1) * 64],
                                                          qk[:, h, ck * 64:(ck + 1) * 64], start=True, stop=True), rd=[qk], wr=[psS])
                            sb_ = Sb.next()
                            P.op("dve", lambda e: e.scalar_tensor_tensor(sb_[:], psS[0:64, 0:64], om, mk[:, d, :], ALU.mult, ALU.mult),
                                 rd=[psS, cl, mk], wr=[sb_])
                            psI = psf.next()
                            P.op("pe", lambda e: e.matmul(psI[0:64, 0:129], sb_[:], vtm[:, ck, h, 0:129], start=True, stop=True),
                                 rd=[sb_, vtm], wr=[psI])
                            psC = psf.next()
                            P.op("pe", lambda e: e.matmul(psC[0:64, 0:129], qk[:, h, ck * 64:(ck + 1) * 64], Cb[:, h, 0:129],
                                                          start=True, stop=True), rd=[qk, Cb], wr=[psC])
                            tn, nn, dd = tmpn.next(), ne.next(), dn.next()
                            P.op("act", lambda e: e.activation(out=tn[:, 0:129], in_=psC[0:64, 0:129], func=AF.Copy, scale=win),
                                 rd=[psC, cl], wr=[tn])
                            P.op("dve", lambda e: e.scalar_tensor_tensor(nn[:, 0:129], psI[0:64, 0:129], rho, tn[:, 0:129],
                                                                          ALU.mult, ALU.add), rd=[psI, cl, tn], wr=[nn])
                            P.op("dve", lambda e: e.scalar_tensor_tensor(dd[:, 0:1], nn[:, 128:129], -1.0, nn[:, 128:129],
                                                                          ALU.mult, ALU.max), rd=[nn], wr=[dd])
                            P.op("dve", lambda e: e.tensor_scalar(dd[:, 0:1], dd[:, 0:1], em, None, ALU.max),
                                 rd=[dd, cl], wr=[dd])
                            P.op("dve", lambda e: e.reciprocal(dd[:, 1:2], dd[:, 0:1]), rd=[dd], wr=[dd])
                            P.op("act", lambda e: e.activation(out=hs[:, ck, h * 128:(h + 1) * 128], in_=nn[:, 0:128],
                                                               func=AF.Copy, scale=dd[:, 1:2]), rd=[nn, dd], wr=[hs])
                            vp = Vp.next()
                            P.op("dve", lambda e: e.tensor_scalar(vp[:, 0:129], vtm[:, ck, h, 0:129], om, DH5, ALU.mult, ALU.mult),
                                 rd=[vtm, cl], wr=[vp])
                            psU = psf.next()
                            P.op("pe", lambda e: e.matmul(psU[:, 0:129], ktm[:, ck, h, :], vp[:, 0:129], start=True, stop=True),
                                 rd=[ktm, vp], wr=[psU])
                            P.op("dve", lambda e: e.scalar_tensor_tensor(Cf[:, h, 0:129], Cf[:, h, 0:129],
                                                                          carb[:, h, cidx:cidx + 1], psU[:, 0:129],
                                                                          ALU.mult, ALU.add), rd=[Cf, carb, psU], wr=[Cf])
                            P.op("act", lambda e: e.copy(Cb[:, h, 0:129], Cf[:, h, 0:129]), rd=[Cf], wr=[Cb])
                    P.dma("sp", HM[s, d, e0:e0 + n, :].rearrange("(c t) f -> t c f", t=64), hs[:, 0:nck, :], rd=[hs], wr=[L_HM[s][d]])
        P.scope_end(mark)
        mark = P.scope_begin()
        hf = Rot([P.sb([128, 512], F32, "hf") for _ in range(2)])
        hb = Rot([P.sb([128, 512], F32, "hb") for _ in range(2)])
        hnb = Rot([P.sb([128, 512], BF16, "hnb") for _ in range(2)])
        st = Rot([P.sb([128, 16], F32, "st") for _ in range(2)])
        og = Rot([P.sb([128, 4, 512], BF16, "og") for _ in range(2)])
        yc = Rot([P.sb([128, 4, 512], BF16, "yc") for _ in range(2)])
        for s in range(NS):
            for ti, (e0, n) in enumerate(cfg.tiles):
                if last and e0 < CTX:
                    continue
                o_ = og.next()
                P.dma("sp", o_[:, :, 0:n], U[s, 35:39, :, e0:e0 + n].rearrange("c p t -> p c t"), rd=[L_U[s][ti]], wr=[o_])
                P.op("act", lambda e: e.activation(out=o_[:, :, 0:n], in_=o_[:, :, 0:n], func=AF.Sigmoid), rd=[o_], wr=[o_])
                y = yc.next()
                for sub in range(n // 128):
                    a, b, hn, t_ = hf.next(), hb.next(), hnb.next(), st.next()
                    r0 = e0 + sub * 128
                    P.dma("sp", a[:], HM[s, 0, r0:r0 + 128, :], rd=[L_HM[s][0]], wr=[a])
                    P.dma("pool", b[:], HM[s, 1, r0:r0 + 128, :], rd=[L_HM[s][1]], wr=[b])
                    P.op("dve", lambda e: e.tensor_tensor(a[:], a[:], b[:], ALU.add), rd=[a, b], wr=[a])
                    P.op("dve", lambda e: e.reduce_sum(t_[:, 0:4], a[:].rearrange("p (h c) -> p h c", c=128), AX.X), rd=[a], wr=[t_])
                    for h in range(4):
                        P.op("act", lambda e: e.activation(out=b[:, h * 128:(h + 1) * 128], in_=a[:, h * 128:(h + 1) * 128],
                                                           func=AF.Square, accum_out=t_[:, 4 + h:5 + h]), rd=[a], wr=[b, t_])
                    P.op("dve", lambda e: e.tensor_scalar(t_[:, 0:4], t_[:, 0:4], 1.0 / 128, None, ALU.mult), rd=[t_], wr=[t_])
                    P.op("dve", lambda e: e.tensor_tensor(t_[:, 8:12], t_[:, 0:4], t_[:, 0:4], ALU.mult), rd=[t_], wr=[t_])
                    P.op("dve", lambda e: e.scalar_tensor_tensor(t_[:, 8:12], t_[:, 4:8], 1.0 / 128, t_[:, 8:12], ALU.mult, ALU.subtract),
                         rd=[t_], wr=[t_])
                    rsqrt(t_[:, 12:16], t_[:, 8:12], 1.0, LN_EPS, [t_], [t_])
                    for h in range(4):
                        P.op("dve", lambda e: e.tensor_scalar(hn[:, h * 128:(h + 1) * 128], a[:, h * 128:(h + 1) * 128],
                                                              t_[:, h:h + 1], t_[:, 12 + h:13 + h], ALU.subtract, ALU.mult),
                             rd=[a, t_], wr=[hn])
                    pb = psb.next()
                    for h in range(4):
                        P.op("pe", lambda e: e.transpose(pb[:, h * 128:(h + 1) * 128], hn[:, h * 128:(h + 1) * 128], ident_bf[:]),
                             rd=[hn, ident_bf], wr=[pb])
                    for h in range(4):
                        P.op("dve", lambda e: e.scalar_tensor_tensor(y[:, h, sub * 128:(sub + 1) * 128], pb[:, h * 128:(h + 1) * 128],
                                                                      pcol("mnw", h), o_[:, h, sub * 128:(sub + 1) * 128],
                                                                      ALU.mult, ALU.mult), rd=[pb, pc, o_], wr=[y])
                P.dma("sp", YBR[s, 2, :, :, e0:e0 + n].rearrange("j p t -> p j t"), y[:, :, 0:n], rd=[y], wr=[L_Y[s][2][ti]])
        P.scope_end(mark)

    RF = dscr("RF", [NS, 2, 4, 128, NCK, 4, 64], BF16)
    PLs = dscr("PLs", [NS, 2, 128, 4, NCK], F32)
    VT = dscr("VT", [NS, TC, 512], BF16)
    GB = dscr("GB", [NS, 2, 4, 128, TC], BF16)
    YR = dscr("YR", [NS, 2, 4, 128, TC], F32)
    L_RF = [[[LT() for _ in cfg.tiles] for d in range(2)] for s in range(NS)]
    L_VT = [[LT() for _ in cfg.tiles] for s in range(NS)]
    L_GB = [[LT() for _ in cfg.tiles] for s in range(NS)]
    L_YR = [[[LT() for _ in cfg.tiles] for d in range(2)] for s in range(NS)]
    C0 = float(np.exp(-0.5))

    def rwkv_branch(l, last):
        mark = P.scope_begin()
        wtmp = P.sb([128, 512], F32, "wtmp")
        w2p = P.sb([128, 2, 512], BF16, "w2p")
        a2p = P.sb([128, 2, 512], BF16, "a2p")
        g2b = P.sb([128, 512], BF16, "g2b")
        P.op("dve", lambda e: e.memset(w2p[:], 0.0), wr=[w2p])
        P.op("dve", lambda e: e.memset(a2p[:], 0.0), wr=[a2p])
        for (src, dst) in ((rw_w2, w2p), (rw_a2, a2p)):
            P.dma("sp", wtmp[:], src[l], wr=[wtmp])
            for d in range(2):
                P.op("dve", lambda e: e.tensor_copy(dst[d * 64:(d + 1) * 64, d, :], wtmp[d * 64:(d + 1) * 64, :]), rd=[wtmp], wr=[dst])
        P.dma("sp", wtmp[:], rw_g2[l], wr=[wtmp])
        P.op("dve", lambda e: e.tensor_copy(g2b[:], wtmp[:]), rd=[wtmp], wr=[g2b])
        coef0 = P.sb([128, 15], F32, "coef0")
        omka = P.sb([128, 4], F32, "omka")
        P.op("dve", lambda e: e.tensor_tensor(coef0[:], pcol("mu_p", 0, 15), pcol("mu_n", 0, 15), ALU.add), rd=[pc], wr=[coef0])
        P.op("dve", lambda e: e.tensor_scalar(coef0[:], coef0[:], -1.0, 1.0, ALU.mult, ALU.add), rd=[coef0], wr=[coef0])
        P.op("dve", lambda e: e.tensor_scalar(omka[:], pcol("ka", 0, 4), -1.0, 1.0, ALU.mult, ALU.add), rd=[pc], wr=[omka])
        rmask = P.sb([128, 2, 512], F32, "rmask")
        P.op("dve", lambda e: e.memset(rmask[:], 1.0), wr=[rmask])
        P.op("dve", lambda e: e.memset(rmask[:, 0, :].rearrange("p (c t) -> p c t", t=64)[:, :, 0:1], 0.0), wr=[rmask])
        P.op("dve", lambda e: e.memset(rmask[:, 1, :].rearrange("p (c t) -> p c t", t=64)[:, :, 63:64], 0.0), wr=[rmask])

        mark1 = P.scope_begin()
        ush = P.sb([128, 15, 514], BF16, "ush")
        xs = P.sb([128, 15, 512], F32, "xs")
        tmp = P.sb([128, 15, 512], F32, "tmp")
        sig = [P.sb([128, 4, 512], F32, "sig")] * 2
        aa = [P.sb([128, 4, 512], F32, "aa")] * 2
        gate = P.sb([128, 4, 512], F32, "gate")
        kkn = P.sb([128, 4, 512], F32, "kkn")
        kd = P.sb([128, 4, 512], F32, "kd")
        rf = P.sb([128, 4, 8, 4, 64], BF16, "rf")
        lb = P.sb([128, 3, 512], BF16, "lb")
        b4 = P.sb([128, 4, 512], BF16, "b4")
        gbt = P.sb([128, 2, 4, 512], BF16, "gbt")
        vtm = P.sb([128, 4, 512], BF16, "vtmr")
        plt = P.sb([128, 4, 8], F32, "plt")

        def bc(col_ap, nj, n):
            return col_ap.unsqueeze(2).broadcast_to([128, nj, n])

        for s in range(NS):
            for ti, (e0, n) in enumerate(cfg.tiles):
                nck = n // 64
                lo_edge = (e0 == 0 or e0 == CTX)
                hi_edge = (e0 + n == CTX or e0 + n == TC)
                P.dma("sp", ush[:, :, 1:n + 1], U[s, 0:15, :, e0:e0 + n].rearrange("c p t -> p c t"), rd=[L_U[s][ti]], wr=[ush])
                if lo_edge:
                    P.op("dve", lambda e: e.memset(ush[:, :, 0:1], 0.0), wr=[ush])
                else:
                    P.dma("sp", ush[:, :, 0:1], U[s, 0:15, :, e0 - 1:e0].rearrange("c p t -> p c t"), rd=[L_U[s][ti - 1]], wr=[ush],
                          allow_slow_non_contiguous=True)
                if hi_edge:
                    P.op("dve", lambda e: e.memset(ush[:, :, n + 1:n + 2], 0.0), wr=[ush])
                else:
                    P.dma("sp", ush[:, :, n + 1:n + 2], U[s, 0:15, :, e0 + n:e0 + n + 1].rearrange("c p t -> p c t"),
                          rd=[L_U[s][ti + 1]], wr=[ush], allow_slow_non_contiguous=True)
                X, Tm = xs[:, :, 0:n], tmp[:, :, 0:n]
                P.op("dve", lambda e: e.tensor_tensor(X, ush[:, :, 1:n + 1], bc(coef0[:], 15, n), ALU.mult), rd=[ush, coef0], wr=[xs])
                P.op("dve", lambda e: e.tensor_tensor(Tm, ush[:, :, 0:n], bc(pcol("mu_p", 0, 15), 15, n), ALU.mult), rd=[ush, pc], wr=[tmp])
                P.op("dve", lambda e: e.tensor_tensor(X, X, Tm, ALU.add), rd=[xs, tmp], wr=[xs])
                P.op("dve", lambda e: e.tensor_tensor(Tm, ush[:, :, 2:n + 2], bc(pcol("mu_n", 0, 15), 15, n), ALU.mult), rd=[ush, pc], wr=[tmp])
                P.op("dve", lambda e: e.tensor_tensor(X, X, Tm, ALU.add), rd=[xs, tmp], wr=[xs])
                r_, k_, v_ = xs[:, 0:4, 0:n], xs[:, 4:8, 0:n], xs[:, 8:12, 0:n]
                T0, T1, T2 = tmp[:, 0:4, 0:n], tmp[:, 4:8, 0:n], tmp[:, 8:12, 0:n]
                P.op("act", lambda e: e.activation(out=lb[:, 0, 0:n], in_=xs[:, 12, 0:n], func=AF.Tanh), rd=[xs], wr=[lb])
                P.op("act", lambda e: e.copy(lb[:, 1, 0:n], xs[:, 13, 0:n]), rd=[xs], wr=[lb])
                P.op("act", lambda e: e.activation(out=lb[:, 2, 0:n], in_=xs[:, 14, 0:n], func=AF.Sigmoid), rd=[xs], wr=[lb])
                for j in range(4):
                    ps = psf.next()
                    P.op("pe", lambda e: e.matmul(ps[:, 0:n], g2b[:, j * 128:(j + 1) * 128], lb[:, 2, 0:n], start=True, stop=True),
                         rd=[g2b, lb], wr=[ps])
                    P.op("act", lambda e: e.copy(gate[:, j, 0:n], ps[:, 0:n]), rd=[ps], wr=[gate])
                P.op("dve", lambda e: e.tensor_tensor(T0, k_, bc(pcol("kk", 0, 4), 4, n), ALU.mult), rd=[xs, pc], wr=[tmp])
                P.op("act", lambda e: e.activation(out=b4[:, :, 0:n], in_=T0, func=AF.Square), rd=[tmp], wr=[b4])
                for j in range(4):
                    ps = psf.next()
                    P.op("pe", lambda e: e.matmul(ps[:, 0:n], blk_bf[:], b4[:, j, 0:n], start=True, stop=True), rd=[blk_bf, b4], wr=[ps])
                    rsqrt(tmp[:, 4 + j, 0:n], ps[:, 0:n], 1.0, 1e-24, [ps], [tmp])
                P.op("dve", lambda e: e.tensor_tensor(kkn[:, :, 0:n], T0, T1, ALU.mult), rd=[tmp], wr=[kkn])
                P.op("dve", lambda e: e.tensor_tensor(T0, r_, k_, ALU.mult), rd=[xs], wr=[tmp])
                P.op("dve", lambda e: e.tensor_tensor(b4[:, :, 0:n], T0, bc(pcol("rk", 0, 4), 4, n), ALU.mult), rd=[tmp, pc], wr=[b4])
                for j in range(4):
                    ps = psf.next()
                    P.op("pe", lambda e: e.matmul(ps[:, 0:n], blk_bf[:], b4[:, j, 0:n], start=True, stop=True), rd=[blk_bf, b4], wr=[ps])
                    P.op("dve", lambda e: e.tensor_tensor(tmp[:, 4 + j, 0:n], ps[:, 0:n], xs[:, 8 + j, 0:n], ALU.mult), rd=[ps, xs], wr=[tmp])
                P.op("dve", lambda e: e.tensor_tensor(gbt[:, 1, :, 0:n], T1, gate[:, :, 0:n], ALU.mult), rd=[tmp, gate], wr=[gbt])
                P.op("act", lambda e: e.copy(gbt[:, 0, :, 0:n], gate[:, :, 0:n]), rd=[gate], wr=[gbt])
                for w in range(2):
                    P.dma("sp", GB[s, w, :, :, e0:e0 + n].rearrange("j p t -> p j t"), gbt[:, w, :, 0:n], rd=[gbt], wr=[L_GB[s][ti]])
                P.op("act", lambda e: e.copy(b4[:, :, 0:n], v_), rd=[xs], wr=[b4])
                for sub in range(n // 128):
                    pb = psb.next()
                    for j in range(4):
                        P.op("pe", lambda e: e.transpose(pb[:, j * 128:(j + 1) * 128], b4[:, j, sub * 128:(sub + 1) * 128], ident_bf[:]),
                             rd=[b4, ident_bf], wr=[pb])
                    P.op("dve", lambda e: e.tensor_copy(vtm[:, sub, :], pb[:, 0:512]), rd=[pb], wr=[vtm])
                P.dma("sp", VT[s, e0:e0 + n, :].rearrange("(a p) c -> p a c", p=128), vtm[:, 0:n // 128, :], rd=[vtm], wr=[L_VT[s][ti]])
                for d in range(2):
                    rv = (lambda ap: ap) if d == 0 else (lambda ap: ap[:, ::-1])
                    rfv = lambda kind: rf[:, :, 0:nck, kind, :]
                    v4 = lambda ap: ap.rearrange("p j (c t) -> p j c t", t=64)
                    for (wp, li, dst, bn) in ((w2p, 0, sig[d], "w0"), (a2p, 1, aa[d], "a0")):
                        for j in range(4):
                            ps = psf.next()
                            P.op("pe", lambda e: e.matmul(ps[:, 0:n], wp[:, d, j * 128:(j + 1) * 128], lb[:, li, 0:n], start=True, stop=True),
                                 rd=[wp, lb], wr=[ps])
                            P.op("act", lambda e: e.activation(out=dst[:, j, 0:n], in_=ps[:, 0:n], func=AF.Sigmoid,
                                                               bias=pcol(bn, d * 4 + j)), rd=[ps, pc], wr=[dst])
                    P.op("dve", lambda e: e.tensor_tensor(T0, aa[d][:, :, 0:n], bc(pcol("ka", 0, 4), 4, n), ALU.mult), rd=[aa[d], pc], wr=[tmp])
                    P.op("dve", lambda e: e.tensor_tensor(T0, T0, bc(omka[:], 4, n), ALU.add), rd=[tmp, omka], wr=[tmp])
                    P.op("dve", lambda e: e.tensor_tensor(kd[:, :, 0:n], T0, k_, ALU.mult), rd=[tmp, xs], wr=[kd])
                    for j in range(4):
                        P.op("dve", lambda e: e.tensor_tensor_scan(rv(tmp[:, 8 + j, 0:n]), rv(rmask[:, d, 0:n]), rv(sig[d][:, j, 0:n]),
                                                                   0.0, ALU.mult, ALU.add), rd=[rmask, sig[d]], wr=[tmp])
                    P.op("dve", lambda e: e.tensor_tensor(T0, T2, sig[d][:, :, 0:n], ALU.subtract), rd=[tmp, sig[d]], wr=[tmp])
                    P.op("act", lambda e: e.activation(out=T0, in_=T0, func=AF.Exp, scale=-C0), rd=[tmp], wr=[tmp])
                    P.op("dve", lambda e: e.tensor_scalar(T0, T0, -1.0, None, ALU.mult), rd=[tmp], wr=[tmp])
                    P.op("dve", lambda e: e.tensor_tensor(rfv(0), v4(kkn[:, :, 0:n]), v4(T0), ALU.mult), rd=[kkn, tmp], wr=[rf])
                    P.op("act", lambda e: e.activation(out=T0, in_=T2, func=AF.Exp, scale=-C0), rd=[tmp], wr=[tmp])
                    P.op("dve", lambda e: e.tensor_tensor(rfv(1), v4(r_), v4(T0), ALU.mult), rd=[xs, tmp], wr=[rf])
                    lastpos = 63 if d == 0 else 0
                    P.op("dve", lambda e: e.tensor_copy(plt[:, :, 0:nck], v4(T0)[:, :, :, lastpos]), rd=[tmp], wr=[plt])
                    P.dma("sp", PLs[s, d, :, :, e0 // 64:e0 // 64 + nck], plt[:, :, 0:nck], rd=[plt], wr=[L_RF[s][d][ti]])
                    P.op("act", lambda e: e.activation(out=T0, in_=T2, func=AF.Exp, scale=C0), rd=[tmp], wr=[tmp])
                    P.op("dve", lambda e: e.tensor_tensor(T1, kkn[:, :, 0:n], aa[d][:, :, 0:n], ALU.mult), rd=[kkn, aa[d]], wr=[tmp])
                    P.op("dve", lambda e: e.tensor_tensor(rfv(2), v4(T1), v4(T0), ALU.mult), rd=[tmp], wr=[rf])
                    P.op("dve", lambda e: e.tensor_tensor(rfv(3), v4(kd[:, :, 0:n]), v4(T0), ALU.mult), rd=[kd, tmp], wr=[rf])
                    for j in range(4):
                        P.dma("sp" if j % 2 == 0 else "pool", RF[s, d, j, :, e0 // 64:e0 // 64 + nck, :, :], rf[:, j, 0:nck, :, :],
                              rd=[rf], wr=[L_RF[s][d][ti]])
        P.scope_end(mark1)
        if cfg.stop_after == "rwB1":
            P.scope_end(mark)
            return

        mark2 = P.scope_begin()
        mk2 = P.sb([64, 2, 128], F32, "mk2")
        mkT = P.sb([64, 2, 64], F32, "mkT")
        for d in range(2):
            P.dma("sp", mk2[:, d, 0:64], masks_d[2 * d], wr=[mk2])
            P.dma("sp", mk2[:, d, 64:128], masks_d[2 * d + 1], wr=[mk2])
            P.dma("sp", mkT[:, d, :], masks_d[2 - 2 * d], wr=[mkT])
        rft = Rot([P.sb([64, 8, 8, 4, 64], BF16, "rft") for _ in range(2)])
        vt2 = Rot([P.sb([64, 8, 512], BF16, "vt2") for _ in range(2)])
        plr = Rot([P.sb([64, 8, 8], F32, "plr") for _ in range(2)])
        ysb = Rot([P.sb([64, 8, 512], F32, "ysb") for _ in range(2)])
        ST = P.sb([64, 8, 64], F32, "ST")
        STb = P.sb([64, 8, 64], BF16, "STb")
        AB = P.sb([64, 8, 128], BF16, "AB")
        AK = P.sb([64, 8, 128], BF16, "AK")
        NM = P.sb([64, 2, 6, 8, 64], BF16, "NM")
        TH = P.sb([64, 8, 64], BF16, "TH")
        TT = P.sb([64, 8, 64], BF16, "TT")
        Zs = P.sb([64, 2, 8, 64], BF16, "Zs")
        Wb = P.sb([64, 8, 64], BF16, "Wb")
        lvm_f = P.sb([64, 6, 64], F32, "lvm_f")
        lvm = P.sb([64, 6, 64], BF16, "lvm")
        P.dma("sp", lvm_f[:], lvmask_d.rearrange("l i t -> i l t"), wr=[lvm_f])
        P.op("dve", lambda e: e.tensor_copy(lvm[:], lvm_f[:]), rd=[lvm_f], wr=[lvm])
        XW = P.sb([64, 8, 128], BF16, "XW")
        Wf = P.sb([64, 8, 64], F32, "Wf")
        KB = P.sb([64, 8, 2, 64], BF16, "KB")
        j3 = lambda ap, c: ap.rearrange("p (j c) -> p j c", c=c)
        HS = list(range(8))
        for s in range(NS):
            for d in range(2):
                P.op("dve", lambda e: e.memset(ST[:], 0.0), wr=[ST])
                P.op("dve", lambda e: e.memset(STb[:], 0.0), wr=[STb])
                order = list(range(len(cfg.tiles)))
                nctx_t = sum(1 for (e0, n) in cfg.tiles if e0 < CTX)
                if d == 1:
                    order = list(range(nctx_t - 1, -1, -1)) + list(range(len(cfg.tiles) - 1, nctx_t - 1, -1))
                for ti in order:
                    e0, n = cfg.tiles[ti]
                    nck = n // 64
                    c0 = e0 // 64
                    rt, vt, pl, ys = rft.next(), vt2.next(), plr.next(), ysb.next()
                    for h in HS:
                        j, hl = h // 2, h % 2
                        P.dma("sp" if h % 2 == 0 else "pool", rt[:, h, 0:nck, :, :], RF[s, d, j, hl * 64:(hl + 1) * 64, c0:c0 + nck, :, :],
                              rd=[L_RF[s][d][ti]], wr=[rt])
                        P.dma("sp", pl[:, h, 0:nck], PLs[s, d, hl * 64:(hl + 1) * 64, j, c0:c0 + nck], rd=[L_RF[s][d][ti]], wr=[pl])
                    P.dma("pool", vt[:, 0:nck, :], VT[s, e0:e0 + n, :].rearrange("(c t) f -> t c f", t=64), rd=[L_VT[s][ti]], wr=[vt])
                    for ck in (range(nck) if d == 0 else range(nck - 1, -1, -1)):
                        Vh = lambda h: vt[:, ck, h * 64:(h + 1) * 64]
                        G2 = ((0, slice(0, 4)), (1, slice(4, 8)))
                        psB, psK, psT = [psf.next(), psf.next()], [psf.next(), psf.next()], psf.next()
                        for h in HS:
                            g, c = h // 4, h % 4
                            P.op("pe", lambda e: e.matmul(psB[g][0:64, c * 128:(c + 1) * 128], rt[:, h, ck, 2, :], rt[:, h, ck, 0:2, :],
                                                          start=True, stop=True), rd=[rt], wr=[psB[g]])
                            P.op("pe", lambda e: e.matmul(psK[g][0:64, c * 128:(c + 1) * 128], rt[:, h, ck, 3, :], rt[:, h, ck, 0:2, :],
                                                          start=True, stop=True), rd=[rt], wr=[psK[g]])
                            P.op("pe", lambda e: e.matmul(psT[0:64, h * 64:(h + 1) * 64], rt[:, h, ck, 0, :], rt[:, h, ck, 2, :],
                                                          start=True, stop=True), rd=[rt], wr=[psT])
                        m2 = mk2[:, d, :].unsqueeze(1).broadcast_to([64, 4, 128])
                        for g, hs in G2:
                            P.op("dve", lambda e: e.tensor_tensor(AB[:, hs, :], j3(psB[g][0:64, 0:512], 128), m2, ALU.mult), rd=[psB[g], mk2], wr=[AB])
                            P.op("dve", lambda e: e.tensor_tensor(AK[:, hs, :], j3(psK[g][0:64, 0:512], 128), m2, ALU.mult), rd=[psK[g], mk2], wr=[AK])
                        P.op("dve", lambda e: e.tensor_tensor(XW[:, :, 0:64], j3(psT[0:64, 0:512], 64),
                                                               mkT[:, d, :].unsqueeze(1).broadcast_to([64, 8, 64]), ALU.mult),
                             rd=[psT, mkT], wr=[XW])
                        lv4 = lvm[:].unsqueeze(2).broadcast_to([64, 6, 8, 64])
                        P.op("dve", lambda e: e.tensor_tensor(NM[:, 0], AB[:, :, 0:64].unsqueeze(1).broadcast_to([64, 6, 8, 64]), lv4, ALU.mult),
                             rd=[AB, lvm], wr=[NM])
                        P.op("dve", lambda e: e.tensor_tensor(NM[:, 1], XW[:, :, 0:64].unsqueeze(1).broadcast_to([64, 6, 8, 64]), lv4, ALU.mult),
                             rd=[XW, lvm], wr=[NM])
                        idb = ident_bf[0:64, 0:64].unsqueeze(1).broadcast_to([64, 8, 64])
                        P.op("dve", lambda e: e.tensor_tensor(TH[:], NM[:, 0, 0], idb, ALU.add), rd=[NM, ident_bf], wr=[TH])
                        P.op("dve", lambda e: e.tensor_tensor(TT[:], NM[:, 1, 0], idb, ALU.add), rd=[NM, ident_bf], wr=[TT])
                        psW = psf.next()
                        for h in HS:
                            o_ = psW[0:64, h * 64:(h + 1) * 64]
                            P.op("pe", lambda e: e.matmul(o_, rt[:, h, ck, 0, :], STb[:, h, :], start=True, stop=False), rd=[rt, STb], wr=[psW])
                            P.op("pe", lambda e: e.matmul(o_, AK[:, h, 0:64], Vh(h), start=False, stop=True), rd=[AK, vt], wr=[psW])
                        P.op("act", lambda e: e.copy(Wb[:], j3(psW[0:64, 0:512], 64)), rd=[psW], wr=[Wb])
                        for lvl in range(1, 6):
                            psZ, psZp = psf.next(), psf.next()
                            for h in HS:
                                P.op("pe", lambda e: e.matmul(psZ[0:64, h * 64:(h + 1) * 64], NM[:, 1, lvl, h, :], TH[:, h, :],
                                                              start=True, stop=True), rd=[NM, TH], wr=[psZ])
                                P.op("pe", lambda e: e.matmul(psZp[0:64, h * 64:(h + 1) * 64], NM[:, 0, lvl, h, :], TT[:, h, :],
                                                              start=True, stop=True), rd=[NM, TT], wr=[psZp])
                            P.op("act", lambda e: e.copy(Zs[:, 0], j3(psZ[0:64, 0:512], 64)), rd=[psZ], wr=[Zs])
                            P.op("dve", lambda e: e.tensor_copy(Zs[:, 1], j3(psZp[0:64, 0:512], 64)), rd=[psZp], wr=[Zs])
                            psA, psBt = psf.next(), psf.next()
                            for h in HS:
                                P.op("pe", lambda e: e.matmul(psA[0:64, h * 64:(h + 1) * 64], TT[:, h, :], Zs[:, 0, h, :],
                                                              start=True, stop=True), rd=[TT, Zs], wr=[psA])
                                P.op("pe", lambda e: e.matmul(psBt[0:64, h * 64:(h + 1) * 64], TH[:, h, :], Zs[:, 1, h, :],
                                                              start=True, stop=True), rd=[TH, Zs], wr=[psBt])
                            P.op("dve", lambda e: e.tensor_tensor(TH[:], TH[:], j3(psA[0:64, 0:512], 64), ALU.add), rd=[TH, psA], wr=[TH])
                            P.op("dve", lambda e: e.tensor_tensor(TT[:], TT[:], j3(psBt[0:64, 0:512], 64), ALU.add), rd=[TT, psBt], wr=[TT])
                        psUu = psf.next()
                        for h in HS:
                            P.op("pe", lambda e: e.matmul(psUu[0:64, h * 64:(h + 1) * 64], TH[:, h, :], Wb[:, h, :], start=True, stop=True),
                                 rd=[TH, Wb], wr=[psUu])
                        P.op("act", lambda e: e.copy(XW[:, :, 64:128], j3(psUu[0:64, 0:512], 64)), rd=[psUu], wr=[XW])
                        Ub = lambda h: XW[:, h, 64:128]
                        psY = psf.next()
                        for h in HS:
                            o_ = psY[0:64, h * 64:(h + 1) * 64]
                            P.op("pe", lambda e: e.matmul(o_, STb[:, h, :], rt[:, h, ck, 1, :], start=True, stop=False), rd=[STb, rt], wr=[psY])
                            P.op("pe", lambda e: e.matmul(o_, Vh(h), AK[:, h, 64:128], start=False, stop=False), rd=[vt, AK], wr=[psY])
                            P.op("pe", lambda e: e.matmul(o_, Ub(h), AB[:, h, 64:128], start=False, stop=True), rd=[XW, AB], wr=[psY])
                        P.op("act", lambda e: e.copy(ys[:, :, ck * 64:(ck + 1) * 64], j3(psY[0:64, 0:512], 64)), rd=[psY], wr=[ys])
                        psX = [psf.next(), psf.next()]
                        for h in HS:
                            g, c = h // 4, h % 4
                            for w, kind in enumerate((3, 2)):
                                P.op("pe", lambda e: e.matmul(psX[g][0:64, (c * 2 + w) * 64:(c * 2 + w + 1) * 64], rt[:, h, ck, kind, :],
                                                              ident_bf[0:64, 0:64], start=True, stop=True), rd=[rt, ident_bf], wr=[psX[g]])
                        for g, hs in G2:
                            P.op("dve", lambda e: e.tensor_copy(KB[:, hs, :, :], psX[g][0:64, 0:512].rearrange("p (j w c) -> p j w c", w=2, c=64)),
                                 rd=[psX[g]], wr=[KB])
                        psS = psf.next()
                        for h in HS:
                            o_ = psS[0:64, h * 64:(h + 1) * 64]
                            P.op("pe", lambda e: e.matmul(o_, KB[:, h, 0, :], Vh(h), start=True, stop=False), rd=[KB, vt], wr=[psS])
                            P.op("pe", lambda e: e.matmul(o_, KB[:, h, 1, :], Ub(h), start=False, stop=True), rd=[KB, XW], wr=[psS])
                        P.op("dve", lambda e: e.tensor_tensor(ST[:], ST[:], j3(psS[0:64, 0:512], 64), ALU.add), rd=[ST, psS], wr=[ST])
                        P.op("dve", lambda e: e.tensor_tensor(ST[:], ST[:], pl[:, :, ck:ck + 1].broadcast_to([64, 8, 64]), ALU.mult),
                             rd=[ST, pl], wr=[ST])
                        P.op("act", lambda e: e.copy(STb[:], ST[:]), rd=[ST], wr=[STb])
                    for hl in range(2):
                        P.dma("sp", YR[s, d, :, hl * 64:(hl + 1) * 64, e0:e0 + n].rearrange("j p t -> p j t"), ys[:, hl::2, 0:n],
                              rd=[ys], wr=[L_YR[s][d][ti]])
        P.scope_end(mark2)
        if cfg.stop_after == "rwB2":
            P.scope_end(mark)
            return

        mark3 = P.scope_begin()
        ya = Rot([P.sb([128, 4, 512], F32, "ya") for _ in range(2)])
        yb_ = Rot([P.sb([128, 4, 512], F32, "yb") for _ in range(2)])
        gbl = Rot([P.sb([128, 2, 4, 512], BF16, "gbl") for _ in range(2)])
        yo = Rot([P.sb([128, 4, 512], BF16, "yor") for _ in range(2)])
        scr = {"xb": P.sb([128, 4, 512], BF16, "xbr"), "sq": P.sb([128, 4, 512], BF16, "sqr"),
               "mean": P.sb([128, 4, 512], F32, "meanr"), "rstd": P.sb([128, 4, 512], F32, "rstdr"), "rd": []}
        for s in range(NS):
            for ti, (e0, n) in enumerate(cfg.tiles):
                if last and e0 < CTX:
                    continue
                a, b, g, y = ya.next(), yb_.next(), gbl.next(), yo.next()
                P.dma("sp", a[:, :, 0:n], YR[s, 0, :, :, e0:e0 + n].rearrange("j p t -> p j t"), rd=[L_YR[s][0][ti]], wr=[a])
                P.dma("pool", b[:, :, 0:n], YR[s, 1, :, :, e0:e0 + n].rearrange("j p t -> p j t"), rd=[L_YR[s][1][ti]], wr=[b])
                for w in range(2):
                    P.dma("sp", g[:, w, :, 0:n], GB[s, w, :, :, e0:e0 + n].rearrange("j p t -> p j t"), rd=[L_GB[s][ti]], wr=[g])
                P.op("dve", lambda e: e.tensor_tensor(a[:, :, 0:n], a[:, :, 0:n], b[:, :, 0:n], ALU.add), rd=[a, b], wr=[a])
                scr["rd"] = [a]
                stats_rstd(lambda j: a[:, j, 0:n], 4, n, (blk_bf, False), RW_GN_EPS, 1.0 / 64, scr)
                A_, B_ = a[:, :, 0:n], b[:, :, 0:n]
                P.op("dve", lambda e: e.tensor_tensor(A_, A_, scr["mean"][:, :, 0:n], ALU.subtract), rd=[a, scr["mean"]], wr=[a])
                P.op("dve", lambda e: e.tensor_tensor(A_, A_, scr["rstd"][:, :, 0:n], ALU.mult), rd=[a, scr["rstd"]], wr=[a])
                P.op("dve", lambda e: e.tensor_tensor(A_, A_, bc(pcol("rlnw", 0, 4), 4, n), ALU.mult), rd=[a, pc], wr=[a])
                P.op("dve", lambda e: e.tensor_tensor(A_, A_, bc(pcol("rlnb", 0, 4), 4, n), ALU.add), rd=[a, pc], wr=[a])
                P.op("dve", lambda e: e.tensor_tensor(A_, A_, g[:, 0, :, 0:n], ALU.mult), rd=[a, g], wr=[a])
                P.op("dve", lambda e: e.tensor_tensor(y[:, :, 0:n], A_, g[:, 1, :, 0:n], ALU.add), rd=[a, g], wr=[y])
                P.dma("sp", YBR[s, 0, :, :, e0:e0 + n].rearrange("j p t -> p j t"), y[:, :, 0:n], rd=[y], wr=[L_Y[s][0][ti]])
        P.scope_end(mark3)
        P.scope_end(mark)

    modulation()
    for l in range(DEPTH):
        cast_layer(l)
    for l in range(DEPTH):
        layer_params(l)
        last = (l == DEPTH - 1)
        phase_A(l)
        if cfg.stop_after == "A":
            break
        conv_branch(l, last)
        if cfg.stop_after == "conv":
            break
        if cfg.stop_after not in ("rw", "rwB1", "rwB2"):
            mlstm_branch(l, last)
        if cfg.stop_after == "ml":
            break
        rwkv_branch(l, last)
        if cfg.stop_after in ("rw", "rwB1", "rwB2", "noC"):
            break
        phase_C(l, last)

    P.barrier()
    return nc, P


def host_maps(inp, cfg, n_cores):
    NS, DEPTH = cfg.NS, cfg.DEPTH
    f = lambda a: np.ascontiguousarray(np.asarray(a, np.float32))
    shared = {
        "w_ada": f(inp["w_ada"]), "b_ada": f(inp["b_ada"]),
        "w_in": pad_w_in(f(inp["w_in"])),
        "pcols": np.stack([pack_pcols(inp, l) for l in range(DEPTH)]),
        "rw_w2": f(inp["rw_w2"]).reshape(DEPTH, 128, 512),
        "rw_a2": f(inp["rw_a2"]).reshape(DEPTH, 128, 512),
        "rw_g2": f(inp["rw_g2"]),
        "mlb": np.ascontiguousarray(np.concatenate([np.stack([f(inp["ml_ib"]).reshape(DEPTH, 8), f(inp["ml_fb"]).reshape(DEPTH, 8)], -1)] * 2, 1)),
        "p_abc": np.ascontiguousarray(np.stack([f(inp["p_a"]), f(inp["p_b"]), f(inp["p_c"])], 1)),
        "w_out": f(inp["w_out"]), "w_mlp1": f(inp["w_mlp1"]), "w_mlp2": f(inp["w_mlp2"]),
        "g_final": f(inp["g_final"]).reshape(1, D),
        "ident": np.eye(128, dtype=np.float32),
        "selh": np.ascontiguousarray(np.kron(np.eye(4, dtype=np.float32), np.ones((1, 128), np.float32))),
        "blkones": np.kron(np.eye(2, dtype=np.float32), np.ones((64, 64), np.float32)),
    }
    idx = np.arange(64)
    shared["masks"] = np.stack([(idx[:, None] < idx[None, :]), (idx[:, None] <= idx[None, :]),
                                (idx[:, None] > idx[None, :]), (idx[:, None] >= idx[None, :])]).astype(np.float32)
    lv = lambda sz: ((idx[:, None] // (2 * sz) == idx[None, :] // (2 * sz)) & (idx[:, None] // sz != idx[None, :] // sz))
    shared["lvmask"] = np.stack([lv(sz) for sz in (1, 2, 4, 8, 16, 32)]).astype(np.float32)
    maps = []
    x, c, ctx, c_ctx = f(inp["x"]), f(inp["c"]), f(inp["ctx"]), f(inp["c_ctx"])
    for i in range(n_cores):
        m = dict(shared)
        m["x"] = x[i * NS:(i + 1) * NS]
        m["ctx"] = ctx[i * NS:(i + 1) * NS]
        cv = np.zeros((3, D), np.float32)
        cv[0:NS] = c[i * NS:(i + 1) * NS]
        cv[2] = c_ctx
        m["cvec"] = cv
        maps.append(m)
    return maps


_CACHE = {}


def kernel(**inputs):
    n_cores = 8
    B = inputs["x"].shape[0]
    cfg = Cfg(NS=B // n_cores, T=inputs["x"].shape[1], CTX=inputs["ctx"].shape[1], DEPTH=inputs["w_ada"].shape[0])
    key = (cfg.NS, cfg.T, cfg.CTX, cfg.DEPTH)
    if key not in _CACHE:
        _CACHE[key] = build(cfg)
    nc, _ = _CACHE[key]
    maps = host_maps(inputs, cfg, n_cores)
    res = run_bass_kernel_spmd(nc, maps, core_ids=list(range(n_cores)))
    return np.concatenate([np.asarray(r["out"], np.float32) for r in res.results], axis=0)
```

```python
import numpy as np
import concourse.bass as bass
import concourse.mybir as mybir
from concourse.bass_utils import run_bass_kernel_spmd

F32 = mybir.dt.float32
BF16 = mybir.dt.bfloat16
ALU = mybir.AluOpType
AF = mybir.ActivationFunctionType
AX = mybir.AxisListType

D = 1024
KC = 8
NCH = 64
PW = NCH * 128
DFF = 4096
L_CH = 64
NORM_EPS = 1e-6
LN_EPS = 1e-5
RW_GN_EPS = 64e-5

SELF_WAIT = True


class LT:
    __slots__ = ("name", "w", "r")

    def __init__(self, name=""):
        self.name = name
        self.w = None
        self.r = {}


class Buf:
    def __init__(self, t, lt, psum=False):
        self.t = t
        self.lt = lt
        self.psum = psum

    def __getitem__(self, k):
        return self.t[k]


class Prog:
    def __init__(self, nc, n_dma_sems=6):
        self.nc = nc
        self.engs = {"pe": nc.tensor, "act": nc.scalar, "dve": nc.vector,
                     "pool": nc.gpsimd, "sp": nc.sync}
        self.sems = {}
        self.cnt = {}
        for k in self.engs:
            self.sems[k] = nc.alloc_semaphore(name="s_" + k)
            self.cnt[k] = 0
        self.dq = {}
        for q in ("sp", "pool", "act"):
            lst = []
            for i in range(n_dma_sems):
                key = "d_%s%d" % (q, i)
                self.sems[key] = nc.alloc_semaphore(name=key)
                self.cnt[key] = 0
                lst.append(key)
            self.dq[q] = [lst, 0]
        self.obs = {k: {} for k in self.engs}
        self.ninst = 0
        self.nbuf = 0

    def _need(self, e, deps):
        for (k, v) in deps:
            if k == e and (e == "pe" or not SELF_WAIT):
                continue
            if self.obs[e].get(k, 0) < v:
                self.engs[e].wait_ge(self.sems[k], v)
                self.obs[e][k] = v
                self.ninst += 1

    @staticmethod
    def _lts(bufs):
        out = []
        for b in bufs:
            if isinstance(b, LT):
                out.append(b)
            elif isinstance(b, Buf):
                out.append(b.lt)
            else:
                raise TypeError(b)
        return out

    def _deps(self, reads, writes):
        deps = []
        for t in reads:
            if t.w is not None:
                deps.append(t.w)
        for t in writes:
            if t.w is not None:
                deps.append(t.w)
            for k, v in t.r.items():
                deps.append((k, v))
        return deps

    def op(self, e, fn, rd=(), wr=()):
        if e != "pe":
            pr = [b for b in rd if isinstance(b, Buf) and b.psum]
            if pr:
                rd = [b for b in rd if not (isinstance(b, Buf) and b.psum)]
                wr = list(wr) + [b for b in pr if b not in wr]
        reads, writes = self._lts(rd), self._lts(wr)
        self._need(e, self._deps(reads, writes))
        ins = fn(self.engs[e])
        self.cnt[e] += 1
        c = self.cnt[e]
        ins.then_inc(self.sems[e], 1)
        self.ninst += 1
        for t in reads:
            t.r[e] = c
        for t in writes:
            t.w = (e, c)
            t.r = {}
        return ins

    def dma(self, q, out, in_, rd=(), wr=(), **kw):
        reads, writes = self._lts(rd), self._lts(wr)
        lst, idx = self.dq[q]
        key = lst[idx % len(lst)]
        self.dq[q][1] = idx + 1
        deps = self._deps(reads, writes)
        if self.cnt[key] > 0:
            deps.append((key, self.cnt[key]))
        self._need(q, deps)
        ins = self.engs[q].dma_start(out=out, in_=in_, **kw)
        self.cnt[key] += 16
        c = self.cnt[key]
        ins.then_inc(self.sems[key], 16)
        self.ninst += 1
        for t in reads:
            t.r[key] = c
        for t in writes:
            t.w = (key, c)
            t.r = {}
        return ins

    def finish(self, bufs, e="sp"):
        deps = []
        for t in self._lts(bufs):
            if t.w is not None:
                deps.append(t.w)
        self._need(e, deps)

    def barrier(self):
        allc = [(k, v) for k, v in self.cnt.items() if v > 0]
        for e in self.engs:
            self._need(e, allc)

    def scope_begin(self):
        return (self.nc.sbuf_base, self.nc.sbuf_top)

    def scope_end(self, mark):
        self.barrier()
        self.nc.sbuf_base, self.nc.sbuf_top = mark

    def sb(self, shape, dt, name=None):
        self.nbuf += 1
        nm = "%s_%d" % (name or "b", self.nbuf)
        return Buf(self.nc.alloc_sbuf_tensor(nm, list(shape), dt), LT(nm))

    def ps(self, shape, dt, name=None):
        self.nbuf += 1
        nm = "%s_%d" % (name or "p", self.nbuf)
        return Buf(self.nc.alloc_psum_tensor(nm, list(shape), dt), LT(nm), psum=True)


class Rot:
    def __init__(self, bufs):
        self.bufs = bufs
        self.i = 0

    def next(self):
        b = self.bufs[self.i % len(self.bufs)]
        self.i += 1
        return b


PCOLS = {}
_off = 0
for _nm, _n in (("mu_p", 15), ("mu_n", 15), ("w0", 8), ("a0", 8), ("kk", 4), ("ka", 4), ("rk", 4),
                ("rlnw", 4), ("rlnb", 4), ("cdw", 4 * 31), ("cdb", 4), ("clnw", 4), ("clnb", 4),
                ("mnw", 4), ("g1", 8), ("g2", 8)):
    PCOLS[_nm] = (_off, _n)
    _off += _n
NPCOL = _off


def _cm(vec, nchunks):
    return np.ascontiguousarray(np.asarray(vec, np.float32).reshape(nchunks, 128).T)


def pack_pcols(inp, l):
    out = np.zeros((128, NPCOL), np.float32)

    def put(nm, arr):
        o, n = PCOLS[nm]
        assert arr.shape == (128, n), (nm, arr.shape)
        out[:, o:o + n] = arr
    put("mu_p", _cm(inp["mu_prev"][l], 15))
    put("mu_n", _cm(inp["mu_next"][l], 15))
    put("w0", np.concatenate([_cm(inp["rw_w0"][l, d], 4) for d in range(2)], 1))
    put("a0", np.concatenate([_cm(inp["rw_a0"][l, d], 4) for d in range(2)], 1))
    put("kk", _cm(inp["rw_kk"][l], 4))
    put("ka", _cm(inp["rw_ka"][l], 4))
    put("rk", _cm(inp["rw_rk"][l].reshape(-1), 4))
    put("rlnw", _cm(inp["rw_lnw"][l], 4))
    put("rlnb", _cm(inp["rw_lnb"][l], 4))
    dw = np.asarray(inp["cv_dw"][l], np.float32)
    put("cdw", np.concatenate([np.ascontiguousarray(dw[:, j * 128:(j + 1) * 128].T) for j in range(4)], 1))
    put("cdb", _cm(inp["cv_db"][l], 4))
    put("clnw", _cm(inp["cv_lnw"][l], 4))
    put("clnb", _cm(inp["cv_lnb"][l], 4))
    put("mnw", _cm(inp["ml_nw"][l], 4))
    put("g1", _cm(inp["g_norm1"][l], 8))
    put("g2", _cm(inp["g_norm2"][l], 8))
    return out


def pad_w_in(w):
    Ln = w.shape[0]
    out = np.zeros((Ln, D, PW), np.float32)
    out[:, :, 0:5008] = w[:, :, 0:5008]
    out[:, :, 5120:8192] = w[:, :, 5008:8080]
    return out


class Cfg:
    def __init__(self, NS=2, T=4096, CTX=256, DEPTH=4, debug=False, stop_after=None, c_stop=9):
        self.c_stop = c_stop
        self.NS, self.T, self.CTX, self.DEPTH = NS, T, CTX, DEPTH
        self.TC = T + CTX
        self.debug = debug
        self.stop_after = stop_after
        self.tiles = []
        e = 0
        while e < CTX:
            n = min(512, CTX - e)
            self.tiles.append((e, n)); e += n
        while e < self.TC:
            n = min(512, self.TC - e)
            self.tiles.append((e, n)); e += n


def build(cfg):
    NS, T, CTX, DEPTH, TC = cfg.NS, cfg.T, cfg.CTX, cfg.DEPTH, cfg.TC
    nc = bass.Bass("TRN2", target_bir_lowering=False)
    P = Prog(nc)
    kind_dbg = "ExternalOutput" if cfg.debug else "Internal"

    def din(name, shape, dt=F32):
        return nc.dram_tensor(name, list(shape), dt, kind="ExternalInput").ap()

    def dscr(name, shape, dt, dbg=False):
        return nc.dram_tensor(name, list(shape), dt, kind=(kind_dbg if dbg else "Internal")).ap()

    x_in = din("x", [NS, T, D])
    ctx_in = din("ctx", [NS, CTX, D])
    cvec = din("cvec", [3, D])
    w_ada = din("w_ada", [DEPTH, D, 6 * D])
    b_ada = din("b_ada", [DEPTH, 6 * D])
    w_in = din("w_in", [DEPTH, D, PW])
    pcols_d = din("pcols", [DEPTH, 128, NPCOL])
    rw_w2 = din("rw_w2", [DEPTH, 128, 512])
    rw_a2 = din("rw_a2", [DEPTH, 128, 512])
    rw_g2 = din("rw_g2", [DEPTH, 128, 512])
    mlb = din("mlb", [DEPTH, 16, 2])
    p_abc = din("p_abc", [DEPTH, 3, 512, D])
    w_out = din("w_out", [DEPTH, D, D])
    w_mlp1 = din("w_mlp1", [DEPTH, D, DFF])
    w_mlp2 = din("w_mlp2", [DEPTH, DFF, D])
    g_final = din("g_final", [1, D])
    out = nc.dram_tensor("out", [NS, T, D], F32, kind="ExternalOutput").ap()

    xc = dscr("xc", [NS, CTX, D], F32)
    U = dscr("U", [NS, NCH, 128, TC], BF16, dbg=True)
    MOD = dscr("MOD", [DEPTH, 3, 6 * D], F32, dbg=True)
    WB_in = dscr("WB_in", [DEPTH, D, PW], BF16)
    WB_p = dscr("WB_p", [DEPTH, 3, 512, D], BF16)
    WB_out = dscr("WB_out", [DEPTH, D, D], BF16)
    WB_1 = dscr("WB_1", [DEPTH, D, DFF], BF16)
    WB_2 = dscr("WB_2", [DEPTH, DFF, D], BF16)
    YBR = dscr("YBR", [NS, 3, 4, 128, TC], BF16, dbg=True)

    L_U = [[LT("U%d_%d" % (s, i)) for i in range(len(cfg.tiles))] for s in range(NS)]
    L_X = [[LT("X%d_%d" % (s, i)) for i in range(len(cfg.tiles))] for s in range(NS)]
    L_WB = [LT("WB%d" % l) for l in range(DEPTH)]
    L_MOD = LT("MOD")
    L_Y = [[[LT() for i in range(len(cfg.tiles))] for b in range(3)] for s in range(NS)]

    psf = Rot([P.ps([128, 512], F32, "psf") for _ in range(6)])
    psb = Rot([P.ps([128, 1024], BF16, "psb") for _ in range(2)])

    ident_bf = P.sb([128, 128], BF16, "identb")
    ident_f = P.sb([128, 128], F32, "identf")
    ones_bf = P.sb([128, 128], BF16, "onesb")
    cst = {}

    def const_setup():
        pass

    ident_d = din("ident", [128, 128])
    blk_d = din("blkones", [128, 128])
    masks_d = din("masks", [4, 64, 64])
    lvmask_d = din("lvmask", [6, 64, 64])
    P.dma("sp", ident_f[:], ident_d, wr=[ident_f])
    P.op("dve", lambda e: e.tensor_copy(ident_bf[:], ident_f[:]), rd=[ident_f], wr=[ident_bf])
    P.op("dve", lambda e: e.memset(ones_bf[:], 1.0), wr=[ones_bf])
    blk_f = P.sb([128, 128], F32, "blkf")
    blk_bf = P.sb([128, 128], BF16, "blkb")
    P.dma("sp", blk_f[:], blk_d, wr=[blk_f])
    P.op("dve", lambda e: e.tensor_copy(blk_bf[:], blk_f[:]), rd=[blk_f], wr=[blk_bf])

    cast_i = [0]
    castbuf = {}

    def cast_dram(src, dst, R, C, lt_dst):
        for r0 in range(0, R, 128):
            for c0 in range(0, C, 2048):
                cw = min(2048, C - c0)
                a, b = castbuf["cin"].next(), castbuf["cout"].next()
                P.dma("sp", a[:, :cw], src[r0:r0 + 128, c0:c0 + cw], wr=[a])
                eng = ("dve", "act", "pool")[cast_i[0] % 3]
                cast_i[0] += 1
                if eng == "act":
                    P.op("act", lambda e: e.copy(b[:, :cw], a[:, :cw]), rd=[a], wr=[b])
                else:
                    P.op(eng, lambda e: e.tensor_copy(b[:, :cw], a[:, :cw]), rd=[a], wr=[b])
                P.dma("pool", dst[r0:r0 + 128, c0:c0 + cw], b[:, :cw], rd=[b], wr=[lt_dst])

    def cast_layer(l):
        mark = P.scope_begin()
        castbuf["cin"] = Rot([P.sb([128, 2048], F32, "cin") for _ in range(3)])
        castbuf["cout"] = Rot([P.sb([128, 2048], BF16, "cout") for _ in range(3)])
        cast_dram(w_in[l], WB_in[l], D, PW, L_WB[l])
        for br in range(3):
            cast_dram(p_abc[l, br], WB_p[l, br], 512, D, L_WB[l])
        cast_dram(w_out[l], WB_out[l], D, D, L_WB[l])
        cast_dram(w_mlp1[l], WB_1[l], D, DFF, L_WB[l])
        cast_dram(w_mlp2[l], WB_2[l], DFF, D, L_WB[l])
        P.scope_end(mark)

    def modulation():
        mark = P.scope_begin()
        cT = P.sb([128, KC, 3], F32, "cT")
        scT = P.sb([128, KC, 3], F32, "scT")
        for v in range(3):
            P.dma("sp", cT[:, :, v], cvec[v].rearrange("(k p) -> p k", p=128), wr=[cT], allow_slow_non_contiguous=True)
        P.op("act", lambda e: e.activation(out=scT[:], in_=cT[:], func=AF.Silu), rd=[cT], wr=[scT])
        wada_t = Rot([P.sb([128, KC, 512], F32, "wada") for _ in range(2)])
        bada_t = Rot([P.sb([3, 512], F32, "bada") for _ in range(2)])
        modrow = Rot([P.sb([3, 512], F32, "modrow") for _ in range(2)])
        for l in range(DEPTH):
            for g in range(12):
                wt, bt, mr = wada_t.next(), bada_t.next(), modrow.next()
                P.dma("sp", bt[:], b_ada[l:l + 1, g * 512:(g + 1) * 512].broadcast_to([3, 512]), wr=[bt])
                P.dma("sp", wt[:], w_ada[l, :, g * 512:(g + 1) * 512].rearrange("(k p) c -> p k c", p=128), wr=[wt])
                ps = psf.next()
                for k in range(KC):
                    P.op("pe", lambda e: e.matmul(ps[0:3, :], scT[:, k, :], wt[:, k, :], start=(k == 0), stop=(k == KC - 1)),
                         rd=[scT, wt], wr=[ps])
                P.op("dve", lambda e: e.tensor_tensor(mr[:], ps[0:3, :], bt[:], ALU.add), rd=[ps, bt], wr=[mr])
                P.dma("sp", MOD[l, :, g * 512:(g + 1) * 512], mr[:], rd=[mr], wr=[L_MOD])
        P.scope_end(mark)

    pc = P.sb([128, NPCOL], F32, "pcols")
    modc = P.sb([128, 48, 3], F32, "modc")
    g1c = P.sb([128, 3, KC], F32, "g1c")
    g2c = P.sb([128, 3, KC], F32, "g2c")
    mgrow = [[P.sb([128, D], F32, "mgrow") for w in range(2)] for v in range(3)]

    def pcol(nm, i=0, n=1):
        o, _ = PCOLS[nm]
        return pc[:, o + i:o + i + n]

    def layer_params(l):
        P.dma("sp", pc[:], pcols_d[l], wr=[pc])
        for v in range(3):
            P.dma("sp", modc[:, :, v], MOD[l, v].rearrange("(c p) -> p c", p=128), rd=[L_MOD], wr=[modc],
                  allow_slow_non_contiguous=True)
        for v in range(3):
            for w, mi in enumerate((2, 5)):
                P.dma("sp", mgrow[v][w][:], MOD[l, v:v + 1, mi * D:(mi + 1) * D].broadcast_to([128, D]),
                      rd=[L_MOD], wr=[mgrow[v][w]])
            for (gc, gname, mi) in ((g1c, "g1", 1), (g2c, "g2", 4)):
                P.op("dve", lambda e: e.scalar_tensor_tensor(gc[:, v, :], modc[:, mi * 8:(mi + 1) * 8, v], 1.0,
                                                              pcol(gname, 0, 8), ALU.add, ALU.mult),
                     rd=[modc, pc], wr=[gc])

    def x_src(l, s, e0, n):
        if e0 < CTX:
            return (ctx_in if l == 0 else xc)[s, e0:e0 + n, :]
        return (x_in if l == 0 else out)[s, e0 - CTX:e0 - CTX + n, :]

    def x_dst(s, e0, n):
        if e0 < CTX:
            return xc[s, e0:e0 + n, :]
        return out[s, e0 - CTX:e0 - CTX + n, :]

    ss_t = Rot([P.sb([128, 8], F32, "ss") for _ in range(2)])
    junk = P.sb([128, D], F32, "junk")

    epsc = P.sb([128, 4], F32, "epsc")
    P.op("dve", lambda e: e.memset(epsc[:, 0:1], NORM_EPS), wr=[epsc])
    P.op("dve", lambda e: e.memset(epsc[:, 1:2], LN_EPS), wr=[epsc])
    P.op("dve", lambda e: e.memset(epsc[:, 2:3], RW_GN_EPS), wr=[epsc])
    P.op("dve", lambda e: e.memset(epsc[:, 3:4], 1e-24), wr=[epsc])
    EPSI = {NORM_EPS: 0, LN_EPS: 1, RW_GN_EPS: 2, 1e-24: 3}

    def rsqrt(out_ap, in_ap, scale, eps, rd, wr):
        i = EPSI[eps]
        npart = out_ap.shape[0]
        P.op("act", lambda e: e.activation(out=out_ap, in_=in_ap, func=AF.Sqrt, scale=scale, bias=epsc[0:npart, i:i + 1]),
             rd=list(rd) + [epsc], wr=wr)
        P.op("dve", lambda e: e.reciprocal(out_ap, out_ap), rd=wr, wr=wr)

    def norm_T(xt, nsub, gcol, shcol, hT, xn):
        ss = ss_t.next()
        for sub in range(nsub):
            P.op("act", lambda e: e.activation(out=junk[:], in_=xt[:, sub, :], func=AF.Square,
                                               accum_out=ss[:, sub:sub + 1]), rd=[xt], wr=[junk, ss])
        rsqrt(ss[:, 4:4 + nsub], ss[:, 0:nsub], 1.0 / D, NORM_EPS, [ss], [ss])
        for sub in range(nsub):
            P.op("dve", lambda e: e.tensor_scalar(xn[:, sub, :], xt[:, sub, :], ss[:, 4 + sub:5 + sub], None, ALU.mult),
                 rd=[xt, ss], wr=[xn])
        for k in range(KC):
            pb = psb.next()
            for sub in range(nsub):
                P.op("pe", lambda e: e.transpose(pb[:, sub * 128:(sub + 1) * 128], xn[:, sub, k * 128:(k + 1) * 128],
                                                 ident_bf[:]), rd=[xn, ident_bf], wr=[pb])
            P.op("act", lambda e: e.activation(out=hT[:, k, 0:nsub * 128], in_=pb[:, 0:nsub * 128], func=AF.Identity,
                                               scale=gcol(k), bias=shcol(k)), rd=[pb, g1c, g2c, modc], wr=[hT])

    evac_i = [0]

    def evac(dst_ap, ps, src_ap, wr):
        evac_i[0] += 1
        if evac_i[0] % 2:
            P.op("act", lambda e: e.copy(dst_ap, src_ap), rd=[ps], wr=wr)
        else:
            P.op("dve", lambda e: e.tensor_copy(dst_ap, src_ap), rd=[ps], wr=wr)

    def phase_A(l):
        mark = P.scope_begin()
        wg_rot = Rot([P.sb([128, KC, 512], BF16, "wg") for _ in range(3)])
        ust_rot = Rot([P.sb([128, 4, 512], BF16, "ust") for _ in range(2)])
        xt_rot = Rot([P.sb([128, 4, D], F32, "xt") for _ in range(2)])
        xn = P.sb([128, 4, D], BF16, "xn")
        hT_rot = Rot([P.sb([128, KC, 512], BF16, "hT") for _ in range(2)])
        for s in range(NS):
            for ti, (e0, n) in enumerate(cfg.tiles):
                nsub = n // 128
                v = 2 if e0 < CTX else s
                xt = xt_rot.next()
                P.dma("sp", xt[:, 0:nsub, :], x_src(l, s, e0, n).rearrange("(a p) d -> p a d", p=128),
                      rd=[L_X[s][ti]], wr=[xt])
                hT = hT_rot.next()
                norm_T(xt, nsub, lambda k: g1c[:, v, k:k + 1], lambda k: modc[:, 0 * 8 + k, v:v + 1], hT, xn)
                for g in range(NCH // 4):
                    wg = wg_rot.next()
                    P.dma("sp" if g % 2 == 0 else "pool", wg[:],
                          WB_in[l, :, g * 512:(g + 1) * 512].rearrange("(k p) c -> p k c", p=128),
                          rd=[L_WB[l]], wr=[wg])
                    ust = ust_rot.next()
                    for c4 in range(4):
                        ps = psf.next()
                        for k in range(KC):
                            P.op("pe", lambda e: e.matmul(ps[:, 0:n], wg[:, k, c4 * 128:(c4 + 1) * 128], hT[:, k, 0:n],
                                                          start=(k == 0), stop=(k == KC - 1)), rd=[wg, hT], wr=[ps])
                        evac(ust[:, c4, 0:n], ps, ps[:, 0:n], [ust])
                    P.dma("sp", U[s, g * 4:(g + 1) * 4, :, e0:e0 + n].rearrange("c p t -> p c t"), ust[:, :, 0:n],
                          rd=[ust], wr=[L_U[s][ti]])
        P.scope_end(mark)

    def ln_stats_bcast(xb, sqb, lhsT_ones, nj, n, eps, tag):
        raise NotImplementedError

    def phase_C(l, last):
        mark = P.scope_begin()
        pw = P.sb([128, 3, 4, D], BF16, "pw")
        wo = P.sb([128, KC, D], BF16, "wo")
        for br in range(3):
            P.dma("sp", pw[:, br, :, :], WB_p[l, br].rearrange("(j p) d -> p j d", p=128), rd=[L_WB[l]], wr=[pw])
        P.dma("sp", wo[:], WB_out[l].rearrange("(k p) d -> p k d", p=128), rd=[L_WB[l]], wr=[wo])
        yt_rot = Rot([P.sb([128, 12, 256], BF16, "yt") for _ in range(1)])
        gt_rot = Rot([P.sb([128, 24, 256], BF16, "gt") for _ in range(1)])
        mT = P.sb([128, KC, 256], F32, "mT")
        mTb = P.sb([128, KC, 256], BF16, "mTb")
        tmpc = Rot([P.sb([128, 512], F32, "tmpc") for _ in range(2)])
        xt2 = Rot([P.sb([128, 2, D], F32, "xt2") for _ in range(1)])
        hid = P.sb([128, 32, 256], BF16, "hid")
        w1_rot = Rot([P.sb([128, KC, 512], BF16, "w1g") for _ in range(2)])
        w2_rot = Rot([P.sb([128, 8, D], BF16, "w2g") for _ in range(2)])
        xt_rot = Rot([P.sb([128, 2, D], F32, "xt") for _ in range(2)])
        xn = P.sb([128, 2, D], BF16, "xn")
        hT_rot = Rot([P.sb([128, KC, 256], BF16, "hT") for _ in range(1)])
        gfin = P.sb([128, D], F32, "gfin")
        if last:
            P.dma("sp", gfin[:], g_final.broadcast_to([128, D]), wr=[gfin])
        for s in range(NS):
            for ti, e0, n in [(ti, e0 + o, min(256, n - o)) for ti, (e0, n) in enumerate(cfg.tiles) for o in range(0, n, 256)]:
                if last and e0 < CTX:
                    continue
                nsub = n // 128
                v = 2 if e0 < CTX else s
                xt = xt_rot.next()
                P.dma("sp", xt[:, 0:nsub, :], x_src(l, s, e0, n).rearrange("(a p) d -> p a d", p=128),
                      rd=[L_X[s][ti]], wr=[xt])
                yt, gt = yt_rot.next(), gt_rot.next()
                P.dma("pool", yt[:, :, 0:n], YBR[s, :, :, :, e0:e0 + n].rearrange("b j p t -> p (b j) t"),
                      rd=L_Y[s][0][ti:ti + 1] + L_Y[s][1][ti:ti + 1] + L_Y[s][2][ti:ti + 1], wr=[yt])
                P.dma("pool", gt[:, :, 0:n], U[s, 40:64, :, e0:e0 + n].rearrange("c p t -> p c t"), rd=[L_U[s][ti]], wr=[gt])
                P.op("act", lambda e: e.activation(out=gt[:, :, 0:n], in_=gt[:, :, 0:n], func=AF.Sigmoid), rd=[gt], wr=[gt])
                for oc in range(KC):
                    for br in range(3):
                        ps = psf.next()
                        for j in range(4):
                            P.op("pe", lambda e: e.matmul(ps[:, 0:n], pw[:, br, j, oc * 128:(oc + 1) * 128], yt[:, br * 4 + j, 0:n],
                                                          start=(j == 0), stop=(j == 3)), rd=[pw, yt], wr=[ps])
                        if br == 0:
                            P.op("dve", lambda e: e.tensor_tensor(mT[:, oc, 0:n], ps[:, 0:n], gt[:, br * 8 + oc, 0:n], ALU.mult),
                                 rd=[ps, gt], wr=[mT])
                        else:
                            tc_ = tmpc.next()
                            P.op("dve", lambda e: e.tensor_tensor(tc_[:, 0:n], ps[:, 0:n], gt[:, br * 8 + oc, 0:n], ALU.mult),
                                 rd=[ps, gt], wr=[tc_])
                            P.op("pool", lambda e: e.tensor_tensor(mT[:, oc, 0:n], mT[:, oc, 0:n], tc_[:, 0:n], ALU.add),
                                 rd=[tc_, mT], wr=[mT])
                    P.op("act", lambda e: e.copy(mTb[:, oc, 0:n], mT[:, oc, 0:n]), rd=[mT], wr=[mTb])
                if cfg.c_stop <= 2:
                    continue
                x2 = xt2.next()
                for sub in range(nsub):
                    for half in range(2):
                        ps = psf.next()
                        for k in range(KC):
                            P.op("pe", lambda e: e.matmul(ps[:, :], mTb[:, k, sub * 128:(sub + 1) * 128],
                                                          wo[:, k, half * 512:(half + 1) * 512], start=(k == 0), stop=(k == KC - 1)),
                                 rd=[mTb, wo], wr=[ps])
                        tc_ = tmpc.next()
                        P.op("dve", lambda e: e.tensor_tensor(tc_[:], ps[:], mgrow[v][0][:, half * 512:(half + 1) * 512], ALU.mult),
                             rd=[ps, mgrow[v][0]], wr=[tc_])
                        P.op("pool", lambda e: e.tensor_tensor(x2[:, sub, half * 512:(half + 1) * 512], tc_[:],
                                                               xt[:, sub, half * 512:(half + 1) * 512], ALU.add),
                             rd=[tc_, xt], wr=[x2])
                if cfg.c_stop <= 3:
                    continue
                hT = hT_rot.next()
                norm_T(x2, nsub, lambda k: g2c[:, v, k:k + 1], lambda k: modc[:, 3 * 8 + k, v:v + 1], hT, xn)
                for g in range(8):
                    w1 = w1_rot.next()
                    P.dma("sp", w1[:], WB_1[l, :, g * 512:(g + 1) * 512].rearrange("(k p) c -> p k c", p=128),
                          rd=[L_WB[l]], wr=[w1])
                    for c4 in range(4):
                        hc = g * 4 + c4
                        ps = psf.next()
                        for k in range(KC):
                            P.op("pe", lambda e: e.matmul(ps[:, 0:n], w1[:, k, c4 * 128:(c4 + 1) * 128], hT[:, k, 0:n],
                                                          start=(k == 0), stop=(k == KC - 1)), rd=[w1, hT], wr=[ps])
                        tc_ = tmpc.next()
                        P.op("act", lambda e: e.activation(out=tc_[:, 0:n], in_=ps[:, 0:n], func=AF.Relu), rd=[ps], wr=[tc_])
                        P.op("dve" if hc % 2 else "pool", lambda e: e.tensor_tensor(hid[:, hc, 0:n], tc_[:, 0:n], tc_[:, 0:n], ALU.mult),
                             rd=[tc_], wr=[hid])
                if cfg.c_stop <= 4:
                    continue
                xo = xt_rot.next()
                pss = [[psf.next() for half in range(2)] for sub in range(2)]
                for sp in range(0, nsub, 2):
                    subs = list(range(sp, min(sp + 2, nsub)))
                    for g in range(4):
                        w2 = w2_rot.next()
                        P.dma("sp", w2[:], WB_2[l, g * 1024:(g + 1) * 1024, :].rearrange("(k p) d -> p k d", p=128),
                              rd=[L_WB[l]], wr=[w2])
                        for kk_ in range(8):
                            hc = g * 8 + kk_
                            for sub in subs:
                                for half in range(2):
                                    ps = pss[sub - sp][half]
                                    P.op("pe", lambda e: e.matmul(ps[:, :], hid[:, hc, sub * 128:(sub + 1) * 128],
                                                                  w2[:, kk_, half * 512:(half + 1) * 512],
                                                                  start=(hc == 0), stop=(hc == 31)), rd=[hid, w2], wr=[ps])
                    for sub in subs:
                        for half in range(2):
                            ps = pss[sub - sp][half]
                            tc_ = tmpc.next()
                            P.op("dve", lambda e: e.tensor_tensor(tc_[:], ps[:], mgrow[v][1][:, half * 512:(half + 1) * 512], ALU.mult),
                                 rd=[ps, mgrow[v][1]], wr=[tc_])
                            P.op("pool", lambda e: e.tensor_tensor(xo[:, sub, half * 512:(half + 1) * 512], tc_[:],
                                                                   x2[:, sub, half * 512:(half + 1) * 512], ALU.add),
                                 rd=[tc_, x2], wr=[xo])
                if cfg.c_stop <= 5:
                    continue
                if last:
                    ss = ss_t.next()
                    for sub in range(nsub):
                        P.op("act", lambda e: e.activation(out=junk[:], in_=xo[:, sub, :], func=AF.Square,
                                                           accum_out=ss[:, sub:sub + 1]), rd=[xo], wr=[junk, ss])
                    rsqrt(ss[:, 4:4 + nsub], ss[:, 0:nsub], 1.0 / D, NORM_EPS, [ss], [ss])
                    for sub in range(nsub):
                        P.op("dve", lambda e: e.scalar_tensor_tensor(xo[:, sub, :], xo[:, sub, :], ss[:, 4 + sub:5 + sub], gfin[:],
                                                                      ALU.mult, ALU.mult), rd=[xo, ss, gfin], wr=[xo])
                if cfg.c_stop <= 6:
                    continue
                for sub in range(nsub):
                    P.dma("sp", x_dst(s, e0 + sub * 128, 128), xo[:, sub, :], rd=[xo], wr=[L_X[s][ti]])
        P.scope_end(mark)

    def stats_rstd(x_f32, nj, n, ones_l, eps, inv_n, scr):
        lhs, full = ones_l
        xb, sq = scr["xb"], scr["sq"]
        for j in range(nj):
            P.op("act", lambda e: e.copy(xb[:, j, 0:n], x_f32(j)), rd=scr["rd"], wr=[xb])
            P.op("act", lambda e: e.activation(out=sq[:, j, 0:n], in_=x_f32(j), func=AF.Square), rd=scr["rd"], wr=[sq])
        groups = [list(range(nj))] if full else [[j] for j in range(nj)]
        for gi, g in enumerate(groups):
            ps1, ps2 = psf.next(), psf.next()
            for a, j in enumerate(g):
                P.op("pe", lambda e: e.matmul(ps1[:, 0:n], lhs[:], xb[:, j, 0:n], start=(a == 0), stop=(a == len(g) - 1)),
                     rd=[lhs, xb], wr=[ps1])
            for a, j in enumerate(g):
                P.op("pe", lambda e: e.matmul(ps2[:, 0:n], lhs[:], sq[:, j, 0:n], start=(a == 0), stop=(a == len(g) - 1)),
                     rd=[lhs, sq], wr=[ps2])
            mean, rstd = scr["mean"], scr["rstd"]
            P.op("act", lambda e: e.mul(mean[:, gi, 0:n], ps1[:, 0:n], inv_n), rd=[ps1], wr=[mean])
            P.op("dve", lambda e: e.tensor_tensor(rstd[:, gi, 0:n], mean[:, gi, 0:n], mean[:, gi, 0:n], ALU.mult),
                 rd=[mean], wr=[rstd])
            P.op("dve", lambda e: e.scalar_tensor_tensor(rstd[:, gi, 0:n], ps2[:, 0:n], inv_n, rstd[:, gi, 0:n],
                                                          ALU.mult, ALU.subtract), rd=[ps2, rstd], wr=[rstd])
            rsqrt(rstd[:, gi, 0:n], rstd[:, gi, 0:n], 1.0, eps, [rstd], [rstd])

    def conv_branch(l, last):
        mark = P.scope_begin()
        ug = Rot([P.sb([128, 8, 512], BF16, "ug") for _ in range(2)])
        zp = P.sb([128, 4, 8 * 94], F32, "zp")
        zc = P.sb([128, 4, 542], F32, "zc")
        sg = P.sb([128, 4, 512], F32, "sg")
        acc = P.sb([128, 4, 512], F32, "acc")
        scr = {"xb": P.sb([128, 4, 512], BF16, "xb"), "sq": P.sb([128, 4, 512], BF16, "sq"),
               "mean": P.sb([128, 1, 512], F32, "mean"), "rstd": P.sb([128, 1, 512], F32, "rstd"), "rd": [acc]}
        yo = Rot([P.sb([128, 4, 512], BF16, "yo") for _ in range(2)])
        P.op("dve", lambda e: e.memset(zp[:], 0.0), wr=[zp])
        P.op("dve", lambda e: e.memset(zc[:], 0.0), wr=[zc])
        o_dw = PCOLS["cdw"][0]
        for s in range(NS):
            for ti, (e0, n) in enumerate(cfg.tiles):
                isctx = e0 < CTX
                if last and isctx:
                    continue
                u = ug.next()
                P.dma("sp", u[:, :, 0:n], U[s, 15:23, :, e0:e0 + n].rearrange("c p t -> p c t"), rd=[L_U[s][ti]], wr=[u])
                P.op("act", lambda e: e.activation(out=sg[:, :, 0:n], in_=u[:, 4:8, 0:n], func=AF.Sigmoid), rd=[u], wr=[sg])
                if isctx:
                    assert CTX <= 512 and n == CTX
                    zbuf = zc
                    zin = zc[:, :, 15:15 + n]
                    P.op("dve", lambda e: e.tensor_tensor(zin, u[:, 0:4, 0:n], sg[:, :, 0:n], ALU.mult), rd=[u, sg], wr=[zc])
                    win = lambda j, tau: zc[:, j, tau:tau + n]
                    av = lambda j: acc[:, j, 0:n]
                else:
                    nr = n // 64
                    zbuf = zp
                    zin = zp[:, :, 0:nr * 94].rearrange("p j (r w) -> p j r w", w=94)[:, :, :, 15:79]
                    P.op("dve", lambda e: e.tensor_tensor(zin, u[:, 0:4, 0:n].rearrange("p j (r w) -> p j r w", w=64),
                                                           sg[:, :, 0:n].rearrange("p j (r w) -> p j r w", w=64), ALU.mult),
                         rd=[u, sg], wr=[zp])
                    win = lambda j, tau: zp[:, j, 0:nr * 94].rearrange("p (r w) -> p r w", w=94)[:, :, tau:tau + 64]
                    av = lambda j: acc[:, j, 0:n].rearrange("p (r w) -> p r w", w=64)
                for j in range(4):
                    eng = "dve"
                    P.op(eng, lambda e: e.tensor_scalar(av(j), win(j, 0), pc[:, o_dw + j * 31:o_dw + j * 31 + 1],
                                                        pcol("cdb", j), ALU.mult, ALU.add), rd=[zbuf, pc], wr=[acc])
                    for tau in range(1, 31):
                        P.op(eng, lambda e: e.scalar_tensor_tensor(av(j), win(j, tau),
                                                                   pc[:, o_dw + j * 31 + tau:o_dw + j * 31 + tau + 1],
                                                                   av(j), ALU.mult, ALU.add), rd=[zbuf, pc, acc], wr=[acc])
                stats_rstd(lambda j: acc[:, j, 0:n], 4, n, (ones_bf, True), LN_EPS, 1.0 / 512, scr)
                y = yo.next()
                for j in range(4):
                    P.op("dve", lambda e: e.tensor_tensor(acc[:, j, 0:n], acc[:, j, 0:n], scr["mean"][:, 0, 0:n], ALU.subtract),
                         rd=[acc, scr["mean"]], wr=[acc])
                    P.op("dve", lambda e: e.tensor_tensor(acc[:, j, 0:n], acc[:, j, 0:n], scr["rstd"][:, 0, 0:n], ALU.mult),
                         rd=[acc, scr["rstd"]], wr=[acc])
                    P.op("act", lambda e: e.activation(out=y[:, j, 0:n], in_=acc[:, j, 0:n], func=AF.Silu,
                                                       scale=pcol("clnw", j), bias=pcol("clnb", j)), rd=[acc, pc], wr=[y])
                P.dma("sp", YBR[s, 1, :, :, e0:e0 + n].rearrange("j p t -> p j t"), y[:, :, 0:n], rd=[y], wr=[L_Y[s][1][ti]])
        P.scope_end(mark)

    HM = dscr("HM", [NS, 2, TC, 512], F32)
    L_HM = [[LT() for d in range(2)] for s in range(NS)]
    selh_d = din("selh", [4, 4 * 128])
    NCK = TC // 64
    DH5 = 128 ** -0.5

    def mlstm_branch(l, last):
        mark = P.scope_begin()
        selh = P.sb([4, 4 * 128], F32, "selh")
        P.dma("sp", selh[:], selh_d, wr=[selh])
        mb = P.sb([4, 4], F32, "mb")
        for d in range(2):
            P.dma("sp", mb[:, 2 * d:2 * d + 2], mlb[l, d * 4:(d + 1) * 4, :], wr=[mb])
        mk = P.sb([64, 2, 64], F32, "mk")
        P.dma("sp", mk[:, 0, :], masks_d[1], wr=[mk])
        P.dma("sp", mk[:, 1, :], masks_d[3], wr=[mk])
        P.op("dve", lambda e: e.tensor_scalar(mk[:], mk[:], DH5, None, ALU.mult), rd=[mk], wr=[mk])
        PAD = 64
        gb = P.sb([4, 2, TC], BF16, "gb")
        IG = P.sb([4, TC], F32, "IG")
        Bc = P.sb([4, TC], F32, "Bc")
        G = P.sb([4, TC], F32, "G")
        MU = P.sb([4, TC + 2 * PAD], F32, "MU")
        Q2_ = P.sb([4, TC], F32, "Q2")
        Q = [G, IG, Q2_, Bc]
        CAR = P.sb([4, NCK], F32, "CAR")
        carb = P.sb([128, 4, NCK], F32, "carb")
        qk_rot = Rot([P.sb([128, 8, 512], BF16, "qk") for _ in range(1)])
        vk_rot = Rot([P.sb([128, 4, 512], BF16, "vk") for _ in range(2)])
        vtm = P.sb([64, 8, 4, 130], BF16, "vtm")
        ktm = P.sb([64, 8, 4, 128], BF16, "ktm")
        cols = Rot([P.sb([64, 16], F32, "cols") for _ in range(2)])
        Sb = Rot([P.sb([64, 64], BF16, "Sb") for _ in range(2)])
        Vp = Rot([P.sb([64, 130], BF16, "Vp") for _ in range(2)])
        tmpn = Rot([P.sb([64, 130], F32, "tmpn") for _ in range(2)])
        ne = Rot([P.sb([64, 130], F32, "ne") for _ in range(2)])
        dn = Rot([P.sb([64, 2], F32, "dn") for _ in range(2)])
        hst = Rot([P.sb([64, 8, 512], F32, "hst") for _ in range(1)])
        Cf = P.sb([128, 4, 130], F32, "Cf")
        Cb = P.sb([128, 4, 130], BF16, "Cb")
        P.op("dve", lambda e: e.memset(vtm[:], 1.0), wr=[vtm])
        for s in range(NS):
            for d in range(2):
                def seg(e_lo, e_hi):
                    if d == 0:
                        return slice(e_lo, e_hi)
                    return slice(e_lo - CTX, e_hi - CTX) if e_lo >= CTX else slice(T + e_lo, T + e_hi)
                for (lo, hi) in ((0, CTX), (CTX, TC)):
                    for gi in range(2):
                        P.dma("sp", gb[:, gi, seg(lo, hi)], U[s, 39, d * 8 + gi * 4:d * 8 + gi * 4 + 4, lo:hi],
                              rd=[t_ for t_ in L_U[s]], wr=[gb])
                P.op("act", lambda e: e.activation(out=IG[:], in_=gb[:, 0, :], func=AF.Identity, bias=mb[:, 2 * d:2 * d + 1]),
                     rd=[gb, mb], wr=[IG])
                P.op("act", lambda e: e.activation(out=Bc[:], in_=gb[:, 1, :], func=AF.Sigmoid, bias=mb[:, 2 * d + 1:2 * d + 2]),
                     rd=[gb, mb], wr=[Bc])
                P.op("act", lambda e: e.activation(out=Bc[:], in_=Bc[:], func=AF.Ln), rd=[Bc], wr=[Bc])
                rv = (lambda ap: ap) if d == 0 else (lambda ap: ap[:, ::-1])
                P.op("dve", lambda e: e.memset(MU[:], 1.0), wr=[MU])
                P.op("dve", lambda e: e.tensor_tensor_scan(rv(G[:]), rv(MU[:, 0:TC]), rv(Bc[:]), 0.0, ALU.mult, ALU.add),
                     rd=[MU, Bc], wr=[G])
                P.op("dve", lambda e: e.memset(MU[:], 0.0), rd=[G], wr=[MU])
                P.op("dve", lambda e: e.tensor_copy(Bc[:], G[:]), rd=[G], wr=[Bc])
                P.op("dve", lambda e: e.tensor_tensor(G[:], IG[:], Bc[:], ALU.subtract), rd=[IG, Bc], wr=[G])
                mu = MU[:, PAD:PAD + TC]
                P.op("dve", lambda e: e.tensor_tensor_scan(rv(mu), rv(G[:]), rv(G[:]), 0.0, ALU.max, ALU.max),
                     rd=[G], wr=[MU])
                mu3 = mu.rearrange("p (c t) -> p c t", t=64)
                if d == 0:
                    mu_last = mu3[:, :, 63:64]
                    mu_ent = MU[:, PAD - 1:PAD - 1 + TC].rearrange("p (c t) -> p c t", t=64)[:, :, 0:1]
                else:
                    mu_last = mu3[:, :, 0:1]
                    mu_ent = MU[:, PAD + 64:PAD + 64 + TC].rearrange("p (c t) -> p c t", t=64)[:, :, 0:1]
                v3 = lambda b: b[:].rearrange("p (c t) -> p c t", t=64)
                bc3 = lambda ap: ap.broadcast_to([4, NCK, 64])
                P.op("dve", lambda e: e.tensor_tensor(v3(Q[0]), v3(G), bc3(mu_last), ALU.subtract), rd=[G, MU], wr=[Q[0]])
                P.op("act", lambda e: e.activation(out=Q[0][:], in_=Q[0][:], func=AF.Exp), rd=[Q[0]], wr=[Q[0]])
                P.op("dve", lambda e: e.tensor_tensor(v3(Q[1]), bc3(mu_last), mu3, ALU.subtract), rd=[MU], wr=[Q[1]])
                P.op("act", lambda e: e.activation(out=Q[1][:], in_=Q[1][:], func=AF.Exp), rd=[Q[1]], wr=[Q[1]])
                P.op("dve", lambda e: e.tensor_tensor(v3(Q[2]), bc3(mu_ent), mu3, ALU.subtract), rd=[MU], wr=[Q[2]])
                P.op("act", lambda e: e.activation(out=Q[2][:], in_=Q[2][:], func=AF.Exp), rd=[Q[2]], wr=[Q[2]])
                P.op("dve", lambda e: e.tensor_tensor(Q[3][:], Bc[:], mu, ALU.add), rd=[Bc, MU], wr=[Q[3]])
                P.op("act", lambda e: e.activation(out=Q[3][:], in_=Q[3][:], func=AF.Exp, scale=-1.0), rd=[Q[3]], wr=[Q[3]])
                P.op("dve", lambda e: e.tensor_tensor(CAR[:].unsqueeze(2), mu_ent, mu_last, ALU.subtract), rd=[MU], wr=[CAR])
                P.op("act", lambda e: e.activation(out=CAR[:], in_=CAR[:], func=AF.Exp), rd=[CAR], wr=[CAR])
                for h in range(4):
                    ps = psf.next()
                    P.op("pe", lambda e: e.matmul(ps[:, 0:NCK], selh[:, h * 128:(h + 1) * 128], CAR[:], start=True, stop=True),
                         rd=[selh, CAR], wr=[ps])
                    P.op("dve", lambda e: e.tensor_copy(carb[:, h, :], ps[:, 0:NCK]), rd=[ps], wr=[carb])
                P.op("dve", lambda e: e.memset(Cf[:], 0.0), wr=[Cf])
                P.op("dve", lambda e: e.memset(Cb[:], 0.0), wr=[Cb])
                order = list(range(len(cfg.tiles)))
                nctx_t = sum(1 for (e0, n) in cfg.tiles if e0 < CTX)
                if d == 1:
                    order = list(range(nctx_t - 1, -1, -1)) + list(range(len(cfg.tiles) - 1, nctx_t - 1, -1))
                for ti in order:
                    e0, n = cfg.tiles[ti]
                    nck = n // 64
                    qk, vk = qk_rot.next(), vk_rot.next()
                    P.dma("sp", qk[:, :, 0:n], U[s, 23:31, :, e0:e0 + n].rearrange("c p t -> p c t"), rd=[L_U[s][ti]], wr=[qk])
                    P.dma("pool", vk[:, 0:4, 0:n], U[s, 31:35, :, e0:e0 + n].rearrange("c p t -> p c t"), rd=[L_U[s][ti]], wr=[vk])
                    for ck in range(nck):
                        for (src, srcbuf, dst, w) in ((lambda h: vk[:, h, ck * 64:(ck + 1) * 64], vk, vtm, 130),
                                                      (lambda h: qk[:, 4 + h, ck * 64:(ck + 1) * 64], qk, ktm, 128)):
                            pb = psb.next()
                            for h in range(4):
                                P.op("pe", lambda e: e.transpose(pb[0:64, h * 128:(h + 1) * 128], src(h), ident_bf[:]),
                                     rd=[srcbuf, ident_bf], wr=[pb])
                            P.op("act", lambda e: e.copy(dst[:, ck, :, 0:128], pb[0:64, 0:512].rearrange("p (h c) -> p h c", c=128)),
                                 rd=[pb], wr=[dst])
                    hs = hst.next()
                    ckorder = range(nck) if d == 0 else range(nck - 1, -1, -1)
                    for ck in ckorder:
                        ec = (e0 + ck * 64)
                        mc = seg(ec, ec + 64)
                        cidx = mc.start // 64
                        cl = cols.next()
                        psq = psf.next()
                        for q in range(4):
                            P.op("pe", lambda e: e.matmul(psq[0:64, q * 4:(q + 1) * 4], Q[q][:, mc], ident_f[0:4, 0:4],
                                                          start=True, stop=True), rd=[Q[q], ident_f], wr=[psq])
                        P.op("dve", lambda e: e.tensor_copy(cl[:], psq[0:64, 0:16]), rd=[psq], wr=[cl])
                        for h in range(4):
                            om, rho, win, em = (cl[:, q * 4 + h:q * 4 + h + 1] for q in range(4))
                            psS = psf.next()
                            P.op("pe", lambda e: e.matmul(psS[0:64, 0:64], qk[:, 4 + h, ck * 64:(ck + 1) * 64],
                                                          qk[:, h, ck * 64:(ck + 1) * 64], start=True, stop=True), rd=[qk], wr=[psS])
                            sb_ = Sb.next()
                            P.op("dve", lambda e: e.scalar_tensor_tensor(sb_[:], psS[0:64, 0:64], om, mk[:, d, :], ALU.mult, ALU.mult),
                                 rd=[psS, cl, mk], wr=[sb_])
                            psI = psf.next()
                            P.op("pe", lambda e: e.matmul(psI[0:64, 0:129], sb_[:], vtm[:, ck, h, 0:129], start=True, stop=True),
                                 rd=[sb_, vtm], wr=[psI])
                            psC = psf.next()
                            P.op("pe", lambda e: e.matmul(psC[0:64, 0:129], qk[:, h, ck * 64:(ck + 1) * 64], Cb[:, h, 0:129],
                                                          start=True, stop=True), rd=[qk, Cb], wr=[psC])
                            tn, nn, dd = tmpn.next(), ne.next(), dn.next()
                            P.op("act", lambda e: e.activation(out=tn[:, 0:129], in_=psC[0:64, 0:129], func=AF.Copy, scale=win),
                                 rd=[psC, cl], wr=[tn])
                            P.op("dve", lambda e: e.scalar_tensor_tensor(nn[:, 0:129], psI[0:64, 0:129], rho, tn[:, 0:129],
                                                                          ALU.mult, ALU.add), rd=[psI, cl, tn], wr=[nn])
                            P.op("dve", lambda e: e.scalar_tensor_tensor(dd[:, 0:1], nn[:, 128:129], -1.0, nn[:, 128:129],
                                                                          ALU.mult, ALU.max), rd=[nn], wr=[dd])
                            P.op("dve", lambda e: e.tensor_scalar(dd[:, 0:1], dd[:, 0:1], em, None, ALU.max),
                                 rd=[dd, cl], wr=[dd])
                            P.op("dve", lambda e: e.reciprocal(dd[:, 1:2], dd[:, 0:1]), rd=[dd], wr=[dd])
                            P.op("act", lambda e: e.activation(out=hs[:, ck, h * 128:(h + 1) * 128], in_=nn[:, 0:128],
                                                               func=AF.Copy, scale=dd[:, 1:2]), rd=[nn, dd], wr=[hs])
                            vp = Vp.next()
                            P.op("dve", lambda e: e.tensor_scalar(vp[:, 0:129], vtm[:, ck, h, 0:129], om, DH5, ALU.mult, ALU.mult),
                                 rd=[vtm, cl], wr=[vp])
                            psU = psf.next()
                            P.op("pe", lambda e: e.matmul(psU[:, 0:129], ktm[:, ck, h, :], vp[:, 0:129], start=True, stop=True),
                                 rd=[ktm, vp], wr=[psU])
                            P.op("dve", lambda e: e.scalar_tensor_tensor(Cf[:, h, 0:129], Cf[:, h, 0:129],
                                                                          carb[:, h, cidx:cidx + 1], psU[:, 0:129],
                                                                          ALU.mult, ALU.add), rd=[Cf, carb, psU], wr=[Cf])
                            P.op("act", lambda e: e.copy(Cb[:, h, 0:129], Cf[:, h, 0:129]), rd=[Cf], wr=[Cb])
                    P.dma("sp", HM[s, d, e0:e0 + n, :].rearrange("(c t) f -> t c f", t=64), hs[:, 0:nck, :], rd=[hs], wr=[L_HM[s][d]])
        P.scope_end(mark)
        mark = P.scope_begin()
        hf = Rot([P.sb([128, 512], F32, "hf") for _ in range(2)])
        hb = Rot([P.sb([128, 512], F32, "hb") for _ in range(2)])
        hnb = Rot([P.sb([128, 512], BF16, "hnb") for _ in range(2)])
        st = Rot([P.sb([128, 16], F32, "st") for _ in range(2)])
        og = Rot([P.sb([128, 4, 512], BF16, "og") for _ in range(2)])
        yc = Rot([P.sb([128, 4, 512], BF16, "yc") for _ in range(2)])
        for s in range(NS):
            for ti, (e0, n) in enumerate(cfg.tiles):
                if last and e0 < CTX:
                    continue
                o_ = og.next()
                P.dma("sp", o_[:, :, 0:n], U[s, 35:39, :, e0:e0 + n].rearrange("c p t -> p c t"), rd=[L_U[s][ti]], wr=[o_])
                P.op("act", lambda e: e.activation(out=o_[:, :, 0:n], in_=o_[:, :, 0:n], func=AF.Sigmoid), rd=[o_], wr=[o_])
                y = yc.next()
                for sub in range(n // 128):
                    a, b, hn, t_ = hf.next(), hb.next(), hnb.next(), st.next()
                    r0 = e0 + sub * 128
                    P.dma("sp", a[:], HM[s, 0, r0:r0 + 128, :], rd=[L_HM[s][0]], wr=[a])
                    P.dma("pool", b[:], HM[s, 1, r0:r0 + 128, :], rd=[L_HM[s][1]], wr=[b])
                    P.op("dve", lambda e: e.tensor_tensor(a[:], a[:], b[:], ALU.add), rd=[a, b], wr=[a])
                    P.op("dve", lambda e: e.reduce_sum(t_[:, 0:4], a[:].rearrange("p (h c) -> p h c", c=128), AX.X), rd=[a], wr=[t_])
                    for h in range(4):
                        P.op("act", lambda e: e.activation(out=b[:, h * 128:(h + 1) * 128], in_=a[:, h * 128:(h + 1) * 128],
                                                           func=AF.Square, accum_out=t_[:, 4 + h:5 + h]), rd=[a], wr=[b, t_])
                    P.op("dve", lambda e: e.tensor_scalar(t_[:, 0:4], t_[:, 0:4], 1.0 / 128, None, ALU.mult), rd=[t_], wr=[t_])
                    P.op("dve", lambda e: e.tensor_tensor(t_[:, 8:12], t_[:, 0:4], t_[:, 0:4], ALU.mult), rd=[t_], wr=[t_])
                    P.op("dve", lambda e: e.scalar_tensor_tensor(t_[:, 8:12], t_[:, 4:8], 1.0 / 128, t_[:, 8:12], ALU.mult, ALU.subtract),
                         rd=[t_], wr=[t_])
                    rsqrt(t_[:, 12:16], t_[:, 8:12], 1.0, LN_EPS, [t_], [t_])
                    for h in range(4):
                        P.op("dve", lambda e: e.tensor_scalar(hn[:, h * 128:(h + 1) * 128], a[:, h * 128:(h + 1) * 128],
                                                              t_[:, h:h + 1], t_[:, 12 + h:13 + h], ALU.subtract, ALU.mult),
                             rd=[a, t_], wr=[hn])
                    pb = psb.next()
                    for h in range(4):
                        P.op("pe", lambda e: e.transpose(pb[:, h * 128:(h + 1) * 128], hn[:, h * 128:(h + 1) * 128], ident_bf[:]),
                             rd=[hn, ident_bf], wr=[pb])
                    for h in range(4):
                        P.op("dve", lambda e: e.scalar_tensor_tensor(y[:, h, sub * 128:(sub + 1) * 128], pb[:, h * 128:(h + 1) * 128],
                                                                      pcol("mnw", h), o_[:, h, sub * 128:(sub + 1) * 128],
                                                                      ALU.mult, ALU.mult), rd=[pb, pc, o_], wr=[y])
                P.dma("sp", YBR[s, 2, :, :, e0:e0 + n].rearrange("j p t -> p j t"), y[:, :, 0:n], rd=[y], wr=[L_Y[s][2][ti]])
        P.scope_end(mark)

    RF = dscr("RF", [NS, 2, 4, 128, NCK, 4, 64], BF16)
    PLs = dscr("PLs", [NS, 2, 128, 4, NCK], F32)
    VT = dscr("VT", [NS, TC, 512], BF16)
    GB = dscr("GB", [NS, 2, 4, 128, TC], BF16)
    YR = dscr("YR", [NS, 2, 4, 128, TC], F32)
    L_RF = [[[LT() for _ in cfg.tiles] for d in range(2)] for s in range(NS)]
    L_VT = [[LT() for _ in cfg.tiles] for s in range(NS)]
    L_GB = [[LT() for _ in cfg.tiles] for s in range(NS)]
    L_YR = [[[LT() for _ in cfg.tiles] for d in range(2)] for s in range(NS)]
    C0 = float(np.exp(-0.5))

    def rwkv_branch(l, last):
        mark = P.scope_begin()
        wtmp = P.sb([128, 512], F32, "wtmp")
        w2p = P.sb([128, 2, 512], BF16, "w2p")
        a2p = P.sb([128, 2, 512], BF16, "a2p")
        g2b = P.sb([128, 512], BF16, "g2b")
        P.op("dve", lambda e: e.memset(w2p[:], 0.0), wr=[w2p])
        P.op("dve", lambda e: e.memset(a2p[:], 0.0), wr=[a2p])
        for (src, dst) in ((rw_w2, w2p), (rw_a2, a2p)):
            P.dma("sp", wtmp[:], src[l], wr=[wtmp])
            for d in range(2):
                P.op("dve", lambda e: e.tensor_copy(dst[d * 64:(d + 1) * 64, d, :], wtmp[d * 64:(d + 1) * 64, :]), rd=[wtmp], wr=[dst])
        P.dma("sp", wtmp[:], rw_g2[l], wr=[wtmp])
        P.op("dve", lambda e: e.tensor_copy(g2b[:], wtmp[:]), rd=[wtmp], wr=[g2b])
        coef0 = P.sb([128, 15], F32, "coef0")
        omka = P.sb([128, 4], F32, "omka")
        P.op("dve", lambda e: e.tensor_tensor(coef0[:], pcol("mu_p", 0, 15), pcol("mu_n", 0, 15), ALU.add), rd=[pc], wr=[coef0])
        P.op("dve", lambda e: e.tensor_scalar(coef0[:], coef0[:], -1.0, 1.0, ALU.mult, ALU.add), rd=[coef0], wr=[coef0])
        P.op("dve", lambda e: e.tensor_scalar(omka[:], pcol("ka", 0, 4), -1.0, 1.0, ALU.mult, ALU.add), rd=[pc], wr=[omka])
        rmask = P.sb([128, 2, 512], F32, "rmask")
        P.op("dve", lambda e: e.memset(rmask[:], 1.0), wr=[rmask])
        P.op("dve", lambda e: e.memset(rmask[:, 0, :].rearrange("p (c t) -> p c t", t=64)[:, :, 0:1], 0.0), wr=[rmask])
        P.op("dve", lambda e: e.memset(rmask[:, 1, :].rearrange("p (c t) -> p c t", t=64)[:, :, 63:64], 0.0), wr=[rmask])

        mark1 = P.scope_begin()
        ush = P.sb([128, 15, 514], BF16, "ush")
        xs = P.sb([128, 15, 512], F32, "xs")
        tmp = P.sb([128, 15, 512], F32, "tmp")
        sig = [P.sb([128, 4, 512], F32, "sig")] * 2
        aa = [P.sb([128, 4, 512], F32, "aa")] * 2
        gate = P.sb([128, 4, 512], F32, "gate")
        kkn = P.sb([128, 4, 512], F32, "kkn")
        kd = P.sb([128, 4, 512], F32, "kd")
        rf = P.sb([128, 4, 8, 4, 64], BF16, "rf")
        lb = P.sb([128, 3, 512], BF16, "lb")
        b4 = P.sb([128, 4, 512], BF16, "b4")
        gbt = P.sb([128, 2, 4, 512], BF16, "gbt")
        vtm = P.sb([128, 4, 512], BF16, "vtmr")
        plt = P.sb([128, 4, 8], F32, "plt")

        def bc(col_ap, nj, n):
            return col_ap.unsqueeze(2).broadcast_to([128, nj, n])

        for s in range(NS):
            for ti, (e0, n) in enumerate(cfg.tiles):
                nck = n // 64
                lo_edge = (e0 == 0 or e0 == CTX)
                hi_edge = (e0 + n == CTX or e0 + n == TC)
                P.dma("sp", ush[:, :, 1:n + 1], U[s, 0:15, :, e0:e0 + n].rearrange("c p t -> p c t"), rd=[L_U[s][ti]], wr=[ush])
                if lo_edge:
                    P.op("dve", lambda e: e.memset(ush[:, :, 0:1], 0.0), wr=[ush])
                else:
                    P.dma("sp", ush[:, :, 0:1], U[s, 0:15, :, e0 - 1:e0].rearrange("c p t -> p c t"), rd=[L_U[s][ti - 1]], wr=[ush],
                          allow_slow_non_contiguous=True)
                if hi_edge:
                    P.op("dve", lambda e: e.memset(ush[:, :, n + 1:n + 2], 0.0), wr=[ush])
                else:
                    P.dma("sp", ush[:, :, n + 1:n + 2], U[s, 0:15, :, e0 + n:e0 + n + 1].rearrange("c p t -> p c t"),
                          rd=[L_U[s][ti + 1]], wr=[ush], allow_slow_non_contiguous=True)
                X, Tm = xs[:, :, 0:n], tmp[:, :, 0:n]
                P.op("dve", lambda e: e.tensor_tensor(X, ush[:, :, 1:n + 1], bc(coef0[:], 15, n), ALU.mult), rd=[ush, coef0], wr=[xs])
                P.op("dve", lambda e: e.tensor_tensor(Tm, ush[:, :, 0:n], bc(pcol("mu_p", 0, 15), 15, n), ALU.mult), rd=[ush, pc], wr=[tmp])
                P.op("dve", lambda e: e.tensor_tensor(X, X, Tm, ALU.add), rd=[xs, tmp], wr=[xs])
                P.op("dve", lambda e: e.tensor_tensor(Tm, ush[:, :, 2:n + 2], bc(pcol("mu_n", 0, 15), 15, n), ALU.mult), rd=[ush, pc], wr=[tmp])
                P.op("dve", lambda e: e.tensor_tensor(X, X, Tm, ALU.add), rd=[xs, tmp], wr=[xs])
                r_, k_, v_ = xs[:, 0:4, 0:n], xs[:, 4:8, 0:n], xs[:, 8:12, 0:n]
                T0, T1, T2 = tmp[:, 0:4, 0:n], tmp[:, 4:8, 0:n], tmp[:, 8:12, 0:n]
                P.op("act", lambda e: e.activation(out=lb[:, 0, 0:n], in_=xs[:, 12, 0:n], func=AF.Tanh), rd=[xs], wr=[lb])
                P.op("act", lambda e: e.copy(lb[:, 1, 0:n], xs[:, 13, 0:n]), rd=[xs], wr=[lb])
                P.op("act", lambda e: e.activation(out=lb[:, 2, 0:n], in_=xs[:, 14, 0:n], func=AF.Sigmoid), rd=[xs], wr=[lb])
                for j in range(4):
                    ps = psf.next()
                    P.op("pe", lambda e: e.matmul(ps[:, 0:n], g2b[:, j * 128:(j + 1) * 128], lb[:, 2, 0:n], start=True, stop=True),
                         rd=[g2b, lb], wr=[ps])
                    P.op("act", lambda e: e.copy(gate[:, j, 0:n], ps[:, 0:n]), rd=[ps], wr=[gate])
                P.op("dve", lambda e: e.tensor_tensor(T0, k_, bc(pcol("kk", 0, 4), 4, n), ALU.mult), rd=[xs, pc], wr=[tmp])
                P.op("act", lambda e: e.activation(out=b4[:, :, 0:n], in_=T0, func=AF.Square), rd=[tmp], wr=[b4])
                for j in range(4):
                    ps = psf.next()
                    P.op("pe", lambda e: e.matmul(ps[:, 0:n], blk_bf[:], b4[:, j, 0:n], start=True, stop=True), rd=[blk_bf, b4], wr=[ps])
                    rsqrt(tmp[:, 4 + j, 0:n], ps[:, 0:n], 1.0, 1e-24, [ps], [tmp])
                P.op("dve", lambda e: e.tensor_tensor(kkn[:, :, 0:n], T0, T1, ALU.mult), rd=[tmp], wr=[kkn])
                P.op("dve", lambda e: e.tensor_tensor(T0, r_, k_, ALU.mult), rd=[xs], wr=[tmp])
                P.op("dve", lambda e: e.tensor_tensor(b4[:, :, 0:n], T0, bc(pcol("rk", 0, 4), 4, n), ALU.mult), rd=[tmp, pc], wr=[b4])
                for j in range(4):
                    ps = psf.next()
                    P.op("pe", lambda e: e.matmul(ps[:, 0:n], blk_bf[:], b4[:, j, 0:n], start=True, stop=True), rd=[blk_bf, b4], wr=[ps])
                    P.op("dve", lambda e: e.tensor_tensor(tmp[:, 4 + j, 0:n], ps[:, 0:n], xs[:, 8 + j, 0:n], ALU.mult), rd=[ps, xs], wr=[tmp])
                P.op("dve", lambda e: e.tensor_tensor(gbt[:, 1, :, 0:n], T1, gate[:, :, 0:n], ALU.mult), rd=[tmp, gate], wr=[gbt])
                P.op("act", lambda e: e.copy(gbt[:, 0, :, 0:n], gate[:, :, 0:n]), rd=[gate], wr=[gbt])
                for w in range(2):
                    P.dma("sp", GB[s, w, :, :, e0:e0 + n].rearrange("j p t -> p j t"), gbt[:, w, :, 0:n], rd=[gbt], wr=[L_GB[s][ti]])
                P.op("act", lambda e: e.copy(b4[:, :, 0:n], v_), rd=[xs], wr=[b4])
                for sub in range(n // 128):
                    pb = psb.next()
                    for j in range(4):
                        P.op("pe", lambda e: e.transpose(pb[:, j * 128:(j + 1) * 128], b4[:, j, sub * 128:(sub + 1) * 128], ident_bf[:]),
                             rd=[b4, ident_bf], wr=[pb])
                    P.op("dve", lambda e: e.tensor_copy(vtm[:, sub, :], pb[:, 0:512]), rd=[pb], wr=[vtm])
                P.dma("sp", VT[s, e0:e0 + n, :].rearrange("(a p) c -> p a c", p=128), vtm[:, 0:n // 128, :], rd=[vtm], wr=[L_VT[s][ti]])
                for d in range(2):
                    rv = (lambda ap: ap) if d == 0 else (lambda ap: ap[:, ::-1])
                    rfv = lambda kind: rf[:, :, 0:nck, kind, :]
                    v4 = lambda ap: ap.rearrange("p j (c t) -> p j c t", t=64)
                    for (wp, li, dst, bn) in ((w2p, 0, sig[d], "w0"), (a2p, 1, aa[d], "a0")):
                        for j in range(4):
                            ps = psf.next()
                            P.op("pe", lambda e: e.matmul(ps[:, 0:n], wp[:, d, j * 128:(j + 1) * 128], lb[:, li, 0:n], start=True, stop=True),
                                 rd=[wp, lb], wr=[ps])
                            P.op("act", lambda e: e.activation(out=dst[:, j, 0:n], in_=ps[:, 0:n], func=AF.Sigmoid,
                                                               bias=pcol(bn, d * 4 + j)), rd=[ps, pc], wr=[dst])
                    P.op("dve", lambda e: e.tensor_tensor(T0, aa[d][:, :, 0:n], bc(pcol("ka", 0, 4), 4, n), ALU.mult), rd=[aa[d], pc], wr=[tmp])
                    P.op("dve", lambda e: e.tensor_tensor(T0, T0, bc(omka[:], 4, n), ALU.add), rd=[tmp, omka], wr=[tmp])
                    P.op("dve", lambda e: e.tensor_tensor(kd[:, :, 0:n], T0, k_, ALU.mult), rd=[tmp, xs], wr=[kd])
                    for j in range(4):
                        P.op("dve", lambda e: e.tensor_tensor_scan(rv(tmp[:, 8 + j, 0:n]), rv(rmask[:, d, 0:n]), rv(sig[d][:, j, 0:n]),
                                                                   0.0, ALU.mult, ALU.add), rd=[rmask, sig[d]], wr=[tmp])
                    P.op("dve", lambda e: e.tensor_tensor(T0, T2, sig[d][:, :, 0:n], ALU.subtract), rd=[tmp, sig[d]], wr=[tmp])
                    P.op("act", lambda e: e.activation(out=T0, in_=T0, func=AF.Exp, scale=-C0), rd=[tmp], wr=[tmp])
                    P.op("dve", lambda e: e.tensor_scalar(T0, T0, -1.0, None, ALU.mult), rd=[tmp], wr=[tmp])
                    P.op("dve", lambda e: e.tensor_tensor(rfv(0), v4(kkn[:, :, 0:n]), v4(T0), ALU.mult), rd=[kkn, tmp], wr=[rf])
                    P.op("act", lambda e: e.activation(out=T0, in_=T2, func=AF.Exp, scale=-C0), rd=[tmp], wr=[tmp])
                    P.op("dve", lambda e: e.tensor_tensor(rfv(1), v4(r_), v4(T0), ALU.mult), rd=[xs, tmp], wr=[rf])
                    lastpos = 63 if d == 0 else 0
                    P.op("dve", lambda e: e.tensor_copy(plt[:, :, 0:nck], v4(T0)[:, :, :, lastpos]), rd=[tmp], wr=[plt])
                    P.dma("sp", PLs[s, d, :, :, e0 // 64:e0 // 64 + nck], plt[:, :, 0:nck], rd=[plt], wr=[L_RF[s][d][ti]])
                    P.op("act", lambda e: e.activation(out=T0, in_=T2, func=AF.Exp, scale=C0), rd=[tmp], wr=[tmp])
                    P.op("dve", lambda e: e.tensor_tensor(T1, kkn[:, :, 0:n], aa[d][:, :, 0:n], ALU.mult), rd=[kkn, aa[d]], wr=[tmp])
                    P.op("dve", lambda e: e.tensor_tensor(rfv(2), v4(T1), v4(T0), ALU.mult), rd=[tmp], wr=[rf])
                    P.op("dve", lambda e: e.tensor_tensor(rfv(3), v4(kd[:, :, 0:n]), v4(T0), ALU.mult), rd=[kd, tmp], wr=[rf])
                    for j in range(4):
                        P.dma("sp" if j % 2 == 0 else "pool", RF[s, d, j, :, e0 // 64:e0 // 64 + nck, :, :], rf[:, j, 0:nck, :, :],
                              rd=[rf], wr=[L_RF[s][d][ti]])
        P.scope_end(mark1)
        if cfg.stop_after == "rwB1":
            P.scope_end(mark)
            return

        mark2 = P.scope_begin()
        mk2 = P.sb([64, 2, 128], F32, "mk2")
        mkT = P.sb([64, 2, 64], F32, "mkT")
        for d in range(2):
            P.dma("sp", mk2[:, d, 0:64], masks_d[2 * d], wr=[mk2])
            P.dma("sp", mk2[:, d, 64:128], masks_d[2 * d + 1], wr=[mk2])
            P.dma("sp", mkT[:, d, :], masks_d[2 - 2 * d], wr=[mkT])
        lvm_f = P.sb([64, 6, 64], F32, "lvm_f")
        lvm = P.sb([64, 6, 64], BF16, "lvm")
        P.dma("sp", lvm_f[:], lvmask_d.rearrange("l i t -> i l t"), wr=[lvm_f])
        P.op("dve", lambda e: e.tensor_copy(lvm[:], lvm_f[:]), rd=[lvm_f], wr=[lvm])
        NSLOT = 2

        def alloc_slot():
            return dict(rt=P.sb([64, 8, 4, 4, 64], BF16, "rft"), vt=P.sb([64, 4, 512], BF16, "vt2"),
                        pl=P.sb([64, 8, 4], F32, "plr"), ys=P.sb([64, 8, 256], F32, "ysb"),
                        ST=P.sb([64, 8, 64], F32, "ST"), STb=P.sb([64, 8, 64], BF16, "STb"),
                        AB=P.sb([64, 8, 128], BF16, "AB"), AK=P.sb([64, 8, 128], BF16, "AK"),
                        NM=P.sb([64, 2, 6, 8, 64], BF16, "NM"), TH=P.sb([64, 8, 64], BF16, "TH"),
                        TT=P.sb([64, 8, 64], BF16, "TT"), Zs=P.sb([64, 2, 8, 64], BF16, "Zs"),
                        Wb=P.sb([64, 8, 64], BF16, "Wb"), XW=P.sb([64, 8, 128], BF16, "XW"),
                        KB=P.sb([64, 8, 2, 64], BF16, "KB"))
        slots = [alloc_slot() for _ in range(NSLOT)]
        for i_, sl_ in enumerate(slots):
            sl_["ps"] = Rot(psf.bufs[i_ * 3:(i_ + 1) * 3])
        j3 = lambda ap, c: ap.rearrange("p (j c) -> p j c", c=c)
        HS = list(range(8))

        def scan_stream(s, d, B_):
                rt, vt, pl, ys = B_["rt"], B_["vt"], B_["pl"], B_["ys"]
                ST, STb, AB, AK, NM, TH, TT = B_["ST"], B_["STb"], B_["AB"], B_["AK"], B_["NM"], B_["TH"], B_["TT"]
                Zs, Wb, XW, KB = B_["Zs"], B_["Wb"], B_["XW"], B_["KB"]
                myps = B_["ps"]
                P.op("dve", lambda e: e.memset(ST[:], 0.0), wr=[ST])
                P.op("dve", lambda e: e.memset(STb[:], 0.0), wr=[STb])
                order = list(range(len(cfg.tiles)))
                nctx_t = sum(1 for (e0, n) in cfg.tiles if e0 < CTX)
                if d == 1:
                    order = list(range(nctx_t - 1, -1, -1)) + list(range(len(cfg.tiles) - 1, nctx_t - 1, -1))
                pieces = []
                for ti in order:
                    e0_, n_ = cfg.tiles[ti]
                    offs = list(range(0, n_, 256))
                    if d == 1:
                        offs = offs[::-1]
                    pieces += [(ti, e0_ + o, min(256, n_ - o)) for o in offs]
                for (ti, e0, n) in pieces:
                    nck = n // 64
                    c0 = e0 // 64
                    for h in HS:
                        j, hl = h // 2, h % 2
                        P.dma("sp" if h % 2 == 0 else "pool", rt[:, h, 0:nck, :, :], RF[s, d, j, hl * 64:(hl + 1) * 64, c0:c0 + nck, :, :],
                              rd=[L_RF[s][d][ti]], wr=[rt])
                        P.dma("sp", pl[:, h, 0:nck], PLs[s, d, hl * 64:(hl + 1) * 64, j, c0:c0 + nck], rd=[L_RF[s][d][ti]], wr=[pl])
                    P.dma("pool", vt[:, 0:nck, :], VT[s, e0:e0 + n, :].rearrange("(c t) f -> t c f", t=64), rd=[L_VT[s][ti]], wr=[vt])
                    for ck in (range(nck) if d == 0 else range(nck - 1, -1, -1)):
                        Vh = lambda h: vt[:, ck, h * 64:(h + 1) * 64]
                        G2 = ((0, slice(0, 4)), (1, slice(4, 8)))
                        psB, psT = [myps.next(), myps.next()], myps.next()
                        for h in HS:
                            g, c = h // 4, h % 4
                            P.op("pe", lambda e: e.matmul(psB[g][0:64, c * 128:(c + 1) * 128], rt[:, h, ck, 2, :], rt[:, h, ck, 0:2, :],
                                                          start=True, stop=True), rd=[rt], wr=[psB[g]])
                            P.op("pe", lambda e: e.matmul(psT[0:64, h * 64:(h + 1) * 64], rt[:, h, ck, 0, :], rt[:, h, ck, 2, :],
                                                          start=True, stop=True), rd=[rt], wr=[psT])
                        yield
                        m2 = mk2[:, d, :].unsqueeze(1).broadcast_to([64, 4, 128])
                        for g, hs in G2:
                            P.op("dve", lambda e: e.tensor_tensor(AB[:, hs, :], j3(psB[g][0:64, 0:512], 128), m2, ALU.mult), rd=[psB[g], mk2], wr=[AB])
                        P.op("dve", lambda e: e.tensor_tensor(XW[:, :, 0:64], j3(psT[0:64, 0:512], 64),
                                                               mkT[:, d, :].unsqueeze(1).broadcast_to([64, 8, 64]), ALU.mult),
                             rd=[psT, mkT], wr=[XW])
                        psK = [myps.next(), myps.next()]
                        for h in HS:
                            g, c = h // 4, h % 4
                            P.op("pe", lambda e: e.matmul(psK[g][0:64, c * 128:(c + 1) * 128], rt[:, h, ck, 3, :], rt[:, h, ck, 0:2, :],
                                                          start=True, stop=True), rd=[rt], wr=[psK[g]])
                        yield
                        for g, hs in G2:
                            P.op("dve", lambda e: e.tensor_tensor(AK[:, hs, :], j3(psK[g][0:64, 0:512], 128), m2, ALU.mult), rd=[psK[g], mk2], wr=[AK])
                        lv4 = lvm[:].unsqueeze(2).broadcast_to([64, 6, 8, 64])
                        P.op("dve", lambda e: e.tensor_tensor(NM[:, 0], AB[:, :, 0:64].unsqueeze(1).broadcast_to([64, 6, 8, 64]), lv4, ALU.mult),
                             rd=[AB, lvm], wr=[NM])
                        P.op("dve", lambda e: e.tensor_tensor(NM[:, 1], XW[:, :, 0:64].unsqueeze(1).broadcast_to([64, 6, 8, 64]), lv4, ALU.mult),
                             rd=[XW, lvm], wr=[NM])
                        idb = ident_bf[0:64, 0:64].unsqueeze(1).broadcast_to([64, 8, 64])
                        P.op("dve", lambda e: e.tensor_tensor(TH[:], NM[:, 0, 0], idb, ALU.add), rd=[NM, ident_bf], wr=[TH])
                        P.op("dve", lambda e: e.tensor_tensor(TT[:], NM[:, 1, 0], idb, ALU.add), rd=[NM, ident_bf], wr=[TT])
                        yield
                        psW = myps.next()
                        for h in HS:
                            o_ = psW[0:64, h * 64:(h + 1) * 64]
                            P.op("pe", lambda e: e.matmul(o_, rt[:, h, ck, 0, :], STb[:, h, :], start=True, stop=False), rd=[rt, STb], wr=[psW])
                            P.op("pe", lambda e: e.matmul(o_, AK[:, h, 0:64], Vh(h), start=False, stop=True), rd=[AK, vt], wr=[psW])
                        P.op("act", lambda e: e.copy(Wb[:], j3(psW[0:64, 0:512], 64)), rd=[psW], wr=[Wb])
                        yield
                        for lvl in range(1, 6):
                            psZ, psZp = myps.next(), myps.next()
                            for h in HS:
                                P.op("pe", lambda e: e.matmul(psZ[0:64, h * 64:(h + 1) * 64], NM[:, 1, lvl, h, :], TH[:, h, :],
                                                              start=True, stop=True), rd=[NM, TH], wr=[psZ])
                                P.op("pe", lambda e: e.matmul(psZp[0:64, h * 64:(h + 1) * 64], NM[:, 0, lvl, h, :], TT[:, h, :],
                                                              start=True, stop=True), rd=[NM, TT], wr=[psZp])
                            P.op("act", lambda e: e.copy(Zs[:, 0], j3(psZ[0:64, 0:512], 64)), rd=[psZ], wr=[Zs])
                            P.op("dve", lambda e: e.tensor_copy(Zs[:, 1], j3(psZp[0:64, 0:512], 64)), rd=[psZp], wr=[Zs])
                            yield
                            psA, psBt = myps.next(), myps.next()
                            for h in HS:
                                P.op("pe", lambda e: e.matmul(psA[0:64, h * 64:(h + 1) * 64], TT[:, h, :], Zs[:, 0, h, :],
                                                              start=True, stop=True), rd=[TT, Zs], wr=[psA])
                                P.op("pe", lambda e: e.matmul(psBt[0:64, h * 64:(h + 1) * 64], TH[:, h, :], Zs[:, 1, h, :],
                                                              start=True, stop=True), rd=[TH, Zs], wr=[psBt])
                            P.op("dve", lambda e: e.tensor_tensor(TH[:], TH[:], j3(psA[0:64, 0:512], 64), ALU.add), rd=[TH, psA], wr=[TH])
                            P.op("dve", lambda e: e.tensor_tensor(TT[:], TT[:], j3(psBt[0:64, 0:512], 64), ALU.add), rd=[TT, psBt], wr=[TT])
                            yield
                        psUu = myps.next()
                        for h in HS:
                            P.op("pe", lambda e: e.matmul(psUu[0:64, h * 64:(h + 1) * 64], TH[:, h, :], Wb[:, h, :], start=True, stop=True),
                                 rd=[TH, Wb], wr=[psUu])
                        P.op("act", lambda e: e.copy(XW[:, :, 64:128], j3(psUu[0:64, 0:512], 64)), rd=[psUu], wr=[XW])
                        yield
                        Ub = lambda h: XW[:, h, 64:128]
                        psY = myps.next()
                        for h in HS:
                            o_ = psY[0:64, h * 64:(h + 1) * 64]
                            P.op("pe", lambda e: e.matmul(o_, STb[:, h, :], rt[:, h, ck, 1, :], start=True, stop=False), rd=[STb, rt], wr=[psY])
                            P.op("pe", lambda e: e.matmul(o_, Vh(h), AK[:, h, 64:128], start=False, stop=False), rd=[vt, AK], wr=[psY])
                            P.op("pe", lambda e: e.matmul(o_, Ub(h), AB[:, h, 64:128], start=False, stop=True), rd=[XW, AB], wr=[psY])
                        P.op("act", lambda e: e.copy(ys[:, :, ck * 64:(ck + 1) * 64], j3(psY[0:64, 0:512], 64)), rd=[psY], wr=[ys])
                        yield
                        psX = [myps.next(), myps.next()]
                        for h in HS:
                            g, c = h // 4, h % 4
                            for w, kind in enumerate((3, 2)):
                                P.op("pe", lambda e: e.matmul(psX[g][0:64, (c * 2 + w) * 64:(c * 2 + w + 1) * 64], rt[:, h, ck, kind, :],
                                                              ident_bf[0:64, 0:64], start=True, stop=True), rd=[rt, ident_bf], wr=[psX[g]])
                        for g, hs in G2:
                            P.op("dve", lambda e: e.tensor_copy(KB[:, hs, :, :], psX[g][0:64, 0:512].rearrange("p (j w c) -> p j w c", w=2, c=64)),
                                 rd=[psX[g]], wr=[KB])
                        yield
                        psS = myps.next()
                        for h in HS:
                            o_ = psS[0:64, h * 64:(h + 1) * 64]
                            P.op("pe", lambda e: e.matmul(o_, KB[:, h, 0, :], Vh(h), start=True, stop=False), rd=[KB, vt], wr=[psS])
                            P.op("pe", lambda e: e.matmul(o_, KB[:, h, 1, :], Ub(h), start=False, stop=True), rd=[KB, XW], wr=[psS])
                        P.op("dve", lambda e: e.tensor_tensor(ST[:], ST[:], j3(psS[0:64, 0:512], 64), ALU.add), rd=[ST, psS], wr=[ST])
                        P.op("dve", lambda e: e.tensor_tensor(ST[:], ST[:], pl[:, :, ck:ck + 1].broadcast_to([64, 8, 64]), ALU.mult),
                             rd=[ST, pl], wr=[ST])
                        P.op("act", lambda e: e.copy(STb[:], ST[:]), rd=[ST], wr=[STb])
                        yield
                    for hl in range(2):
                        P.dma("sp", YR[s, d, :, hl * 64:(hl + 1) * 64, e0:e0 + n].rearrange("j p t -> p j t"), ys[:, hl::2, 0:n],
                              rd=[ys], wr=[L_YR[s][d][ti]])

        streams = [(s_, d_) for s_ in range(NS) for d_ in range(2)]
        for g0 in range(0, len(streams), NSLOT):
            gens = [scan_stream(s_, d_, slots[i]) for i, (s_, d_) in enumerate(streams[g0:g0 + NSLOT])]
            while gens:
                for g_ in list(gens):
                    try:
                        next(g_)
                    except StopIteration:
                        gens.remove(g_)
        P.scope_end(mark2)
        if cfg.stop_after == "rwB2":
            P.scope_end(mark)
            return

        mark3 = P.scope_begin()
        ya = Rot([P.sb([128, 4, 512], F32, "ya") for _ in range(2)])
        yb_ = Rot([P.sb([128, 4, 512], F32, "yb") for _ in range(2)])
        gbl = Rot([P.sb([128, 2, 4, 512], BF16, "gbl") for _ in range(2)])
        yo = Rot([P.sb([128, 4, 512], BF16, "yor") for _ in range(2)])
        scr = {"xb": P.sb([128, 4, 512], BF16, "xbr"), "sq": P.sb([128, 4, 512], BF16, "sqr"),
               "mean": P.sb([128, 4, 512], F32, "meanr"), "rstd": P.sb([128, 4, 512], F32, "rstdr"), "rd": []}
        for s in range(NS):
            for ti, (e0, n) in enumerate(cfg.tiles):
                if last and e0 < CTX:
                    continue
                a, b, g, y = ya.next(), yb_.next(), gbl.next(), yo.next()
                P.dma("sp", a[:, :, 0:n], YR[s, 0, :, :, e0:e0 + n].rearrange("j p t -> p j t"), rd=[L_YR[s][0][ti]], wr=[a])
                P.dma("pool", b[:, :, 0:n], YR[s, 1, :, :, e0:e0 + n].rearrange("j p t -> p j t"), rd=[L_YR[s][1][ti]], wr=[b])
                for w in range(2):
                    P.dma("sp", g[:, w, :, 0:n], GB[s, w, :, :, e0:e0 + n].rearrange("j p t -> p j t"), rd=[L_GB[s][ti]], wr=[g])
                P.op("dve", lambda e: e.tensor_tensor(a[:, :, 0:n], a[:, :, 0:n], b[:, :, 0:n], ALU.add), rd=[a, b], wr=[a])
                scr["rd"] = [a]
                stats_rstd(lambda j: a[:, j, 0:n], 4, n, (blk_bf, False), RW_GN_EPS, 1.0 / 64, scr)
                A_, B_ = a[:, :, 0:n], b[:, :, 0:n]
                P.op("dve", lambda e: e.tensor_tensor(A_, A_, scr["mean"][:, :, 0:n], ALU.subtract), rd=[a, scr["mean"]], wr=[a])
                P.op("dve", lambda e: e.tensor_tensor(A_, A_, scr["rstd"][:, :, 0:n], ALU.mult), rd=[a, scr["rstd"]], wr=[a])
                P.op("dve", lambda e: e.tensor_tensor(A_, A_, bc(pcol("rlnw", 0, 4), 4, n), ALU.mult), rd=[a, pc], wr=[a])
                P.op("dve", lambda e: e.tensor_tensor(A_, A_, bc(pcol("rlnb", 0, 4), 4, n), ALU.add), rd=[a, pc], wr=[a])
                P.op("dve", lambda e: e.tensor_tensor(A_, A_, g[:, 0, :, 0:n], ALU.mult), rd=[a, g], wr=[a])
                P.op("dve", lambda e: e.tensor_tensor(y[:, :, 0:n], A_, g[:, 1, :, 0:n], ALU.add), rd=[a, g], wr=[y])
                P.dma("sp", YBR[s, 0, :, :, e0:e0 + n].rearrange("j p t -> p j t"), y[:, :, 0:n], rd=[y], wr=[L_Y[s][0][ti]])
        P.scope_end(mark3)
        P.scope_end(mark)

    modulation()
    for l in range(DEPTH):
        cast_layer(l)
    for l in range(DEPTH):
        layer_params(l)
        last = (l == DEPTH - 1)
        if cfg.stop_after == "cast":
            break
        phase_A(l)
        if cfg.stop_after == "A":
            break
        conv_branch(l, last)
        if cfg.stop_after == "conv":
            break
        if cfg.stop_after not in ("rw", "rwB1", "rwB2"):
            mlstm_branch(l, last)
        if cfg.stop_after == "ml":
            break
        rwkv_branch(l, last)
        if cfg.stop_after in ("rw", "rwB1", "rwB2", "noC"):
            break
        phase_C(l, last)

    P.barrier()
    return nc, P


def host_maps(inp, cfg, n_cores):
    NS, DEPTH = cfg.NS, cfg.DEPTH
    f = lambda a: np.ascontiguousarray(np.asarray(a, np.float32))
    shared = {
        "w_ada": f(inp["w_ada"]), "b_ada": f(inp["b_ada"]),
        "w_in": pad_w_in(f(inp["w_in"])),
        "pcols": np.stack([pack_pcols(inp, l) for l in range(DEPTH)]),
        "rw_w2": f(inp["rw_w2"]).reshape(DEPTH, 128, 512),
        "rw_a2": f(inp["rw_a2"]).reshape(DEPTH, 128, 512),
        "rw_g2": f(inp["rw_g2"]),
        "mlb": np.ascontiguousarray(np.concatenate([np.stack([f(inp["ml_ib"]).reshape(DEPTH, 8), f(inp["ml_fb"]).reshape(DEPTH, 8)], -1)] * 2, 1)),
        "p_abc": np.ascontiguousarray(np.stack([f(inp["p_a"]), f(inp["p_b"]), f(inp["p_c"])], 1)),
        "w_out": f(inp["w_out"]), "w_mlp1": f(inp["w_mlp1"]), "w_mlp2": f(inp["w_mlp2"]),
        "g_final": f(inp["g_final"]).reshape(1, D),
        "ident": np.eye(128, dtype=np.float32),
        "selh": np.ascontiguousarray(np.kron(np.eye(4, dtype=np.float32), np.ones((1, 128), np.float32))),
        "blkones": np.kron(np.eye(2, dtype=np.float32), np.ones((64, 64), np.float32)),
    }
    idx = np.arange(64)
    shared["masks"] = np.stack([(idx[:, None] < idx[None, :]), (idx[:, None] <= idx[None, :]),
                                (idx[:, None] > idx[None, :]), (idx[:, None] >= idx[None, :])]).astype(np.float32)
    lv = lambda sz: ((idx[:, None] // (2 * sz) == idx[None, :] // (2 * sz)) & (idx[:, None] // sz != idx[None, :] // sz))
    shared["lvmask"] = np.stack([lv(sz) for sz in (1, 2, 4, 8, 16, 32)]).astype(np.float32)
    maps = []
    x, c, ctx, c_ctx = f(inp["x"]), f(inp["c"]), f(inp["ctx"]), f(inp["c_ctx"])
    for i in range(n_cores):
        m = dict(shared)
        m["x"] = x[i * NS:(i + 1) * NS]
        m["ctx"] = ctx[i * NS:(i + 1) * NS]
        cv = np.zeros((3, D), np.float32)
        cv[0:NS] = c[i * NS:(i + 1) * NS]
        cv[2] = c_ctx
        m["cvec"] = cv
        maps.append(m)
    return maps


_CACHE = {}


def kernel(**inputs):
    n_cores = 8
    B = inputs["x"].shape[0]
    cfg = Cfg(NS=B // n_cores, T=inputs["x"].shape[1], CTX=inputs["ctx"].shape[1], DEPTH=inputs["w_ada"].shape[0])
    key = (cfg.NS, cfg.T, cfg.CTX, cfg.DEPTH)
    if key not in _CACHE:
        _CACHE[key] = build(cfg)
    nc, _ = _CACHE[key]
    maps = host_maps(inputs, cfg, n_cores)
    res = run_bass_kernel_spmd(nc, maps, core_ids=list(range(n_cores)))
    return np.concatenate([np.asarray(r["out"], np.float32) for r in res.results], axis=0)
```

```python
import numpy as np
import concourse.bass as bass
import concourse.mybir as mybir
from concourse.bass_utils import run_bass_kernel_spmd

F32 = mybir.dt.float32
BF16 = mybir.dt.bfloat16
ALU = mybir.AluOpType
AF = mybir.ActivationFunctionType
AX = mybir.AxisListType

D = 1024
KC = 8
NCH = 64
PW = NCH * 128
DFF = 4096
L_CH = 64
NORM_EPS = 1e-6
LN_EPS = 1e-5
RW_GN_EPS = 64e-5

SELF_WAIT = True


class LT:
    __slots__ = ("name", "w", "r")

    def __init__(self, name=""):
        self.name = name
        self.w = None
        self.r = {}


class Buf:
    def __init__(self, t, lt, psum=False):
        self.t = t
        self.lt = lt
        self.psum = psum

    def __getitem__(self, k):
        return self.t[k]


class Prog:
    def __init__(self, nc, n_dma_sems=6):
        self.nc = nc
        self.engs = {"pe": nc.tensor, "act": nc.scalar, "dve": nc.vector,
                     "pool": nc.gpsimd, "sp": nc.sync}
        self.sems = {}
        self.cnt = {}
        for k in self.engs:
            self.sems[k] = nc.alloc_semaphore(name="s_" + k)
            self.cnt[k] = 0
        self.dq = {}
        for q in ("sp", "pool", "act"):
            lst = []
            for i in range(n_dma_sems):
                key = "d_%s%d" % (q, i)
                self.sems[key] = nc.alloc_semaphore(name=key)
                self.cnt[key] = 0
                lst.append(key)
            self.dq[q] = [lst, 0]
        self.obs = {k: {} for k in self.engs}
        self.ninst = 0
        self.nbuf = 0

    def _need(self, e, deps):
        for (k, v) in deps:
            if k == e and (e == "pe" or not SELF_WAIT):
                continue
            if self.obs[e].get(k, 0) < v:
                self.engs[e].wait_ge(self.sems[k], v)
                self.obs[e][k] = v
                self.ninst += 1

    @staticmethod
    def _lts(bufs):
        out = []
        for b in bufs:
            if isinstance(b, LT):
                out.append(b)
            elif isinstance(b, Buf):
                out.append(b.lt)
            else:
                raise TypeError(b)
        return out

    def _deps(self, reads, writes):
        deps = []
        for t in reads:
            if t.w is not None:
                deps.append(t.w)
        for t in writes:
            if t.w is not None:
                deps.append(t.w)
            for k, v in t.r.items():
                deps.append((k, v))
        return deps

    def op(self, e, fn, rd=(), wr=()):
        if e != "pe":
            pr = [b for b in rd if isinstance(b, Buf) and b.psum]
            if pr:
                rd = [b for b in rd if not (isinstance(b, Buf) and b.psum)]
                wr = list(wr) + [b for b in pr if b not in wr]
        reads, writes = self._lts(rd), self._lts(wr)
        self._need(e, self._deps(reads, writes))
        ins = fn(self.engs[e])
        self.cnt[e] += 1
        c = self.cnt[e]
        ins.then_inc(self.sems[e], 1)
        self.ninst += 1
        for t in reads:
            t.r[e] = c
        for t in writes:
            t.w = (e, c)
            t.r = {}
        return ins

    def dma(self, q, out, in_, rd=(), wr=(), **kw):
        reads, writes = self._lts(rd), self._lts(wr)
        lst, idx = self.dq[q]
        key = lst[idx % len(lst)]
        self.dq[q][1] = idx + 1
        deps = self._deps(reads, writes)
        if self.cnt[key] > 0:
            deps.append((key, self.cnt[key]))
        self._need(q, deps)
        ins = self.engs[q].dma_start(out=out, in_=in_, **kw)
        self.cnt[key] += 16
        c = self.cnt[key]
        ins.then_inc(self.sems[key], 16)
        self.ninst += 1
        for t in reads:
            t.r[key] = c
        for t in writes:
            t.w = (key, c)
            t.r = {}
        return ins

    def finish(self, bufs, e="sp"):
        deps = []
        for t in self._lts(bufs):
            if t.w is not None:
                deps.append(t.w)
        self._need(e, deps)

    def barrier(self):
        allc = [(k, v) for k, v in self.cnt.items() if v > 0]
        for e in self.engs:
            self._need(e, allc)

    def scope_begin(self):
        return (self.nc.sbuf_base, self.nc.sbuf_top)

    def scope_end(self, mark):
        self.barrier()
        self.nc.sbuf_base, self.nc.sbuf_top = mark

    def sb(self, shape, dt, name=None):
        self.nbuf += 1
        nm = "%s_%d" % (name or "b", self.nbuf)
        return Buf(self.nc.alloc_sbuf_tensor(nm, list(shape), dt), LT(nm))

    def ps(self, shape, dt, name=None):
        self.nbuf += 1
        nm = "%s_%d" % (name or "p", self.nbuf)
        return Buf(self.nc.alloc_psum_tensor(nm, list(shape), dt), LT(nm), psum=True)


class Rot:
    def __init__(self, bufs):
        self.bufs = bufs
        self.i = 0

    def next(self):
        b = self.bufs[self.i % len(self.bufs)]
        self.i += 1
        return b


PCOLS = {}
_off = 0
for _nm, _n in (("mu_p", 15), ("mu_n", 15), ("w0", 8), ("a0", 8), ("kk", 4), ("ka", 4), ("rk", 4),
                ("rlnw", 4), ("rlnb", 4), ("cdw", 4 * 31), ("cdb", 4), ("clnw", 4), ("clnb", 4),
                ("mnw", 4), ("g1", 8), ("g2", 8)):
    PCOLS[_nm] = (_off, _n)
    _off += _n
NPCOL = _off


def _cm(vec, nchunks):
    return np.ascontiguousarray(np.asarray(vec, np.float32).reshape(nchunks, 128).T)


def pack_pcols(inp, l):
    out = np.zeros((128, NPCOL), np.float32)

    def put(nm, arr):
        o, n = PCOLS[nm]
        assert arr.shape == (128, n), (nm, arr.shape)
        out[:, o:o + n] = arr
    put("mu_p", _cm(inp["mu_prev"][l], 15))
    put("mu_n", _cm(inp["mu_next"][l], 15))
    put("w0", np.concatenate([_cm(inp["rw_w0"][l, d], 4) for d in range(2)], 1))
    put("a0", np.concatenate([_cm(inp["rw_a0"][l, d], 4) for d in range(2)], 1))
    put("kk", _cm(inp["rw_kk"][l], 4))
    put("ka", _cm(inp["rw_ka"][l], 4))
    put("rk", _cm(inp["rw_rk"][l].reshape(-1), 4))
    put("rlnw", _cm(inp["rw_lnw"][l], 4))
    put("rlnb", _cm(inp["rw_lnb"][l], 4))
    dw = np.asarray(inp["cv_dw"][l], np.float32)
    put("cdw", np.concatenate([np.ascontiguousarray(dw[:, j * 128:(j + 1) * 128].T) for j in range(4)], 1))
    put("cdb", _cm(inp["cv_db"][l], 4))
    put("clnw", _cm(inp["cv_lnw"][l], 4))
    put("clnb", _cm(inp["cv_lnb"][l], 4))
    put("mnw", _cm(inp["ml_nw"][l], 4))
    put("g1", _cm(inp["g_norm1"][l], 8))
    put("g2", _cm(inp["g_norm2"][l], 8))
    return out


def pad_w_in(w):
    Ln = w.shape[0]
    out = np.zeros((Ln, D, PW), np.float32)
    out[:, :, 0:5008] = w[:, :, 0:5008]
    out[:, :, 5120:8192] = w[:, :, 5008:8080]
    return out


class Cfg:
    def __init__(self, NS=2, T=4096, CTX=256, DEPTH=4, debug=False, stop_after=None, c_stop=9):
        self.c_stop = c_stop
        self.NS, self.T, self.CTX, self.DEPTH = NS, T, CTX, DEPTH
        self.TC = T + CTX
        self.debug = debug
        self.stop_after = stop_after
        self.tiles = []
        e = 0
        while e < CTX:
            n = min(512, CTX - e)
            self.tiles.append((e, n)); e += n
        while e < self.TC:
            n = min(512, self.TC - e)
            self.tiles.append((e, n)); e += n


def build(cfg):
    NS, T, CTX, DEPTH, TC = cfg.NS, cfg.T, cfg.CTX, cfg.DEPTH, cfg.TC
    nc = bass.Bass("TRN2", target_bir_lowering=False)
    P = Prog(nc)
    kind_dbg = "ExternalOutput" if cfg.debug else "Internal"

    def din(name, shape, dt=F32):
        return nc.dram_tensor(name, list(shape), dt, kind="ExternalInput").ap()

    def dscr(name, shape, dt, dbg=False):
        return nc.dram_tensor(name, list(shape), dt, kind=(kind_dbg if dbg else "Internal")).ap()

    x_in = din("x", [NS, T, D])
    ctx_in = din("ctx", [NS, CTX, D])
    cvec = din("cvec", [3, D])
    w_ada = din("w_ada", [DEPTH, D, 6 * D])
    b_ada = din("b_ada", [DEPTH, 6 * D])
    w_in = din("w_in", [DEPTH, D, PW])
    pcols_d = din("pcols", [DEPTH, 128, NPCOL])
    rw_w2 = din("rw_w2", [DEPTH, 128, 512])
    rw_a2 = din("rw_a2", [DEPTH, 128, 512])
    rw_g2 = din("rw_g2", [DEPTH, 128, 512])
    mlb = din("mlb", [DEPTH, 16, 2])
    p_abc = din("p_abc", [DEPTH, 3, 512, D])
    w_out = din("w_out", [DEPTH, D, D])
    w_mlp1 = din("w_mlp1", [DEPTH, D, DFF])
    w_mlp2 = din("w_mlp2", [DEPTH, DFF, D])
    g_final = din("g_final", [1, D])
    out = nc.dram_tensor("out", [NS, T, D], F32, kind="ExternalOutput").ap()

    xc = dscr("xc", [NS, CTX, D], F32)
    U = dscr("U", [NS, NCH, 128, TC], BF16, dbg=True)
    MOD = dscr("MOD", [DEPTH, 3, 6 * D], F32, dbg=True)
    WB_in = dscr("WB_in", [DEPTH, D, PW], BF16)
    WB_p = dscr("WB_p", [DEPTH, 3, 512, D], BF16)
    WB_out = dscr("WB_out", [DEPTH, D, D], BF16)
    WB_1 = dscr("WB_1", [DEPTH, D, DFF], BF16)
    WB_2 = dscr("WB_2", [DEPTH, DFF, D], BF16)
    YBR = dscr("YBR", [NS, 3, 4, 128, TC], BF16, dbg=True)

    L_U = [[LT("U%d_%d" % (s, i)) for i in range(len(cfg.tiles))] for s in range(NS)]
    L_X = [[LT("X%d_%d" % (s, i)) for i in range(len(cfg.tiles))] for s in range(NS)]
    L_WB = [LT("WB%d" % l) for l in range(DEPTH)]
    L_MOD = LT("MOD")
    L_Y = [[[LT() for i in range(len(cfg.tiles))] for b in range(3)] for s in range(NS)]

    psf = Rot([P.ps([128, 512], F32, "psf") for _ in range(6)])
    psb = Rot([P.ps([128, 1024], BF16, "psb") for _ in range(2)])

    ident_bf = P.sb([128, 128], BF16, "identb")
    ident_f = P.sb([128, 128], F32, "identf")
    ones_bf = P.sb([128, 128], BF16, "onesb")
    cst = {}

    def const_setup():
        pass

    ident_d = din("ident", [128, 128])
    blk_d = din("blkones", [128, 128])
    masks_d = din("masks", [4, 64, 64])
    lvmask_d = din("lvmask", [6, 64, 64])
    P.dma("sp", ident_f[:], ident_d, wr=[ident_f])
    P.op("dve", lambda e: e.tensor_copy(ident_bf[:], ident_f[:]), rd=[ident_f], wr=[ident_bf])
    P.op("dve", lambda e: e.memset(ones_bf[:], 1.0), wr=[ones_bf])
    blk_f = P.sb([128, 128], F32, "blkf")
    blk_bf = P.sb([128, 128], BF16, "blkb")
    P.dma("sp", blk_f[:], blk_d, wr=[blk_f])
    P.op("dve", lambda e: e.tensor_copy(blk_bf[:], blk_f[:]), rd=[blk_f], wr=[blk_bf])

    cast_i = [0]
    castbuf = {}

    def cast_dram(src, dst, R, C, lt_dst):
        for r0 in range(0, R, 128):
            for c0 in range(0, C, 2048):
                cw = min(2048, C - c0)
                a, b = castbuf["cin"].next(), castbuf["cout"].next()
                P.dma("sp", a[:, :cw], src[r0:r0 + 128, c0:c0 + cw], wr=[a])
                eng = ("dve", "act", "pool")[cast_i[0] % 3]
                cast_i[0] += 1
                if eng == "act":
                    P.op("act", lambda e: e.copy(b[:, :cw], a[:, :cw]), rd=[a], wr=[b])
                else:
                    P.op(eng, lambda e: e.tensor_copy(b[:, :cw], a[:, :cw]), rd=[a], wr=[b])
                P.dma("pool", dst[r0:r0 + 128, c0:c0 + cw], b[:, :cw], rd=[b], wr=[lt_dst])

    def cast_layer(l):
        mark = P.scope_begin()
        castbuf["cin"] = Rot([P.sb([128, 2048], F32, "cin") for _ in range(3)])
        castbuf["cout"] = Rot([P.sb([128, 2048], BF16, "cout") for _ in range(3)])
        cast_dram(w_in[l], WB_in[l], D, PW, L_WB[l])
        for br in range(3):
            cast_dram(p_abc[l, br], WB_p[l, br], 512, D, L_WB[l])
        cast_dram(w_out[l], WB_out[l], D, D, L_WB[l])
        cast_dram(w_mlp1[l], WB_1[l], D, DFF, L_WB[l])
        cast_dram(w_mlp2[l], WB_2[l], DFF, D, L_WB[l])
        P.scope_end(mark)

    def modulation():
        mark = P.scope_begin()
        cT = P.sb([128, KC, 3], F32, "cT")
        scT = P.sb([128, KC, 3], F32, "scT")
        for v in range(3):
            P.dma("sp", cT[:, :, v], cvec[v].rearrange("(k p) -> p k", p=128), wr=[cT], allow_slow_non_contiguous=True)
        P.op("act", lambda e: e.activation(out=scT[:], in_=cT[:], func=AF.Silu), rd=[cT], wr=[scT])
        wada_t = Rot([P.sb([128, KC, 512], F32, "wada") for _ in range(2)])
        bada_t = Rot([P.sb([3, 512], F32, "bada") for _ in range(2)])
        modrow = Rot([P.sb([3, 512], F32, "modrow") for _ in range(2)])
        for l in range(DEPTH):
            for g in range(12):
                wt, bt, mr = wada_t.next(), bada_t.next(), modrow.next()
                P.dma("sp", bt[:], b_ada[l:l + 1, g * 512:(g + 1) * 512].broadcast_to([3, 512]), wr=[bt])
                P.dma("sp", wt[:], w_ada[l, :, g * 512:(g + 1) * 512].rearrange("(k p) c -> p k c", p=128), wr=[wt])
                ps = psf.next()
                for k in range(KC):
                    P.op("pe", lambda e: e.matmul(ps[0:3, :], scT[:, k, :], wt[:, k, :], start=(k == 0), stop=(k == KC - 1)),
                         rd=[scT, wt], wr=[ps])
                P.op("dve", lambda e: e.tensor_tensor(mr[:], ps[0:3, :], bt[:], ALU.add), rd=[ps, bt], wr=[mr])
                P.dma("sp", MOD[l, :, g * 512:(g + 1) * 512], mr[:], rd=[mr], wr=[L_MOD])
        P.scope_end(mark)

    pc = P.sb([128, NPCOL], F32, "pcols")
    modc = P.sb([128, 48, 3], F32, "modc")
    g1c = P.sb([128, 3, KC], F32, "g1c")
    g2c = P.sb([128, 3, KC], F32, "g2c")
    mgrow = [[P.sb([128, D], F32, "mgrow") for w in range(2)] for v in range(3)]

    def pcol(nm, i=0, n=1):
        o, _ = PCOLS[nm]
        return pc[:, o + i:o + i + n]

    def layer_params(l):
        P.dma("sp", pc[:], pcols_d[l], wr=[pc])
        for v in range(3):
            P.dma("sp", modc[:, :, v], MOD[l, v].rearrange("(c p) -> p c", p=128), rd=[L_MOD], wr=[modc],
                  allow_slow_non_contiguous=True)
        for v in range(3):
            for w, mi in enumerate((2, 5)):
                P.dma("sp", mgrow[v][w][:], MOD[l, v:v + 1, mi * D:(mi + 1) * D].broadcast_to([128, D]),
                      rd=[L_MOD], wr=[mgrow[v][w]])
            for (gc, gname, mi) in ((g1c, "g1", 1), (g2c, "g2", 4)):
                P.op("dve", lambda e: e.scalar_tensor_tensor(gc[:, v, :], modc[:, mi * 8:(mi + 1) * 8, v], 1.0,
                                                              pcol(gname, 0, 8), ALU.add, ALU.mult),
                     rd=[modc, pc], wr=[gc])

    def x_src(l, s, e0, n):
        if e0 < CTX:
            return (ctx_in if l == 0 else xc)[s, e0:e0 + n, :]
        return (x_in if l == 0 else out)[s, e0 - CTX:e0 - CTX + n, :]

    def x_dst(s, e0, n):
        if e0 < CTX:
            return xc[s, e0:e0 + n, :]
        return out[s, e0 - CTX:e0 - CTX + n, :]

    ss_t = Rot([P.sb([128, 8], F32, "ss") for _ in range(2)])
    junk = P.sb([128, D], F32, "junk")

    epsc = P.sb([128, 4], F32, "epsc")
    P.op("dve", lambda e: e.memset(epsc[:, 0:1], NORM_EPS), wr=[epsc])
    P.op("dve", lambda e: e.memset(epsc[:, 1:2], LN_EPS), wr=[epsc])
    P.op("dve", lambda e: e.memset(epsc[:, 2:3], RW_GN_EPS), wr=[epsc])
    P.op("dve", lambda e: e.memset(epsc[:, 3:4], 1e-24), wr=[epsc])
    EPSI = {NORM_EPS: 0, LN_EPS: 1, RW_GN_EPS: 2, 1e-24: 3}

    def rsqrt(out_ap, in_ap, scale, eps, rd, wr):
        i = EPSI[eps]
        npart = out_ap.shape[0]
        P.op("act", lambda e: e.activation(out=out_ap, in_=in_ap, func=AF.Sqrt, scale=scale, bias=epsc[0:npart, i:i + 1]),
             rd=list(rd) + [epsc], wr=wr)
        P.op("dve", lambda e: e.reciprocal(out_ap, out_ap), rd=wr, wr=wr)

    def norm_T(xt, nsub, gcol, shcol, hT, xn):
        ss = ss_t.next()
        for sub in range(nsub):
            P.op("act", lambda e: e.activation(out=junk[:], in_=xt[:, sub, :], func=AF.Square,
                                               accum_out=ss[:, sub:sub + 1]), rd=[xt], wr=[junk, ss])
        rsqrt(ss[:, 4:4 + nsub], ss[:, 0:nsub], 1.0 / D, NORM_EPS, [ss], [ss])
        for sub in range(nsub):
            P.op("dve", lambda e: e.tensor_scalar(xn[:, sub, :], xt[:, sub, :], ss[:, 4 + sub:5 + sub], None, ALU.mult),
                 rd=[xt, ss], wr=[xn])
        for k in range(KC):
            pb = psb.next()
            for sub in range(nsub):
                P.op("pe", lambda e: e.transpose(pb[:, sub * 128:(sub + 1) * 128], xn[:, sub, k * 128:(k + 1) * 128],
                                                 ident_bf[:]), rd=[xn, ident_bf], wr=[pb])
            P.op("act", lambda e: e.activation(out=hT[:, k, 0:nsub * 128], in_=pb[:, 0:nsub * 128], func=AF.Identity,
                                               scale=gcol(k), bias=shcol(k)), rd=[pb, g1c, g2c, modc], wr=[hT])

    evac_i = [0]

    def evac(dst_ap, ps, src_ap, wr):
        evac_i[0] += 1
        if evac_i[0] % 2:
            P.op("act", lambda e: e.copy(dst_ap, src_ap), rd=[ps], wr=wr)
        else:
            P.op("dve", lambda e: e.tensor_copy(dst_ap, src_ap), rd=[ps], wr=wr)

    def phase_A(l):
        mark = P.scope_begin()
        wg_rot = Rot([P.sb([128, KC, 512], BF16, "wg") for _ in range(3)])
        ust_rot = Rot([P.sb([128, 4, 512], BF16, "ust") for _ in range(2)])
        xt_rot = Rot([P.sb([128, 4, D], F32, "xt") for _ in range(2)])
        xn = P.sb([128, 4, D], BF16, "xn")
        hT_rot = Rot([P.sb([128, KC, 512], BF16, "hT") for _ in range(2)])
        for s in range(NS):
            for ti, (e0, n) in enumerate(cfg.tiles):
                nsub = n // 128
                v = 2 if e0 < CTX else s
                xt = xt_rot.next()
                P.dma("sp", xt[:, 0:nsub, :], x_src(l, s, e0, n).rearrange("(a p) d -> p a d", p=128),
                      rd=[L_X[s][ti]], wr=[xt])
                hT = hT_rot.next()
                norm_T(xt, nsub, lambda k: g1c[:, v, k:k + 1], lambda k: modc[:, 0 * 8 + k, v:v + 1], hT, xn)
                for g in range(NCH // 4):
                    wg = wg_rot.next()
                    P.dma("sp" if g % 2 == 0 else "pool", wg[:],
                          WB_in[l, :, g * 512:(g + 1) * 512].rearrange("(k p) c -> p k c", p=128),
                          rd=[L_WB[l]], wr=[wg])
                    ust = ust_rot.next()
                    for c4 in range(4):
                        ps = psf.next()
                        for k in range(KC):
                            P.op("pe", lambda e: e.matmul(ps[:, 0:n], wg[:, k, c4 * 128:(c4 + 1) * 128], hT[:, k, 0:n],
                                                          start=(k == 0), stop=(k == KC - 1)), rd=[wg, hT], wr=[ps])
                        evac(ust[:, c4, 0:n], ps, ps[:, 0:n], [ust])
                    P.dma("sp", U[s, g * 4:(g + 1) * 4, :, e0:e0 + n].rearrange("c p t -> p c t"), ust[:, :, 0:n],
                          rd=[ust], wr=[L_U[s][ti]])
        P.scope_end(mark)

    def ln_stats_bcast(xb, sqb, lhsT_ones, nj, n, eps, tag):
        raise NotImplementedError

    def phase_C(l, last):
        mark = P.scope_begin()
        pw = P.sb([128, 3, 4, D], BF16, "pw")
        wo = P.sb([128, KC, D], BF16, "wo")
        for br in range(3):
            P.dma("sp", pw[:, br, :, :], WB_p[l, br].rearrange("(j p) d -> p j d", p=128), rd=[L_WB[l]], wr=[pw])
        P.dma("sp", wo[:], WB_out[l].rearrange("(k p) d -> p k d", p=128), rd=[L_WB[l]], wr=[wo])
        yt_rot = Rot([P.sb([128, 12, 256], BF16, "yt") for _ in range(1)])
        gt_rot = Rot([P.sb([128, 24, 256], BF16, "gt") for _ in range(1)])
        mT = P.sb([128, KC, 256], F32, "mT")
        mTb = P.sb([128, KC, 256], BF16, "mTb")
        tmpc = Rot([P.sb([128, 512], F32, "tmpc") for _ in range(2)])
        xt2 = Rot([P.sb([128, 2, D], F32, "xt2") for _ in range(1)])
        hid = P.sb([128, 32, 256], BF16, "hid")
        w1_rot = Rot([P.sb([128, KC, 512], BF16, "w1g") for _ in range(2)])
        w2_rot = Rot([P.sb([128, 8, D], BF16, "w2g") for _ in range(2)])
        xt_rot = Rot([P.sb([128, 2, D], F32, "xt") for _ in range(2)])
        xn = P.sb([128, 2, D], BF16, "xn")
        hT_rot = Rot([P.sb([128, KC, 256], BF16, "hT") for _ in range(1)])
        gfin = P.sb([128, D], F32, "gfin")
        if last:
            P.dma("sp", gfin[:], g_final.broadcast_to([128, D]), wr=[gfin])
        for s in range(NS):
            for ti, e0, n in [(ti, e0 + o, min(256, n - o)) for ti, (e0, n) in enumerate(cfg.tiles) for o in range(0, n, 256)]:
                if last and e0 < CTX:
                    continue
                nsub = n // 128
                v = 2 if e0 < CTX else s
                xt = xt_rot.next()
                P.dma("sp", xt[:, 0:nsub, :], x_src(l, s, e0, n).rearrange("(a p) d -> p a d", p=128),
                      rd=[L_X[s][ti]], wr=[xt])
                yt, gt = yt_rot.next(), gt_rot.next()
                P.dma("pool", yt[:, :, 0:n], YBR[s, :, :, :, e0:e0 + n].rearrange("b j p t -> p (b j) t"),
                      rd=L_Y[s][0][ti:ti + 1] + L_Y[s][1][ti:ti + 1] + L_Y[s][2][ti:ti + 1], wr=[yt])
                P.dma("pool", gt[:, :, 0:n], U[s, 40:64, :, e0:e0 + n].rearrange("c p t -> p c t"), rd=[L_U[s][ti]], wr=[gt])
                P.op("act", lambda e: e.activation(out=gt[:, :, 0:n], in_=gt[:, :, 0:n], func=AF.Sigmoid), rd=[gt], wr=[gt])
                for oc in range(KC):
                    for br in range(3):
                        ps = psf.next()
                        for j in range(4):
                            P.op("pe", lambda e: e.matmul(ps[:, 0:n], pw[:, br, j, oc * 128:(oc + 1) * 128], yt[:, br * 4 + j, 0:n],
                                                          start=(j == 0), stop=(j == 3)), rd=[pw, yt], wr=[ps])
                        if br == 0:
                            P.op("dve", lambda e: e.tensor_tensor(mT[:, oc, 0:n], ps[:, 0:n], gt[:, br * 8 + oc, 0:n], ALU.mult),
                                 rd=[ps, gt], wr=[mT])
                        else:
                            tc_ = tmpc.next()
                            P.op("dve", lambda e: e.tensor_tensor(tc_[:, 0:n], ps[:, 0:n], gt[:, br * 8 + oc, 0:n], ALU.mult),
                                 rd=[ps, gt], wr=[tc_])
                            P.op("pool", lambda e: e.tensor_tensor(mT[:, oc, 0:n], mT[:, oc, 0:n], tc_[:, 0:n], ALU.add),
                                 rd=[tc_, mT], wr=[mT])
                    P.op("act", lambda e: e.copy(mTb[:, oc, 0:n], mT[:, oc, 0:n]), rd=[mT], wr=[mTb])
                if cfg.c_stop <= 2:
                    continue
                x2 = xt2.next()
                for sub in range(nsub):
                    for half in range(2):
                        ps = psf.next()
                        for k in range(KC):
                            P.op("pe", lambda e: e.matmul(ps[:, :], mTb[:, k, sub * 128:(sub + 1) * 128],
                                                          wo[:, k, half * 512:(half + 1) * 512], start=(k == 0), stop=(k == KC - 1)),
                                 rd=[mTb, wo], wr=[ps])
                        tc_ = tmpc.next()
                        P.op("dve", lambda e: e.tensor_tensor(tc_[:], ps[:], mgrow[v][0][:, half * 512:(half + 1) * 512], ALU.mult),
                             rd=[ps, mgrow[v][0]], wr=[tc_])
                        P.op("pool", lambda e: e.tensor_tensor(x2[:, sub, half * 512:(half + 1) * 512], tc_[:],
                                                               xt[:, sub, half * 512:(half + 1) * 512], ALU.add),
                             rd=[tc_, xt], wr=[x2])
                if cfg.c_stop <= 3:
                    continue
                hT = hT_rot.next()
                norm_T(x2, nsub, lambda k: g2c[:, v, k:k + 1], lambda k: modc[:, 3 * 8 + k, v:v + 1], hT, xn)
                for g in range(8):
                    w1 = w1_rot.next()
                    P.dma("sp", w1[:], WB_1[l, :, g * 512:(g + 1) * 512].rearrange("(k p) c -> p k c", p=128),
                          rd=[L_WB[l]], wr=[w1])
                    for c4 in range(4):
                        hc = g * 4 + c4
                        ps = psf.next()
                        for k in range(KC):
                            P.op("pe", lambda e: e.matmul(ps[:, 0:n], w1[:, k, c4 * 128:(c4 + 1) * 128], hT[:, k, 0:n],
                                                          start=(k == 0), stop=(k == KC - 1)), rd=[w1, hT], wr=[ps])
                        tc_ = tmpc.next()
                        P.op("act", lambda e: e.activation(out=tc_[:, 0:n], in_=ps[:, 0:n], func=AF.Relu), rd=[ps], wr=[tc_])
                        P.op("dve" if hc % 2 else "pool", lambda e: e.tensor_tensor(hid[:, hc, 0:n], tc_[:, 0:n], tc_[:, 0:n], ALU.mult),
                             rd=[tc_], wr=[hid])
                if cfg.c_stop <= 4:
                    continue
                xo = xt_rot.next()
                pss = [[psf.next() for half in range(2)] for sub in range(2)]
                for sp in range(0, nsub, 2):
                    subs = list(range(sp, min(sp + 2, nsub)))
                    for g in range(4):
                        w2 = w2_rot.next()
                        P.dma("sp", w2[:], WB_2[l, g * 1024:(g + 1) * 1024, :].rearrange("(k p) d -> p k d", p=128),
                              rd=[L_WB[l]], wr=[w2])
                        for kk_ in range(8):
                            hc = g * 8 + kk_
                            for sub in subs:
                                for half in range(2):
                                    ps = pss[sub - sp][half]
                                    P.op("pe", lambda e: e.matmul(ps[:, :], hid[:, hc, sub * 128:(sub + 1) * 128],
                                                                  w2[:, kk_, half * 512:(half + 1) * 512],
                                                                  start=(hc == 0), stop=(hc == 31)), rd=[hid, w2], wr=[ps])
                    for sub in subs:
                        for half in range(2):
                            ps = pss[sub - sp][half]
                            tc_ = tmpc.next()
                            P.op("dve", lambda e: e.tensor_tensor(tc_[:], ps[:], mgrow[v][1][:, half * 512:(half + 1) * 512], ALU.mult),
                                 rd=[ps, mgrow[v][1]], wr=[tc_])
                            P.op("pool", lambda e: e.tensor_tensor(xo[:, sub, half * 512:(half + 1) * 512], tc_[:],
                                                                   x2[:, sub, half * 512:(half + 1) * 512], ALU.add),
                                 rd=[tc_, x2], wr=[xo])
                if cfg.c_stop <= 5:
                    continue
                if last:
                    ss = ss_t.next()
                    for sub in range(nsub):
                        P.op("act", lambda e: e.activation(out=junk[:], in_=xo[:, sub, :], func=AF.Square,
                                                           accum_out=ss[:, sub:sub + 1]), rd=[xo], wr=[junk, ss])
                    rsqrt(ss[:, 4:4 + nsub], ss[:, 0:nsub], 1.0 / D, NORM_EPS, [ss], [ss])
                    for sub in range(nsub):
                        P.op("dve", lambda e: e.scalar_tensor_tensor(xo[:, sub, :], xo[:, sub, :], ss[:, 4 + sub:5 + sub], gfin[:],
                                                                      ALU.mult, ALU.mult), rd=[xo, ss, gfin], wr=[xo])
                if cfg.c_stop <= 6:
                    continue
                for sub in range(nsub):
                    P.dma("sp", x_dst(s, e0 + sub * 128, 128), xo[:, sub, :], rd=[xo], wr=[L_X[s][ti]])
        P.scope_end(mark)

    def stats_rstd(x_f32, nj, n, ones_l, eps, inv_n, scr):
        lhs, full = ones_l
        xb, sq = scr["xb"], scr["sq"]
        for j in range(nj):
            P.op("act", lambda e: e.copy(xb[:, j, 0:n], x_f32(j)), rd=scr["rd"], wr=[xb])
            P.op("act", lambda e: e.activation(out=sq[:, j, 0:n], in_=x_f32(j), func=AF.Square), rd=scr["rd"], wr=[sq])
        groups = [list(range(nj))] if full else [[j] for j in range(nj)]
        for gi, g in enumerate(groups):
            ps1, ps2 = psf.next(), psf.next()
            for a, j in enumerate(g):
                P.op("pe", lambda e: e.matmul(ps1[:, 0:n], lhs[:], xb[:, j, 0:n], start=(a == 0), stop=(a == len(g) - 1)),
                     rd=[lhs, xb], wr=[ps1])
            for a, j in enumerate(g):
                P.op("pe", lambda e: e.matmul(ps2[:, 0:n], lhs[:], sq[:, j, 0:n], start=(a == 0), stop=(a == len(g) - 1)),
                     rd=[lhs, sq], wr=[ps2])
            mean, rstd = scr["mean"], scr["rstd"]
            P.op("act", lambda e: e.mul(mean[:, gi, 0:n], ps1[:, 0:n], inv_n), rd=[ps1], wr=[mean])
            P.op("dve", lambda e: e.tensor_tensor(rstd[:, gi, 0:n], mean[:, gi, 0:n], mean[:, gi, 0:n], ALU.mult),
                 rd=[mean], wr=[rstd])
            P.op("dve", lambda e: e.scalar_tensor_tensor(rstd[:, gi, 0:n], ps2[:, 0:n], inv_n, rstd[:, gi, 0:n],
                                                          ALU.mult, ALU.subtract), rd=[ps2, rstd], wr=[rstd])
            rsqrt(rstd[:, gi, 0:n], rstd[:, gi, 0:n], 1.0, eps, [rstd], [rstd])

    def conv_branch(l, last):
        mark = P.scope_begin()
        ug = Rot([P.sb([128, 8, 512], BF16, "ug") for _ in range(2)])
        zp = P.sb([128, 4, 8 * 94], F32, "zp")
        zc = P.sb([128, 4, 542], F32, "zc")
        sg = P.sb([128, 4, 512], F32, "sg")
        acc = P.sb([128, 4, 512], F32, "acc")
        scr = {"xb": P.sb([128, 4, 512], BF16, "xb"), "sq": P.sb([128, 4, 512], BF16, "sq"),
               "mean": P.sb([128, 1, 512], F32, "mean"), "rstd": P.sb([128, 1, 512], F32, "rstd"), "rd": [acc]}
        yo = Rot([P.sb([128, 4, 512], BF16, "yo") for _ in range(2)])
        P.op("dve", lambda e: e.memset(zp[:], 0.0), wr=[zp])
        P.op("dve", lambda e: e.memset(zc[:], 0.0), wr=[zc])
        o_dw = PCOLS["cdw"][0]
        for s in range(NS):
            for ti, (e0, n) in enumerate(cfg.tiles):
                isctx = e0 < CTX
                if last and isctx:
                    continue
                u = ug.next()
                P.dma("sp", u[:, :, 0:n], U[s, 15:23, :, e0:e0 + n].rearrange("c p t -> p c t"), rd=[L_U[s][ti]], wr=[u])
                P.op("act", lambda e: e.activation(out=sg[:, :, 0:n], in_=u[:, 4:8, 0:n], func=AF.Sigmoid), rd=[u], wr=[sg])
                if isctx:
                    assert CTX <= 512 and n == CTX
                    zbuf = zc
                    zin = zc[:, :, 15:15 + n]
                    P.op("dve", lambda e: e.tensor_tensor(zin, u[:, 0:4, 0:n], sg[:, :, 0:n], ALU.mult), rd=[u, sg], wr=[zc])
                    win = lambda j, tau: zc[:, j, tau:tau + n]
                    av = lambda j: acc[:, j, 0:n]
                else:
                    nr = n // 64
                    zbuf = zp
                    zin = zp[:, :, 0:nr * 94].rearrange("p j (r w) -> p j r w", w=94)[:, :, :, 15:79]
                    P.op("dve", lambda e: e.tensor_tensor(zin, u[:, 0:4, 0:n].rearrange("p j (r w) -> p j r w", w=64),
                                                           sg[:, :, 0:n].rearrange("p j (r w) -> p j r w", w=64), ALU.mult),
                         rd=[u, sg], wr=[zp])
                    win = lambda j, tau: zp[:, j, 0:nr * 94].rearrange("p (r w) -> p r w", w=94)[:, :, tau:tau + 64]
                    av = lambda j: acc[:, j, 0:n].rearrange("p (r w) -> p r w", w=64)
                for j in range(4):
                    eng = "dve"
                    P.op(eng, lambda e: e.tensor_scalar(av(j), win(j, 0), pc[:, o_dw + j * 31:o_dw + j * 31 + 1],
                                                        pcol("cdb", j), ALU.mult, ALU.add), rd=[zbuf, pc], wr=[acc])
                    for tau in range(1, 31):
                        P.op(eng, lambda e: e.scalar_tensor_tensor(av(j), win(j, tau),
                                                                   pc[:, o_dw + j * 31 + tau:o_dw + j * 31 + tau + 1],
                                                                   av(j), ALU.mult, ALU.add), rd=[zbuf, pc, acc], wr=[acc])
                stats_rstd(lambda j: acc[:, j, 0:n], 4, n, (ones_bf, True), LN_EPS, 1.0 / 512, scr)
                y = yo.next()
                for j in range(4):
                    P.op("dve", lambda e: e.tensor_tensor(acc[:, j, 0:n], acc[:, j, 0:n], scr["mean"][:, 0, 0:n], ALU.subtract),
                         rd=[acc, scr["mean"]], wr=[acc])
                    P.op("dve", lambda e: e.tensor_tensor(acc[:, j, 0:n], acc[:, j, 0:n], scr["rstd"][:, 0, 0:n], ALU.mult),
                         rd=[acc, scr["rstd"]], wr=[acc])
                    P.op("act", lambda e: e.activation(out=y[:, j, 0:n], in_=acc[:, j, 0:n], func=AF.Silu,
                                                       scale=pcol("clnw", j), bias=pcol("clnb", j)), rd=[acc, pc], wr=[y])
                P.dma("sp", YBR[s, 1, :, :, e0:e0 + n].rearrange("j p t -> p j t"), y[:, :, 0:n], rd=[y], wr=[L_Y[s][1][ti]])
        P.scope_end(mark)

    HM = dscr("HM", [NS, 2, TC, 512], F32)
    L_HM = [[LT() for d in range(2)] for s in range(NS)]
    selh_d = din("selh", [4, 4 * 128])
    NCK = TC // 64
    DH5 = 128 ** -0.5

    def mlstm_branch(l, last):
        mark = P.scope_begin()
        selh = P.sb([4, 4 * 128], F32, "selh")
        P.dma("sp", selh[:], selh_d, wr=[selh])
        mb = P.sb([4, 4], F32, "mb")
        for d in range(2):
            P.dma("sp", mb[:, 2 * d:2 * d + 2], mlb[l, d * 4:(d + 1) * 4, :], wr=[mb])
        mk = P.sb([64, 2, 64], F32, "mk")
        P.dma("sp", mk[:, 0, :], masks_d[1], wr=[mk])
        P.dma("sp", mk[:, 1, :], masks_d[3], wr=[mk])
        P.op("dve", lambda e: e.tensor_scalar(mk[:], mk[:], DH5, None, ALU.mult), rd=[mk], wr=[mk])
        PAD = 64
        gb = P.sb([4, 2, TC], BF16, "gb")
        IG = P.sb([4, TC], F32, "IG")
        Bc = P.sb([4, TC], F32, "Bc")
        G = P.sb([4, TC], F32, "G")
        MU = P.sb([4, TC + 2 * PAD], F32, "MU")
        Q2_ = P.sb([4, TC], F32, "Q2")
        Q = [G, IG, Q2_, Bc]
        CAR = P.sb([4, NCK], F32, "CAR")
        carb = P.sb([128, 4, NCK], F32, "carb")
        qk_rot = Rot([P.sb([128, 8, 512], BF16, "qk") for _ in range(1)])
        vk_rot = Rot([P.sb([128, 4, 512], BF16, "vk") for _ in range(2)])
        vtm = P.sb([64, 8, 4, 130], BF16, "vtm")
        ktm = P.sb([64, 8, 4, 128], BF16, "ktm")
        cols = Rot([P.sb([64, 16], F32, "cols") for _ in range(2)])
        Sb = Rot([P.sb([64, 64], BF16, "Sb") for _ in range(2)])
        Vp = Rot([P.sb([64, 130], BF16, "Vp") for _ in range(2)])
        tmpn = Rot([P.sb([64, 130], F32, "tmpn") for _ in range(2)])
        ne = Rot([P.sb([64, 130], F32, "ne") for _ in range(2)])
        dn = Rot([P.sb([64, 2], F32, "dn") for _ in range(2)])
        hst = Rot([P.sb([64, 8, 512], F32, "hst") for _ in range(1)])
        Cf = P.sb([128, 4, 130], F32, "Cf")
        Cb = P.sb([128, 4, 130], BF16, "Cb")
        Sb4 = P.sb([64, 4, 64], BF16, "Sb4")
        tn4 = P.sb([64, 4, 130], F32, "tn4")
        nn4 = P.sb([64, 4, 130], F32, "nn4")
        dd4 = P.sb([64, 2, 4], F32, "dd4")
        vp4 = P.sb([64, 4, 130], BF16, "vp4")
        P.op("dve", lambda e: e.memset(vtm[:], 1.0), wr=[vtm])
        for s in range(NS):
            for d in range(2):
                def seg(e_lo, e_hi):
                    if d == 0:
                        return slice(e_lo, e_hi)
                    return slice(e_lo - CTX, e_hi - CTX) if e_lo >= CTX else slice(T + e_lo, T + e_hi)
                for (lo, hi) in ((0, CTX), (CTX, TC)):
                    for gi in range(2):
                        P.dma("sp", gb[:, gi, seg(lo, hi)], U[s, 39, d * 8 + gi * 4:d * 8 + gi * 4 + 4, lo:hi],
                              rd=[t_ for t_ in L_U[s]], wr=[gb])
                P.op("act", lambda e: e.activation(out=IG[:], in_=gb[:, 0, :], func=AF.Identity, bias=mb[:, 2 * d:2 * d + 1]),
                     rd=[gb, mb], wr=[IG])
                P.op("act", lambda e: e.activation(out=Bc[:], in_=gb[:, 1, :], func=AF.Sigmoid, bias=mb[:, 2 * d + 1:2 * d + 2]),
                     rd=[gb, mb], wr=[Bc])
                P.op("act", lambda e: e.activation(out=Bc[:], in_=Bc[:], func=AF.Ln), rd=[Bc], wr=[Bc])
                rv = (lambda ap: ap) if d == 0 else (lambda ap: ap[:, ::-1])
                P.op("dve", lambda e: e.memset(MU[:], 1.0), wr=[MU])
                P.op("dve", lambda e: e.tensor_tensor_scan(rv(G[:]), rv(MU[:, 0:TC]), rv(Bc[:]), 0.0, ALU.mult, ALU.add),
                     rd=[MU, Bc], wr=[G])
                P.op("dve", lambda e: e.memset(MU[:], 0.0), rd=[G], wr=[MU])
                P.op("dve", lambda e: e.tensor_copy(Bc[:], G[:]), rd=[G], wr=[Bc])
                P.op("dve", lambda e: e.tensor_tensor(G[:], IG[:], Bc[:], ALU.subtract), rd=[IG, Bc], wr=[G])
                mu = MU[:, PAD:PAD + TC]
                P.op("dve", lambda e: e.tensor_tensor_scan(rv(mu), rv(G[:]), rv(G[:]), 0.0, ALU.max, ALU.max),
                     rd=[G], wr=[MU])
                mu3 = mu.rearrange("p (c t) -> p c t", t=64)
                if d == 0:
                    mu_last = mu3[:, :, 63:64]
                    mu_ent = MU[:, PAD - 1:PAD - 1 + TC].rearrange("p (c t) -> p c t", t=64)[:, :, 0:1]
                else:
                    mu_last = mu3[:, :, 0:1]
                    mu_ent = MU[:, PAD + 64:PAD + 64 + TC].rearrange("p (c t) -> p c t", t=64)[:, :, 0:1]
                v3 = lambda b: b[:].rearrange("p (c t) -> p c t", t=64)
                bc3 = lambda ap: ap.broadcast_to([4, NCK, 64])
                P.op("dve", lambda e: e.tensor_tensor(v3(Q[0]), v3(G), bc3(mu_last), ALU.subtract), rd=[G, MU], wr=[Q[0]])
                P.op("act", lambda e: e.activation(out=Q[0][:], in_=Q[0][:], func=AF.Exp), rd=[Q[0]], wr=[Q[0]])
                P.op("dve", lambda e: e.tensor_tensor(v3(Q[1]), bc3(mu_last), mu3, ALU.subtract), rd=[MU], wr=[Q[1]])
                P.op("act", lambda e: e.activation(out=Q[1][:], in_=Q[1][:], func=AF.Exp), rd=[Q[1]], wr=[Q[1]])
                P.op("dve", lambda e: e.tensor_tensor(v3(Q[2]), bc3(mu_ent), mu3, ALU.subtract), rd=[MU], wr=[Q[2]])
                P.op("act", lambda e: e.activation(out=Q[2][:], in_=Q[2][:], func=AF.Exp), rd=[Q[2]], wr=[Q[2]])
                P.op("dve", lambda e: e.tensor_tensor(Q[3][:], Bc[:], mu, ALU.add), rd=[Bc, MU], wr=[Q[3]])
                P.op("act", lambda e: e.activation(out=Q[3][:], in_=Q[3][:], func=AF.Exp, scale=-1.0), rd=[Q[3]], wr=[Q[3]])
                P.op("dve", lambda e: e.tensor_tensor(CAR[:].unsqueeze(2), mu_ent, mu_last, ALU.subtract), rd=[MU], wr=[CAR])
                P.op("act", lambda e: e.activation(out=CAR[:], in_=CAR[:], func=AF.Exp), rd=[CAR], wr=[CAR])
                for h in range(4):
                    ps = psf.next()
                    P.op("pe", lambda e: e.matmul(ps[:, 0:NCK], selh[:, h * 128:(h + 1) * 128], CAR[:], start=True, stop=True),
                         rd=[selh, CAR], wr=[ps])
                    P.op("dve", lambda e: e.tensor_copy(carb[:, h, :], ps[:, 0:NCK]), rd=[ps], wr=[carb])
                P.op("dve", lambda e: e.memset(Cf[:], 0.0), wr=[Cf])
                P.op("dve", lambda e: e.memset(Cb[:], 0.0), wr=[Cb])
                order = list(range(len(cfg.tiles)))
                nctx_t = sum(1 for (e0, n) in cfg.tiles if e0 < CTX)
                if d == 1:
                    order = list(range(nctx_t - 1, -1, -1)) + list(range(len(cfg.tiles) - 1, nctx_t - 1, -1))
                for ti in order:
                    e0, n = cfg.tiles[ti]
                    nck = n // 64
                    qk, vk = qk_rot.next(), vk_rot.next()
                    P.dma("sp", qk[:, :, 0:n], U[s, 23:31, :, e0:e0 + n].rearrange("c p t -> p c t"), rd=[L_U[s][ti]], wr=[qk])
                    P.dma("pool", vk[:, 0:4, 0:n], U[s, 31:35, :, e0:e0 + n].rearrange("c p t -> p c t"), rd=[L_U[s][ti]], wr=[vk])
                    for ck in range(nck):
                        for (src, srcbuf, dst, w) in ((lambda h: vk[:, h, ck * 64:(ck + 1) * 64], vk, vtm, 130),
                                                      (lambda h: qk[:, 4 + h, ck * 64:(ck + 1) * 64], qk, ktm, 128)):
                            pb = psb.next()
                            for h in range(4):
                                P.op("pe", lambda e: e.transpose(pb[0:64, h * 128:(h + 1) * 128], src(h), ident_bf[:]),
                                     rd=[srcbuf, ident_bf], wr=[pb])
                            P.op("act", lambda e: e.copy(dst[:, ck, :, 0:128], pb[0:64, 0:512].rearrange("p (h c) -> p h c", c=128)),
                                 rd=[pb], wr=[dst])
                    hs = hst.next()
                    ckorder = range(nck) if d == 0 else range(nck - 1, -1, -1)
                    for ck in ckorder:
                        ec = (e0 + ck * 64)
                        mc = seg(ec, ec + 64)
                        cidx = mc.start // 64
                        cl = cols.next()
                        psq = psf.next()
                        for q in range(4):
                            P.op("pe", lambda e: e.matmul(psq[0:64, q * 4:(q + 1) * 4], Q[q][:, mc], ident_f[0:4, 0:4],
                                                          start=True, stop=True), rd=[Q[q], ident_f], wr=[psq])
                        P.op("dve", lambda e: e.tensor_copy(cl[:], psq[0:64, 0:16]), rd=[psq], wr=[cl])
                        colq = lambda q, h: cl[:, q * 4 + h:q * 4 + h + 1]
                        H4 = range(4)
                        reg = lambda pp, h: pp[h // 2][0:64, (h % 2) * 130:(h % 2) * 130 + 129]
                        for h in H4:
                            P.op("dve", lambda e: e.tensor_scalar(vp4[:, h, 0:129], vtm[:, ck, h, 0:129], colq(0, h), DH5, ALU.mult, ALU.mult),
                                 rd=[vtm, cl], wr=[vp4])
                        psS = psf.next()
                        for h in H4:
                            P.op("pe", lambda e: e.matmul(psS[0:64, h * 64:(h + 1) * 64], qk[:, 4 + h, ck * 64:(ck + 1) * 64],
                                                          qk[:, h, ck * 64:(ck + 1) * 64], start=True, stop=True), rd=[qk], wr=[psS])
                        psC = [psf.next(), psf.next()]
                        for h in H4:
                            P.op("pe", lambda e: e.matmul(reg(psC, h), qk[:, h, ck * 64:(ck + 1) * 64], Cb[:, h, 0:129],
                                                          start=True, stop=True), rd=[qk, Cb], wr=[psC[h // 2]])
                        psU = [psf.next(), psf.next()]
                        regU = lambda h: psU[h // 2][:, (h % 2) * 130:(h % 2) * 130 + 129]
                        for h in H4:
                            P.op("pe", lambda e: e.matmul(regU(h), ktm[:, ck, h, :], vp4[:, h, 0:129], start=True, stop=True),
                                 rd=[ktm, vp4], wr=[psU[h // 2]])
                        for h in H4:
                            P.op("dve", lambda e: e.scalar_tensor_tensor(Sb4[:, h, :], psS[0:64, h * 64:(h + 1) * 64], colq(0, h), mk[:, d, :],
                                                                          ALU.mult, ALU.mult), rd=[psS, cl, mk], wr=[Sb4])
                        for h in H4:
                            P.op("act", lambda e: e.activation(out=tn4[:, h, 0:129], in_=reg(psC, h), func=AF.Copy, scale=colq(2, h)),
                                 rd=[psC[h // 2], cl], wr=[tn4])
                        for h in H4:
                            P.op("dve", lambda e: e.scalar_tensor_tensor(Cf[:, h, 0:129], Cf[:, h, 0:129],
                                                                          carb[:, h, cidx:cidx + 1], regU(h),
                                                                          ALU.mult, ALU.add), rd=[Cf, carb, psU[h // 2]], wr=[Cf])
                        P.op("act", lambda e: e.copy(Cb[:], Cf[:]), rd=[Cf], wr=[Cb])
                        psI = [psf.next(), psf.next()]
                        for h in H4:
                            P.op("pe", lambda e: e.matmul(reg(psI, h), Sb4[:, h, :], vtm[:, ck, h, 0:129], start=True, stop=True),
                                 rd=[Sb4, vtm], wr=[psI[h // 2]])
                        for h in H4:
                            P.op("dve", lambda e: e.scalar_tensor_tensor(nn4[:, h, 0:129], reg(psI, h), colq(1, h), tn4[:, h, 0:129],
                                                                          ALU.mult, ALU.add), rd=[psI[h // 2], cl, tn4], wr=[nn4])
                        P.op("dve", lambda e: e.scalar_tensor_tensor(dd4[:, 0, :], nn4[:, :, 128], -1.0, nn4[:, :, 128],
                                                                      ALU.mult, ALU.max), rd=[nn4], wr=[dd4])
                        P.op("dve", lambda e: e.tensor_tensor(dd4[:, 0, :], dd4[:, 0, :], cl[:, 12:16], ALU.max), rd=[dd4, cl], wr=[dd4])
                        P.op("dve", lambda e: e.reciprocal(dd4[:, 1, :], dd4[:, 0, :]), rd=[dd4], wr=[dd4])
                        for h in H4:
                            P.op("act", lambda e: e.activation(out=hs[:, ck, h * 128:(h + 1) * 128], in_=nn4[:, h, 0:128],
                                                               func=AF.Copy, scale=dd4[:, 1, h:h + 1]), rd=[nn4, dd4], wr=[hs])
                    P.dma("sp", HM[s, d, e0:e0 + n, :].rearrange("(c t) f -> t c f", t=64), hs[:, 0:nck, :], rd=[hs], wr=[L_HM[s][d]])
        P.scope_end(mark)
        mark = P.scope_begin()
        hf = Rot([P.sb([128, 512], F32, "hf") for _ in range(2)])
        hb = Rot([P.sb([128, 512], F32, "hb") for _ in range(2)])
        hnb = Rot([P.sb([128, 512], BF16, "hnb") for _ in range(2)])
        st = Rot([P.sb([128, 16], F32, "st") for _ in range(2)])
        og = Rot([P.sb([128, 4, 512], BF16, "og") for _ in range(2)])
        yc = Rot([P.sb([128, 4, 512], BF16, "yc") for _ in range(2)])
        for s in range(NS):
            for ti, (e0, n) in enumerate(cfg.tiles):
                if last and e0 < CTX:
                    continue
                o_ = og.next()
                P.dma("sp", o_[:, :, 0:n], U[s, 35:39, :, e0:e0 + n].rearrange("c p t -> p c t"), rd=[L_U[s][ti]], wr=[o_])
                P.op("act", lambda e: e.activation(out=o_[:, :, 0:n], in_=o_[:, :, 0:n], func=AF.Sigmoid), rd=[o_], wr=[o_])
                y = yc.next()
                for sub in range(n // 128):
                    a, b, hn, t_ = hf.next(), hb.next(), hnb.next(), st.next()
                    r0 = e0 + sub * 128
                    P.dma("sp", a[:], HM[s, 0, r0:r0 + 128, :], rd=[L_HM[s][0]], wr=[a])
                    P.dma("pool", b[:], HM[s, 1, r0:r0 + 128, :], rd=[L_HM[s][1]], wr=[b])
                    P.op("dve", lambda e: e.tensor_tensor(a[:], a[:], b[:], ALU.add), rd=[a, b], wr=[a])
                    P.op("dve", lambda e: e.reduce_sum(t_[:, 0:4], a[:].rearrange("p (h c) -> p h c", c=128), AX.X), rd=[a], wr=[t_])
                    for h in range(4):
                        P.op("act", lambda e: e.activation(out=b[:, h * 128:(h + 1) * 128], in_=a[:, h * 128:(h + 1) * 128],
                                                           func=AF.Square, accum_out=t_[:, 4 + h:5 + h]), rd=[a], wr=[b, t_])
                    P.op("dve", lambda e: e.tensor_scalar(t_[:, 0:4], t_[:, 0:4], 1.0 / 128, None, ALU.mult), rd=[t_], wr=[t_])
                    P.op("dve", lambda e: e.tensor_tensor(t_[:, 8:12], t_[:, 0:4], t_[:, 0:4], ALU.mult), rd=[t_], wr=[t_])
                    P.op("dve", lambda e: e.scalar_tensor_tensor(t_[:, 8:12], t_[:, 4:8], 1.0 / 128, t_[:, 8:12], ALU.mult, ALU.subtract),
                         rd=[t_], wr=[t_])
                    rsqrt(t_[:, 12:16], t_[:, 8:12], 1.0, LN_EPS, [t_], [t_])
                    for h in range(4):
                        P.op("dve", lambda e: e.tensor_scalar(hn[:, h * 128:(h + 1) * 128], a[:, h * 128:(h + 1) * 128],
                                                              t_[:, h:h + 1], t_[:, 12 + h:13 + h], ALU.subtract, ALU.mult),
                             rd=[a, t_], wr=[hn])
                    pb = psb.next()
                    for h in range(4):
                        P.op("pe", lambda e: e.transpose(pb[:, h * 128:(h + 1) * 128], hn[:, h * 128:(h + 1) * 128], ident_bf[:]),
                             rd=[hn, ident_bf], wr=[pb])
                    for h in range(4):
                        P.op("dve", lambda e: e.scalar_tensor_tensor(y[:, h, sub * 128:(sub + 1) * 128], pb[:, h * 128:(h + 1) * 128],
                                                                      pcol("mnw", h), o_[:, h, sub * 128:(sub + 1) * 128],
                                                                      ALU.mult, ALU.mult), rd=[pb, pc, o_], wr=[y])
                P.dma("sp", YBR[s, 2, :, :, e0:e0 + n].rearrange("j p t -> p j t"), y[:, :, 0:n], rd=[y], wr=[L_Y[s][2][ti]])
        P.scope_end(mark)

    RF = dscr("RF", [NS, 2, 4, 128, NCK, 4, 64], BF16)
    PLs = dscr("PLs", [NS, 2, 128, 4, NCK], F32)
    VT = dscr("VT", [NS, TC, 512], BF16)
    GB = dscr("GB", [NS, 2, 4, 128, TC], BF16)
    YR = dscr("YR", [NS, 2, 4, 128, TC], F32)
    L_RF = [[[LT() for _ in cfg.tiles] for d in range(2)] for s in range(NS)]
    L_VT = [[LT() for _ in cfg.tiles] for s in range(NS)]
    L_GB = [[LT() for _ in cfg.tiles] for s in range(NS)]
    L_YR = [[[LT() for _ in cfg.tiles] for d in range(2)] for s in range(NS)]
    C0 = float(np.exp(-0.5))

    def rwkv_branch(l, last):
        mark = P.scope_begin()
        wtmp = P.sb([128, 512], F32, "wtmp")
        w2p = P.sb([128, 2, 512], BF16, "w2p")
        a2p = P.sb([128, 2, 512], BF16, "a2p")
        g2b = P.sb([128, 512], BF16, "g2b")
        P.op("dve", lambda e: e.memset(w2p[:], 0.0), wr=[w2p])
        P.op("dve", lambda e: e.memset(a2p[:], 0.0), wr=[a2p])
        for (src, dst) in ((rw_w2, w2p), (rw_a2, a2p)):
            P.dma("sp", wtmp[:], src[l], wr=[wtmp])
            for d in range(2):
                P.op("dve", lambda e: e.tensor_copy(dst[d * 64:(d + 1) * 64, d, :], wtmp[d * 64:(d + 1) * 64, :]), rd=[wtmp], wr=[dst])
        P.dma("sp", wtmp[:], rw_g2[l], wr=[wtmp])
        P.op("dve", lambda e: e.tensor_copy(g2b[:], wtmp[:]), rd=[wtmp], wr=[g2b])
        coef0 = P.sb([128, 15], F32, "coef0")
        omka = P.sb([128, 4], F32, "omka")
        P.op("dve", lambda e: e.tensor_tensor(coef0[:], pcol("mu_p", 0, 15), pcol("mu_n", 0, 15), ALU.add), rd=[pc], wr=[coef0])
        P.op("dve", lambda e: e.tensor_scalar(coef0[:], coef0[:], -1.0, 1.0, ALU.mult, ALU.add), rd=[coef0], wr=[coef0])
        P.op("dve", lambda e: e.tensor_scalar(omka[:], pcol("ka", 0, 4), -1.0, 1.0, ALU.mult, ALU.add), rd=[pc], wr=[omka])
        rmask = P.sb([128, 2, 512], F32, "rmask")
        P.op("dve", lambda e: e.memset(rmask[:], 1.0), wr=[rmask])
        P.op("dve", lambda e: e.memset(rmask[:, 0, :].rearrange("p (c t) -> p c t", t=64)[:, :, 0:1], 0.0), wr=[rmask])
        P.op("dve", lambda e: e.memset(rmask[:, 1, :].rearrange("p (c t) -> p c t", t=64)[:, :, 63:64], 0.0), wr=[rmask])

        mark1 = P.scope_begin()
        ush = P.sb([128, 15, 514], BF16, "ush")
        xs = P.sb([128, 15, 512], F32, "xs")
        tmp = P.sb([128, 15, 512], F32, "tmp")
        sig = [P.sb([128, 4, 512], F32, "sig")] * 2
        aa = [P.sb([128, 4, 512], F32, "aa")] * 2
        gate = P.sb([128, 4, 512], F32, "gate")
        kkn = P.sb([128, 4, 512], F32, "kkn")
        kd = P.sb([128, 4, 512], F32, "kd")
        rf = P.sb([128, 4, 8, 4, 64], BF16, "rf")
        lb = P.sb([128, 3, 512], BF16, "lb")
        b4 = P.sb([128, 4, 512], BF16, "b4")
        gbt = P.sb([128, 2, 4, 512], BF16, "gbt")
        vtm = P.sb([128, 4, 512], BF16, "vtmr")
        plt = P.sb([128, 4, 8], F32, "plt")

        def bc(col_ap, nj, n):
            return col_ap.unsqueeze(2).broadcast_to([128, nj, n])

        for s in range(NS):
            for ti, (e0, n) in enumerate(cfg.tiles):
                nck = n // 64
                lo_edge = (e0 == 0 or e0 == CTX)
                hi_edge = (e0 + n == CTX or e0 + n == TC)
                P.dma("sp", ush[:, :, 1:n + 1], U[s, 0:15, :, e0:e0 + n].rearrange("c p t -> p c t"), rd=[L_U[s][ti]], wr=[ush])
                if lo_edge:
                    P.op("dve", lambda e: e.memset(ush[:, :, 0:1], 0.0), wr=[ush])
                else:
                    P.dma("sp", ush[:, :, 0:1], U[s, 0:15, :, e0 - 1:e0].rearrange("c p t -> p c t"), rd=[L_U[s][ti - 1]], wr=[ush],
                          allow_slow_non_contiguous=True)
                if hi_edge:
                    P.op("dve", lambda e: e.memset(ush[:, :, n + 1:n + 2], 0.0), wr=[ush])
                else:
                    P.dma("sp", ush[:, :, n + 1:n + 2], U[s, 0:15, :, e0 + n:e0 + n + 1].rearrange("c p t -> p c t"),
                          rd=[L_U[s][ti + 1]], wr=[ush], allow_slow_non_contiguous=True)
                X, Tm = xs[:, :, 0:n], tmp[:, :, 0:n]
                P.op("dve", lambda e: e.tensor_tensor(X, ush[:, :, 1:n + 1], bc(coef0[:], 15, n), ALU.mult), rd=[ush, coef0], wr=[xs])
                P.op("dve", lambda e: e.tensor_tensor(Tm, ush[:, :, 0:n], bc(pcol("mu_p", 0, 15), 15, n), ALU.mult), rd=[ush, pc], wr=[tmp])
                P.op("dve", lambda e: e.tensor_tensor(X, X, Tm, ALU.add), rd=[xs, tmp], wr=[xs])
                P.op("dve", lambda e: e.tensor_tensor(Tm, ush[:, :, 2:n + 2], bc(pcol("mu_n", 0, 15), 15, n), ALU.mult), rd=[ush, pc], wr=[tmp])
                P.op("dve", lambda e: e.tensor_tensor(X, X, Tm, ALU.add), rd=[xs, tmp], wr=[xs])
                r_, k_, v_ = xs[:, 0:4, 0:n], xs[:, 4:8, 0:n], xs[:, 8:12, 0:n]
                T0, T1, T2 = tmp[:, 0:4, 0:n], tmp[:, 4:8, 0:n], tmp[:, 8:12, 0:n]
                P.op("act", lambda e: e.activation(out=lb[:, 0, 0:n], in_=xs[:, 12, 0:n], func=AF.Tanh), rd=[xs], wr=[lb])
                P.op("act", lambda e: e.copy(lb[:, 1, 0:n], xs[:, 13, 0:n]), rd=[xs], wr=[lb])
                P.op("act", lambda e: e.activation(out=lb[:, 2, 0:n], in_=xs[:, 14, 0:n], func=AF.Sigmoid), rd=[xs], wr=[lb])
                for j in range(4):
                    ps = psf.next()
                    P.op("pe", lambda e: e.matmul(ps[:, 0:n], g2b[:, j * 128:(j + 1) * 128], lb[:, 2, 0:n], start=True, stop=True),
                         rd=[g2b, lb], wr=[ps])
                    P.op("act", lambda e: e.copy(gate[:, j, 0:n], ps[:, 0:n]), rd=[ps], wr=[gate])
                P.op("dve", lambda e: e.tensor_tensor(T0, k_, bc(pcol("kk", 0, 4), 4, n), ALU.mult), rd=[xs, pc], wr=[tmp])
                P.op("act", lambda e: e.activation(out=b4[:, :, 0:n], in_=T0, func=AF.Square), rd=[tmp], wr=[b4])
                for j in range(4):
                    ps = psf.next()
                    P.op("pe", lambda e: e.matmul(ps[:, 0:n], blk_bf[:], b4[:, j, 0:n], start=True, stop=True), rd=[blk_bf, b4], wr=[ps])
                    rsqrt(tmp[:, 4 + j, 0:n], ps[:, 0:n], 1.0, 1e-24, [ps], [tmp])
                P.op("dve", lambda e: e.tensor_tensor(kkn[:, :, 0:n], T0, T1, ALU.mult), rd=[tmp], wr=[kkn])
                P.op("dve", lambda e: e.tensor_tensor(T0, r_, k_, ALU.mult), rd=[xs], wr=[tmp])
                P.op("dve", lambda e: e.tensor_tensor(b4[:, :, 0:n], T0, bc(pcol("rk", 0, 4), 4, n), ALU.mult), rd=[tmp, pc], wr=[b4])
                for j in range(4):
                    ps = psf.next()
                    P.op("pe", lambda e: e.matmul(ps[:, 0:n], blk_bf[:], b4[:, j, 0:n], start=True, stop=True), rd=[blk_bf, b4], wr=[ps])
                    P.op("dve", lambda e: e.tensor_tensor(tmp[:, 4 + j, 0:n], ps[:, 0:n], xs[:, 8 + j, 0:n], ALU.mult), rd=[ps, xs], wr=[tmp])
                P.op("dve", lambda e: e.tensor_tensor(gbt[:, 1, :, 0:n], T1, gate[:, :, 0:n], ALU.mult), rd=[tmp, gate], wr=[gbt])
                P.op("act", lambda e: e.copy(gbt[:, 0, :, 0:n], gate[:, :, 0:n]), rd=[gate], wr=[gbt])
                for w in range(2):
                    P.dma("sp", GB[s, w, :, :, e0:e0 + n].rearrange("j p t -> p j t"), gbt[:, w, :, 0:n], rd=[gbt], wr=[L_GB[s][ti]])
                P.op("act", lambda e: e.copy(b4[:, :, 0:n], v_), rd=[xs], wr=[b4])
                for sub in range(n // 128):
                    pb = psb.next()
                    for j in range(4):
                        P.op("pe", lambda e: e.transpose(pb[:, j * 128:(j + 1) * 128], b4[:, j, sub * 128:(sub + 1) * 128], ident_bf[:]),
                             rd=[b4, ident_bf], wr=[pb])
                    P.op("dve", lambda e: e.tensor_copy(vtm[:, sub, :], pb[:, 0:512]), rd=[pb], wr=[vtm])
                P.dma("sp", VT[s, e0:e0 + n, :].rearrange("(a p) c -> p a c", p=128), vtm[:, 0:n // 128, :], rd=[vtm], wr=[L_VT[s][ti]])
                for d in range(2):
                    rv = (lambda ap: ap) if d == 0 else (lambda ap: ap[:, ::-1])
                    rfv = lambda kind: rf[:, :, 0:nck, kind, :]
                    v4 = lambda ap: ap.rearrange("p j (c t) -> p j c t", t=64)
                    for (wp, li, dst, bn) in ((w2p, 0, sig[d], "w0"), (a2p, 1, aa[d], "a0")):
                        for j in range(4):
                            ps = psf.next()
                            P.op("pe", lambda e: e.matmul(ps[:, 0:n], wp[:, d, j * 128:(j + 1) * 128], lb[:, li, 0:n], start=True, stop=True),
                                 rd=[wp, lb], wr=[ps])
                            P.op("act", lambda e: e.activation(out=dst[:, j, 0:n], in_=ps[:, 0:n], func=AF.Sigmoid,
                                                               bias=pcol(bn, d * 4 + j)), rd=[ps, pc], wr=[dst])
                    P.op("dve", lambda e: e.tensor_tensor(T0, aa[d][:, :, 0:n], bc(pcol("ka", 0, 4), 4, n), ALU.mult), rd=[aa[d], pc], wr=[tmp])
                    P.op("dve", lambda e: e.tensor_tensor(T0, T0, bc(omka[:], 4, n), ALU.add), rd=[tmp, omka], wr=[tmp])
                    P.op("dve", lambda e: e.tensor_tensor(kd[:, :, 0:n], T0, k_, ALU.mult), rd=[tmp, xs], wr=[kd])
                    for j in range(4):
                        P.op("dve", lambda e: e.tensor_tensor_scan(rv(tmp[:, 8 + j, 0:n]), rv(rmask[:, d, 0:n]), rv(sig[d][:, j, 0:n]),
                                                                   0.0, ALU.mult, ALU.add), rd=[rmask, sig[d]], wr=[tmp])
                    P.op("dve", lambda e: e.tensor_tensor(T0, T2, sig[d][:, :, 0:n], ALU.subtract), rd=[tmp, sig[d]], wr=[tmp])
                    P.op("act", lambda e: e.activation(out=T0, in_=T0, func=AF.Exp, scale=-C0), rd=[tmp], wr=[tmp])
                    P.op("dve", lambda e: e.tensor_scalar(T0, T0, -1.0, None, ALU.mult), rd=[tmp], wr=[tmp])
                    P.op("dve", lambda e: e.tensor_tensor(rfv(0), v4(kkn[:, :, 0:n]), v4(T0), ALU.mult), rd=[kkn, tmp], wr=[rf])
                    P.op("act", lambda e: e.activation(out=T0, in_=T2, func=AF.Exp, scale=-C0), rd=[tmp], wr=[tmp])
                    P.op("dve", lambda e: e.tensor_tensor(rfv(1), v4(r_), v4(T0), ALU.mult), rd=[xs, tmp], wr=[rf])
                    lastpos = 63 if d == 0 else 0
                    P.op("dve", lambda e: e.tensor_copy(plt[:, :, 0:nck], v4(T0)[:, :, :, lastpos]), rd=[tmp], wr=[plt])
                    P.dma("sp", PLs[s, d, :, :, e0 // 64:e0 // 64 + nck], plt[:, :, 0:nck], rd=[plt], wr=[L_RF[s][d][ti]])
                    P.op("act", lambda e: e.activation(out=T0, in_=T2, func=AF.Exp, scale=C0), rd=[tmp], wr=[tmp])
                    P.op("dve", lambda e: e.tensor_tensor(T1, kkn[:, :, 0:n], aa[d][:, :, 0:n], ALU.mult), rd=[kkn, aa[d]], wr=[tmp])
                    P.op("dve", lambda e: e.tensor_tensor(rfv(2), v4(T1), v4(T0), ALU.mult), rd=[tmp], wr=[rf])
                    P.op("dve", lambda e: e.tensor_tensor(rfv(3), v4(kd[:, :, 0:n]), v4(T0), ALU.mult), rd=[kd, tmp], wr=[rf])
                    for j in range(4):
                        P.dma("sp" if j % 2 == 0 else "pool", RF[s, d, j, :, e0 // 64:e0 // 64 + nck, :, :], rf[:, j, 0:nck, :, :],
                              rd=[rf], wr=[L_RF[s][d][ti]])
        P.scope_end(mark1)
        if cfg.stop_after == "rwB1":
            P.scope_end(mark)
            return

        mark2 = P.scope_begin()
        mk2 = P.sb([64, 2, 128], F32, "mk2")
        mkT = P.sb([64, 2, 64], F32, "mkT")
        for d in range(2):
            P.dma("sp", mk2[:, d, 0:64], masks_d[2 * d], wr=[mk2])
            P.dma("sp", mk2[:, d, 64:128], masks_d[2 * d + 1], wr=[mk2])
            P.dma("sp", mkT[:, d, :], masks_d[2 - 2 * d], wr=[mkT])
        lvm_f = P.sb([64, 6, 64], F32, "lvm_f")
        lvm = P.sb([64, 6, 64], BF16, "lvm")
        P.dma("sp", lvm_f[:], lvmask_d.rearrange("l i t -> i l t"), wr=[lvm_f])
        P.op("dve", lambda e: e.tensor_copy(lvm[:], lvm_f[:]), rd=[lvm_f], wr=[lvm])
        NSLOT = 2

        def alloc_slot():
            return dict(rt=P.sb([64, 8, 4, 4, 64], BF16, "rft"), vt=P.sb([64, 4, 512], BF16, "vt2"),
                        pl=P.sb([64, 8, 4], F32, "plr"), ys=P.sb([64, 8, 256], F32, "ysb"),
                        ST=P.sb([64, 8, 64], F32, "ST"), STb=P.sb([64, 8, 64], BF16, "STb"),
                        AB=P.sb([64, 8, 128], BF16, "AB"), AK=P.sb([64, 8, 128], BF16, "AK"),
                        NM=P.sb([64, 2, 6, 8, 64], BF16, "NM"), TH=P.sb([64, 8, 64], BF16, "TH"),
                        TT=P.sb([64, 8, 64], BF16, "TT"), Zs=P.sb([64, 2, 8, 64], BF16, "Zs"),
                        Wb=P.sb([64, 8, 64], BF16, "Wb"), XW=P.sb([64, 8, 128], BF16, "XW"),
                        KB=P.sb([64, 8, 2, 64], BF16, "KB"))
        slots = [alloc_slot() for _ in range(NSLOT)]
        for i_, sl_ in enumerate(slots):
            sl_["ps"] = Rot(psf.bufs[i_ * 3:(i_ + 1) * 3])
        j3 = lambda ap, c: ap.rearrange("p (j c) -> p j c", c=c)
        HS = list(range(8))

        def scan_stream(s, d, B_):
                rt, vt, pl, ys = B_["rt"], B_["vt"], B_["pl"], B_["ys"]
                ST, STb, AB, AK, NM, TH, TT = B_["ST"], B_["STb"], B_["AB"], B_["AK"], B_["NM"], B_["TH"], B_["TT"]
                Zs, Wb, XW, KB = B_["Zs"], B_["Wb"], B_["XW"], B_["KB"]
                myps = B_["ps"]
                P.op("dve", lambda e: e.memset(ST[:], 0.0), wr=[ST])
                P.op("dve", lambda e: e.memset(STb[:], 0.0), wr=[STb])
                order = list(range(len(cfg.tiles)))
                nctx_t = sum(1 for (e0, n) in cfg.tiles if e0 < CTX)
                if d == 1:
                    order = list(range(nctx_t - 1, -1, -1)) + list(range(len(cfg.tiles) - 1, nctx_t - 1, -1))
                pieces = []
                for ti in order:
                    e0_, n_ = cfg.tiles[ti]
                    offs = list(range(0, n_, 256))
                    if d == 1:
                        offs = offs[::-1]
                    pieces += [(ti, e0_ + o, min(256, n_ - o)) for o in offs]
                for (ti, e0, n) in pieces:
                    nck = n // 64
                    c0 = e0 // 64
                    for h in HS:
                        j, hl = h // 2, h % 2
                        P.dma("sp" if h % 2 == 0 else "pool", rt[:, h, 0:nck, :, :], RF[s, d, j, hl * 64:(hl + 1) * 64, c0:c0 + nck, :, :],
                              rd=[L_RF[s][d][ti]], wr=[rt])
                        P.dma("sp", pl[:, h, 0:nck], PLs[s, d, hl * 64:(hl + 1) * 64, j, c0:c0 + nck], rd=[L_RF[s][d][ti]], wr=[pl])
                    P.dma("pool", vt[:, 0:nck, :], VT[s, e0:e0 + n, :].rearrange("(c t) f -> t c f", t=64), rd=[L_VT[s][ti]], wr=[vt])
                    for ck in (range(nck) if d == 0 else range(nck - 1, -1, -1)):
                        Vh = lambda h: vt[:, ck, h * 64:(h + 1) * 64]
                        G2 = ((0, slice(0, 4)), (1, slice(4, 8)))
                        psB, psT = [myps.next(), myps.next()], myps.next()
                        for h in HS:
                            g, c = h // 4, h % 4
                            P.op("pe", lambda e: e.matmul(psB[g][0:64, c * 128:(c + 1) * 128], rt[:, h, ck, 2, :], rt[:, h, ck, 0:2, :],
                                                          start=True, stop=True), rd=[rt], wr=[psB[g]])
                            P.op("pe", lambda e: e.matmul(psT[0:64, h * 64:(h + 1) * 64], rt[:, h, ck, 0, :], rt[:, h, ck, 2, :],
                                                          start=True, stop=True), rd=[rt], wr=[psT])
                        yield
                        m2 = mk2[:, d, :].unsqueeze(1).broadcast_to([64, 4, 128])
                        for g, hs in G2:
                            P.op("dve", lambda e: e.tensor_tensor(AB[:, hs, :], j3(psB[g][0:64, 0:512], 128), m2, ALU.mult), rd=[psB[g], mk2], wr=[AB])
                        P.op("dve", lambda e: e.tensor_tensor(XW[:, :, 0:64], j3(psT[0:64, 0:512], 64),
                                                               mkT[:, d, :].unsqueeze(1).broadcast_to([64, 8, 64]), ALU.mult),
                             rd=[psT, mkT], wr=[XW])
                        psK = [myps.next(), myps.next()]
                        for h in HS:
                            g, c = h // 4, h % 4
                            P.op("pe", lambda e: e.matmul(psK[g][0:64, c * 128:(c + 1) * 128], rt[:, h, ck, 3, :], rt[:, h, ck, 0:2, :],
                                                          start=True, stop=True), rd=[rt], wr=[psK[g]])
                        yield
                        for g, hs in G2:
                            P.op("dve", lambda e: e.tensor_tensor(AK[:, hs, :], j3(psK[g][0:64, 0:512], 128), m2, ALU.mult), rd=[psK[g], mk2], wr=[AK])
                        lv4 = lvm[:].unsqueeze(2).broadcast_to([64, 6, 8, 64])
                        P.op("dve", lambda e: e.tensor_tensor(NM[:, 0], AB[:, :, 0:64].unsqueeze(1).broadcast_to([64, 6, 8, 64]), lv4, ALU.mult),
                             rd=[AB, lvm], wr=[NM])
                        P.op("dve", lambda e: e.tensor_tensor(NM[:, 1], XW[:, :, 0:64].unsqueeze(1).broadcast_to([64, 6, 8, 64]), lv4, ALU.mult),
                             rd=[XW, lvm], wr=[NM])
                        idb = ident_bf[0:64, 0:64].unsqueeze(1).broadcast_to([64, 8, 64])
                        P.op("dve", lambda e: e.tensor_tensor(TH[:], NM[:, 0, 0], idb, ALU.add), rd=[NM, ident_bf], wr=[TH])
                        P.op("dve", lambda e: e.tensor_tensor(TT[:], NM[:, 1, 0], idb, ALU.add), rd=[NM, ident_bf], wr=[TT])
                        yield
                        psW = myps.next()
                        for h in HS:
                            o_ = psW[0:64, h * 64:(h + 1) * 64]
                            P.op("pe", lambda e: e.matmul(o_, rt[:, h, ck, 0, :], STb[:, h, :], start=True, stop=False), rd=[rt, STb], wr=[psW])
                            P.op("pe", lambda e: e.matmul(o_, AK[:, h, 0:64], Vh(h), start=False, stop=True), rd=[AK, vt], wr=[psW])
                        P.op("act", lambda e: e.copy(Wb[:], j3(psW[0:64, 0:512], 64)), rd=[psW], wr=[Wb])
                        yield
                        for lvl in range(1, 6):
                            psZ, psZp = myps.next(), myps.next()
                            for h in HS:
                                P.op("pe", lambda e: e.matmul(psZ[0:64, h * 64:(h + 1) * 64], NM[:, 1, lvl, h, :], TH[:, h, :],
                                                              start=True, stop=True), rd=[NM, TH], wr=[psZ])
                                P.op("pe", lambda e: e.matmul(psZp[0:64, h * 64:(h + 1) * 64], NM[:, 0, lvl, h, :], TT[:, h, :],
                                                              start=True, stop=True), rd=[NM, TT], wr=[psZp])
                            P.op("act", lambda e: e.copy(Zs[:, 0], j3(psZ[0:64, 0:512], 64)), rd=[psZ], wr=[Zs])
                            P.op("dve", lambda e: e.tensor_copy(Zs[:, 1], j3(psZp[0:64, 0:512], 64)), rd=[psZp], wr=[Zs])
                            yield
                            psA, psBt = myps.next(), myps.next()
                            for h in HS:
                                P.op("pe", lambda e: e.matmul(psA[0:64, h * 64:(h + 1) * 64], TT[:, h, :], Zs[:, 0, h, :],
                                                              start=True, stop=True), rd=[TT, Zs], wr=[psA])
                                P.op("pe", lambda e: e.matmul(psBt[0:64, h * 64:(h + 1) * 64], TH[:, h, :], Zs[:, 1, h, :],
                                                              start=True, stop=True), rd=[TH, Zs], wr=[psBt])
                            P.op("dve", lambda e: e.tensor_tensor(TH[:], TH[:], j3(psA[0:64, 0:512], 64), ALU.add), rd=[TH, psA], wr=[TH])
                            P.op("dve", lambda e: e.tensor_tensor(TT[:], TT[:], j3(psBt[0:64, 0:512], 64), ALU.add), rd=[TT, psBt], wr=[TT])
                            yield
                        psUu = myps.next()
                        for h in HS:
                            P.op("pe", lambda e: e.matmul(psUu[0:64, h * 64:(h + 1) * 64], TH[:, h, :], Wb[:, h, :], start=True, stop=True),
                                 rd=[TH, Wb], wr=[psUu])
                        P.op("act", lambda e: e.copy(XW[:, :, 64:128], j3(psUu[0:64, 0:512], 64)), rd=[psUu], wr=[XW])
                        yield
                        Ub = lambda h: XW[:, h, 64:128]
                        psY = myps.next()
                        for h in HS:
                            o_ = psY[0:64, h * 64:(h + 1) * 64]
                            P.op("pe", lambda e: e.matmul(o_, STb[:, h, :], rt[:, h, ck, 1, :], start=True, stop=False), rd=[STb, rt], wr=[psY])
                            P.op("pe", lambda e: e.matmul(o_, Vh(h), AK[:, h, 64:128], start=False, stop=False), rd=[vt, AK], wr=[psY])
                            P.op("pe", lambda e: e.matmul(o_, Ub(h), AB[:, h, 64:128], start=False, stop=True), rd=[XW, AB], wr=[psY])
                        P.op("act", lambda e: e.copy(ys[:, :, ck * 64:(ck + 1) * 64], j3(psY[0:64, 0:512], 64)), rd=[psY], wr=[ys])
                        yield
                        psX = [myps.next(), myps.next()]
                        for h in HS:
                            g, c = h // 4, h % 4
                            for w, kind in enumerate((3, 2)):
                                P.op("pe", lambda e: e.matmul(psX[g][0:64, (c * 2 + w) * 64:(c * 2 + w + 1) * 64], rt[:, h, ck, kind, :],
                                                              ident_bf[0:64, 0:64], start=True, stop=True), rd=[rt, ident_bf], wr=[psX[g]])
                        for g, hs in G2:
                            P.op("dve", lambda e: e.tensor_copy(KB[:, hs, :, :], psX[g][0:64, 0:512].rearrange("p (j w c) -> p j w c", w=2, c=64)),
                                 rd=[psX[g]], wr=[KB])
                        yield
                        psS = myps.next()
                        for h in HS:
                            o_ = psS[0:64, h * 64:(h + 1) * 64]
                            P.op("pe", lambda e: e.matmul(o_, KB[:, h, 0, :], Vh(h), start=True, stop=False), rd=[KB, vt], wr=[psS])
                            P.op("pe", lambda e: e.matmul(o_, KB[:, h, 1, :], Ub(h), start=False, stop=True), rd=[KB, XW], wr=[psS])
                        P.op("dve", lambda e: e.tensor_tensor(ST[:], ST[:], j3(psS[0:64, 0:512], 64), ALU.add), rd=[ST, psS], wr=[ST])
                        P.op("dve", lambda e: e.tensor_tensor(ST[:], ST[:], pl[:, :, ck:ck + 1].broadcast_to([64, 8, 64]), ALU.mult),
                             rd=[ST, pl], wr=[ST])
                        P.op("act", lambda e: e.copy(STb[:], ST[:]), rd=[ST], wr=[STb])
                        yield
                    for hl in range(2):
                        P.dma("sp", YR[s, d, :, hl * 64:(hl + 1) * 64, e0:e0 + n].rearrange("j p t -> p j t"), ys[:, hl::2, 0:n],
                              rd=[ys], wr=[L_YR[s][d][ti]])

        streams = [(s_, d_) for s_ in range(NS) for d_ in range(2)]
        for g0 in range(0, len(streams), NSLOT):
            gens = [scan_stream(s_, d_, slots[i]) for i, (s_, d_) in enumerate(streams[g0:g0 + NSLOT])]
            while gens:
                for g_ in list(gens):
                    try:
                        next(g_)
                    except StopIteration:
                        gens.remove(g_)
        P.scope_end(mark2)
        if cfg.stop_after == "rwB2":
            P.scope_end(mark)
            return

        mark3 = P.scope_begin()
        ya = Rot([P.sb([128, 4, 512], F32, "ya") for _ in range(2)])
        yb_ = Rot([P.sb([128, 4, 512], F32, "yb") for _ in range(2)])
        gbl = Rot([P.sb([128, 2, 4, 512], BF16, "gbl") for _ in range(2)])
        yo = Rot([P.sb([128, 4, 512], BF16, "yor") for _ in range(2)])
        scr = {"xb": P.sb([128, 4, 512], BF16, "xbr"), "sq": P.sb([128, 4, 512], BF16, "sqr"),
               "mean": P.sb([128, 4, 512], F32, "meanr"), "rstd": P.sb([128, 4, 512], F32, "rstdr"), "rd": []}
        for s in range(NS):
            for ti, (e0, n) in enumerate(cfg.tiles):
                if last and e0 < CTX:
                    continue
                a, b, g, y = ya.next(), yb_.next(), gbl.next(), yo.next()
                P.dma("sp", a[:, :, 0:n], YR[s, 0, :, :, e0:e0 + n].rearrange("j p t -> p j t"), rd=[L_YR[s][0][ti]], wr=[a])
                P.dma("pool", b[:, :, 0:n], YR[s, 1, :, :, e0:e0 + n].rearrange("j p t -> p j t"), rd=[L_YR[s][1][ti]], wr=[b])
                for w in range(2):
                    P.dma("sp", g[:, w, :, 0:n], GB[s, w, :, :, e0:e0 + n].rearrange("j p t -> p j t"), rd=[L_GB[s][ti]], wr=[g])
                P.op("dve", lambda e: e.tensor_tensor(a[:, :, 0:n], a[:, :, 0:n], b[:, :, 0:n], ALU.add), rd=[a, b], wr=[a])
                scr["rd"] = [a]
                stats_rstd(lambda j: a[:, j, 0:n], 4, n, (blk_bf, False), RW_GN_EPS, 1.0 / 64, scr)
                A_, B_ = a[:, :, 0:n], b[:, :, 0:n]
                P.op("dve", lambda e: e.tensor_tensor(A_, A_, scr["mean"][:, :, 0:n], ALU.subtract), rd=[a, scr["mean"]], wr=[a])
                P.op("dve", lambda e: e.tensor_tensor(A_, A_, scr["rstd"][:, :, 0:n], ALU.mult), rd=[a, scr["rstd"]], wr=[a])
                P.op("dve", lambda e: e.tensor_tensor(A_, A_, bc(pcol("rlnw", 0, 4), 4, n), ALU.mult), rd=[a, pc], wr=[a])
                P.op("dve", lambda e: e.tensor_tensor(A_, A_, bc(pcol("rlnb", 0, 4), 4, n), ALU.add), rd=[a, pc], wr=[a])
                P.op("dve", lambda e: e.tensor_tensor(A_, A_, g[:, 0, :, 0:n], ALU.mult), rd=[a, g], wr=[a])
                P.op("dve", lambda e: e.tensor_tensor(y[:, :, 0:n], A_, g[:, 1, :, 0:n], ALU.add), rd=[a, g], wr=[y])
                P.dma("sp", YBR[s, 0, :, :, e0:e0 + n].rearrange("j p t -> p j t"), y[:, :, 0:n], rd=[y], wr=[L_Y[s][0][ti]])
        P.scope_end(mark3)
        P.scope_end(mark)

    modulation()
    for l in range(DEPTH):
        cast_layer(l)
    for l in range(DEPTH):
        layer_params(l)
        last = (l == DEPTH - 1)
        if cfg.stop_after == "cast":
            break
        phase_A(l)
        if cfg.stop_after == "A":
            break
        conv_branch(l, last)
        if cfg.stop_after == "conv":
            break
        if cfg.stop_after not in ("rw", "rwB1", "rwB2"):
            mlstm_branch(l, last)
        if cfg.stop_after == "ml":
            break
        rwkv_branch(l, last)
        if cfg.stop_after in ("rw", "rwB1", "rwB2", "noC"):
            break
        phase_C(l, last)

    P.barrier()
    return nc, P


def host_maps(inp, cfg, n_cores):
    NS, DEPTH = cfg.NS, cfg.DEPTH
    f = lambda a: np.ascontiguousarray(np.asarray(a, np.float32))
    shared = {
        "w_ada": f(inp["w_ada"]), "b_ada": f(inp["b_ada"]),
        "w_in": pad_w_in(f(inp["w_in"])),
        "pcols": np.stack([pack_pcols(inp, l) for l in range(DEPTH)]),
        "rw_w2": f(inp["rw_w2"]).reshape(DEPTH, 128, 512),
        "rw_a2": f(inp["rw_a2"]).reshape(DEPTH, 128, 512),
        "rw_g2": f(inp["rw_g2"]),
        "mlb": np.ascontiguousarray(np.concatenate([np.stack([f(inp["ml_ib"]).reshape(DEPTH, 8), f(inp["ml_fb"]).reshape(DEPTH, 8)], -1)] * 2, 1)),
        "p_abc": np.ascontiguousarray(np.stack([f(inp["p_a"]), f(inp["p_b"]), f(inp["p_c"])], 1)),
        "w_out": f(inp["w_out"]), "w_mlp1": f(inp["w_mlp1"]), "w_mlp2": f(inp["w_mlp2"]),
        "g_final": f(inp["g_final"]).reshape(1, D),
        "ident": np.eye(128, dtype=np.float32),
        "selh": np.ascontiguousarray(np.kron(np.eye(4, dtype=np.float32), np.ones((1, 128), np.float32))),
        "blkones": np.kron(np.eye(2, dtype=np.float32), np.ones((64, 64), np.float32)),
    }
    idx = np.arange(64)
    shared["masks"] = np.stack([(idx[:, None] < idx[None, :]), (idx[:, None] <= idx[None, :]),
                                (idx[:, None] > idx[None, :]), (idx[:, None] >= idx[None, :])]).astype(np.float32)
    lv = lambda sz: ((idx[:, None] // (2 * sz) == idx[None, :] // (2 * sz)) & (idx[:, None] // sz != idx[None, :] // sz))
    shared["lvmask"] = np.stack([lv(sz) for sz in (1, 2, 4, 8, 16, 32)]).astype(np.float32)
    maps = []
    x, c, ctx, c_ctx = f(inp["x"]), f(inp["c"]), f(inp["ctx"]), f(inp["c_ctx"])
    for i in range(n_cores):
        m = dict(shared)
        m["x"] = x[i * NS:(i + 1) * NS]
        m["ctx"] = ctx[i * NS:(i + 1) * NS]
        cv = np.zeros((3, D), np.float32)
        cv[0:NS] = c[i * NS:(i + 1) * NS]
        cv[2] = c_ctx
        m["cvec"] = cv
        maps.append(m)
    return maps


_CACHE = {}


def kernel(**inputs):
    n_cores = 8
    B = inputs["x"].shape[0]
    cfg = Cfg(NS=B // n_cores, T=inputs["x"].shape[1], CTX=inputs["ctx"].shape[1], DEPTH=inputs["w_ada"].shape[0])
    key = (cfg.NS, cfg.T, cfg.CTX, cfg.DEPTH)
    if key not in _CACHE:
        _CACHE[key] = build(cfg)
    nc, _ = _CACHE[key]
    maps = host_maps(inputs, cfg, n_cores)
    res = run_bass_kernel_spmd(nc, maps, core_ids=list(range(n_cores)))
    return np.concatenate([np.asarray(r["out"], np.float32) for r in res.results], axis=0)
```

```python
import numpy as np
import concourse.bass as bass
import concourse.mybir as mybir
from concourse.bass_utils import run_bass_kernel_spmd

F32 = mybir.dt.float32
BF16 = mybir.dt.bfloat16
ALU = mybir.AluOpType
AF = mybir.ActivationFunctionType
AX = mybir.AxisListType

D = 1024
KC = 8
NCH = 64
PW = NCH * 128
DFF = 4096
L_CH = 64
NORM_EPS = 1e-6
LN_EPS = 1e-5
RW_GN_EPS = 64e-5

SELF_WAIT = True


class LT:
    __slots__ = ("name", "w", "r")

    def __init__(self, name=""):
        self.name = name
        self.w = None
        self.r = {}


class Buf:
    def __init__(self, t, lt, psum=False):
        self.t = t
        self.lt = lt
        self.psum = psum

    def __getitem__(self, k):
        return self.t[k]


class Prog:
    def __init__(self, nc, n_dma_sems=6):
        self.nc = nc
        self.engs = {"pe": nc.tensor, "act": nc.scalar, "dve": nc.vector,
                     "pool": nc.gpsimd, "sp": nc.sync}
        self.sems = {}
        self.cnt = {}
        for k in self.engs:
            self.sems[k] = nc.alloc_semaphore(name="s_" + k)
            self.cnt[k] = 0
        self.dq = {}
        for q in ("sp", "pool", "act"):
            lst = []
            for i in range(n_dma_sems):
                key = "d_%s%d" % (q, i)
                self.sems[key] = nc.alloc_semaphore(name=key)
                self.cnt[key] = 0
                lst.append(key)
            self.dq[q] = [lst, 0]
        self.obs = {k: {} for k in self.engs}
        self.ninst = 0
        self.nbuf = 0

    def _need(self, e, deps):
        for (k, v) in deps:
            if k == e and (e == "pe" or not SELF_WAIT):
                continue
            if self.obs[e].get(k, 0) < v:
                self.engs[e].wait_ge(self.sems[k], v)
                self.obs[e][k] = v
                self.ninst += 1

    @staticmethod
    def _lts(bufs):
        out = []
        for b in bufs:
            if isinstance(b, LT):
                out.append(b)
            elif isinstance(b, Buf):
                out.append(b.lt)
            else:
                raise TypeError(b)
        return out

    def _deps(self, reads, writes):
        deps = []
        for t in reads:
            if t.w is not None:
                deps.append(t.w)
        for t in writes:
            if t.w is not None:
                deps.append(t.w)
            for k, v in t.r.items():
                deps.append((k, v))
        return deps

    def op(self, e, fn, rd=(), wr=()):
        if e != "pe":
            pr = [b for b in rd if isinstance(b, Buf) and b.psum]
            if pr:
                rd = [b for b in rd if not (isinstance(b, Buf) and b.psum)]
                wr = list(wr) + [b for b in pr if b not in wr]
        reads, writes = self._lts(rd), self._lts(wr)
        self._need(e, self._deps(reads, writes))
        ins = fn(self.engs[e])
        self.cnt[e] += 1
        c = self.cnt[e]
        ins.then_inc(self.sems[e], 1)
        self.ninst += 1
        for t in reads:
            t.r[e] = c
        for t in writes:
            t.w = (e, c)
            t.r = {}
        return ins

    def dma(self, q, out, in_, rd=(), wr=(), **kw):
        reads, writes = self._lts(rd), self._lts(wr)
        lst, idx = self.dq[q]
        key = lst[idx % len(lst)]
        self.dq[q][1] = idx + 1
        deps = self._deps(reads, writes)
        if self.cnt[key] > 0:
            deps.append((key, self.cnt[key]))
        self._need(q, deps)
        ins = self.engs[q].dma_start(out=out, in_=in_, **kw)
        self.cnt[key] += 16
        c = self.cnt[key]
        ins.then_inc(self.sems[key], 16)
        self.ninst += 1
        for t in reads:
            t.r[key] = c
        for t in writes:
            t.w = (key, c)
            t.r = {}
        return ins

    def finish(self, bufs, e="sp"):
        deps = []
        for t in self._lts(bufs):
            if t.w is not None:
                deps.append(t.w)
        self._need(e, deps)

    def barrier(self):
        allc = [(k, v) for k, v in self.cnt.items() if v > 0]
        for e in self.engs:
            self._need(e, allc)

    def scope_begin(self):
        return (self.nc.sbuf_base, self.nc.sbuf_top)

    def scope_end(self, mark):
        self.barrier()
        self.nc.sbuf_base, self.nc.sbuf_top = mark

    def sb(self, shape, dt, name=None):
        self.nbuf += 1
        nm = "%s_%d" % (name or "b", self.nbuf)
        return Buf(self.nc.alloc_sbuf_tensor(nm, list(shape), dt), LT(nm))

    def ps(self, shape, dt, name=None):
        self.nbuf += 1
        nm = "%s_%d" % (name or "p", self.nbuf)
        return Buf(self.nc.alloc_psum_tensor(nm, list(shape), dt), LT(nm), psum=True)


class Rot:
    def __init__(self, bufs):
        self.bufs = bufs
        self.i = 0

    def next(self):
        b = self.bufs[self.i % len(self.bufs)]
        self.i += 1
        return b


PCOLS = {}
_off = 0
for _nm, _n in (("mu_p", 15), ("mu_n", 15), ("w0", 8), ("a0", 8), ("kk", 4), ("ka", 4), ("rk", 4),
                ("rlnw", 4), ("rlnb", 4), ("cdw", 4 * 31), ("cdb", 4), ("clnw", 4), ("clnb", 4),
                ("mnw", 4), ("g1", 8), ("g2", 8)):
    PCOLS[_nm] = (_off, _n)
    _off += _n
NPCOL = _off


def _cm(vec, nchunks):
    return np.ascontiguousarray(np.asarray(vec, np.float32).reshape(nchunks, 128).T)


def pack_pcols(inp, l):
    out = np.zeros((128, NPCOL), np.float32)

    def put(nm, arr):
        o, n = PCOLS[nm]
        assert arr.shape == (128, n), (nm, arr.shape)
        out[:, o:o + n] = arr
    put("mu_p", _cm(inp["mu_prev"][l], 15))
    put("mu_n", _cm(inp["mu_next"][l], 15))
    put("w0", np.concatenate([_cm(inp["rw_w0"][l, d], 4) for d in range(2)], 1))
    put("a0", np.concatenate([_cm(inp["rw_a0"][l, d], 4) for d in range(2)], 1))
    put("kk", _cm(inp["rw_kk"][l], 4))
    put("ka", _cm(inp["rw_ka"][l], 4))
    put("rk", _cm(inp["rw_rk"][l].reshape(-1), 4))
    put("rlnw", _cm(inp["rw_lnw"][l], 4))
    put("rlnb", _cm(inp["rw_lnb"][l], 4))
    dw = np.asarray(inp["cv_dw"][l], np.float32)
    put("cdw", np.concatenate([np.ascontiguousarray(dw[:, j * 128:(j + 1) * 128].T) for j in range(4)], 1))
    put("cdb", _cm(inp["cv_db"][l], 4))
    put("clnw", _cm(inp["cv_lnw"][l], 4))
    put("clnb", _cm(inp["cv_lnb"][l], 4))
    put("mnw", _cm(inp["ml_nw"][l], 4))
    put("g1", _cm(inp["g_norm1"][l], 8))
    put("g2", _cm(inp["g_norm2"][l], 8))
    return out


def pad_w_in(w):
    Ln = w.shape[0]
    out = np.zeros((Ln, D, PW), np.float32)
    out[:, :, 0:5008] = w[:, :, 0:5008]
    out[:, :, 5120:8192] = w[:, :, 5008:8080]
    return out


class Cfg:
    def __init__(self, NS=2, T=4096, CTX=256, DEPTH=4, debug=False, stop_after=None, c_stop=9):
        self.c_stop = c_stop
        self.NS, self.T, self.CTX, self.DEPTH = NS, T, CTX, DEPTH
        self.TC = T + CTX
        self.debug = debug
        self.stop_after = stop_after
        self.tiles = []
        e = 0
        while e < CTX:
            n = min(512, CTX - e)
            self.tiles.append((e, n)); e += n
        while e < self.TC:
            n = min(512, self.TC - e)
            self.tiles.append((e, n)); e += n


def build(cfg):
    NS, T, CTX, DEPTH, TC = cfg.NS, cfg.T, cfg.CTX, cfg.DEPTH, cfg.TC
    nc = bass.Bass("TRN2", target_bir_lowering=False)
    P = Prog(nc)
    kind_dbg = "ExternalOutput" if cfg.debug else "Internal"

    def din(name, shape, dt=F32):
        return nc.dram_tensor(name, list(shape), dt, kind="ExternalInput").ap()

    def dscr(name, shape, dt, dbg=False):
        return nc.dram_tensor(name, list(shape), dt, kind=(kind_dbg if dbg else "Internal")).ap()

    x_in = din("x", [NS, T, D])
    ctx_in = din("ctx", [NS, CTX, D])
    cvec = din("cvec", [3, D])
    w_ada = din("w_ada", [DEPTH, D, 6 * D])
    b_ada = din("b_ada", [DEPTH, 6 * D])
    w_in = din("w_in", [DEPTH, D, PW])
    pcols_d = din("pcols", [DEPTH, 128, NPCOL])
    rw_w2 = din("rw_w2", [DEPTH, 128, 512])
    rw_a2 = din("rw_a2", [DEPTH, 128, 512])
    rw_g2 = din("rw_g2", [DEPTH, 128, 512])
    mlb = din("mlb", [DEPTH, 16, 2])
    p_abc = din("p_abc", [DEPTH, 3, 512, D])
    w_out = din("w_out", [DEPTH, D, D])
    w_mlp1 = din("w_mlp1", [DEPTH, D, DFF])
    w_mlp2 = din("w_mlp2", [DEPTH, DFF, D])
    g_final = din("g_final", [1, D])
    out = nc.dram_tensor("out", [NS, T, D], F32, kind="ExternalOutput").ap()

    xc = dscr("xc", [NS, CTX, D], F32)
    U = dscr("U", [NS, NCH, 128, TC], BF16, dbg=True)
    MOD = dscr("MOD", [DEPTH, 3, 6 * D], F32, dbg=True)
    WB_in = dscr("WB_in", [DEPTH, D, PW], BF16)
    WB_p = dscr("WB_p", [DEPTH, 3, 512, D], BF16)
    WB_out = dscr("WB_out", [DEPTH, D, D], BF16)
    WB_1 = dscr("WB_1", [DEPTH, D, DFF], BF16)
    WB_2 = dscr("WB_2", [DEPTH, DFF, D], BF16)
    YBR = dscr("YBR", [NS, 3, 4, 128, TC], BF16, dbg=True)

    L_U = [[LT("U%d_%d" % (s, i)) for i in range(len(cfg.tiles))] for s in range(NS)]
    L_X = [[LT("X%d_%d" % (s, i)) for i in range(len(cfg.tiles))] for s in range(NS)]
    L_WB = [LT("WB%d" % l) for l in range(DEPTH)]
    L_MOD = LT("MOD")
    L_Y = [[[LT() for i in range(len(cfg.tiles))] for b in range(3)] for s in range(NS)]

    psf = Rot([P.ps([128, 512], F32, "psf") for _ in range(6)])
    psb = Rot([P.ps([128, 1024], BF16, "psb") for _ in range(2)])

    ident_bf = P.sb([128, 128], BF16, "identb")
    ident_f = P.sb([128, 128], F32, "identf")
    ones_bf = P.sb([128, 128], BF16, "onesb")
    cst = {}

    def const_setup():
        pass

    ident_d = din("ident", [128, 128])
    blk_d = din("blkones", [128, 128])
    masks_d = din("masks", [4, 64, 64])
    lvmask_d = din("lvmask", [6, 64, 64])
    P.dma("sp", ident_f[:], ident_d, wr=[ident_f])
    P.op("dve", lambda e: e.tensor_copy(ident_bf[:], ident_f[:]), rd=[ident_f], wr=[ident_bf])
    P.op("dve", lambda e: e.memset(ones_bf[:], 1.0), wr=[ones_bf])
    blk_f = P.sb([128, 128], F32, "blkf")
    blk_bf = P.sb([128, 128], BF16, "blkb")
    P.dma("sp", blk_f[:], blk_d, wr=[blk_f])
    P.op("dve", lambda e: e.tensor_copy(blk_bf[:], blk_f[:]), rd=[blk_f], wr=[blk_bf])

    cast_i = [0]
    castbuf = {}

    def cast_dram(src, dst, R, C, lt_dst):
        for r0 in range(0, R, 128):
            for c0 in range(0, C, 2048):
                cw = min(2048, C - c0)
                a, b = castbuf["cin"].next(), castbuf["cout"].next()
                P.dma("sp", a[:, :cw], src[r0:r0 + 128, c0:c0 + cw], wr=[a])
                eng = ("dve", "act", "pool")[cast_i[0] % 3]
                cast_i[0] += 1
                if eng == "act":
                    P.op("act", lambda e: e.copy(b[:, :cw], a[:, :cw]), rd=[a], wr=[b])
                else:
                    P.op(eng, lambda e: e.tensor_copy(b[:, :cw], a[:, :cw]), rd=[a], wr=[b])
                P.dma("pool", dst[r0:r0 + 128, c0:c0 + cw], b[:, :cw], rd=[b], wr=[lt_dst])

    def cast_layer(l):
        mark = P.scope_begin()
        castbuf["cin"] = Rot([P.sb([128, 2048], F32, "cin") for _ in range(3)])
        castbuf["cout"] = Rot([P.sb([128, 2048], BF16, "cout") for _ in range(3)])
        cast_dram(w_in[l], WB_in[l], D, PW, L_WB[l])
        for br in range(3):
            cast_dram(p_abc[l, br], WB_p[l, br], 512, D, L_WB[l])
        cast_dram(w_out[l], WB_out[l], D, D, L_WB[l])
        cast_dram(w_mlp1[l], WB_1[l], D, DFF, L_WB[l])
        cast_dram(w_mlp2[l], WB_2[l], DFF, D, L_WB[l])
        P.scope_end(mark)

    def modulation():
        mark = P.scope_begin()
        cT = P.sb([128, KC, 3], F32, "cT")
        scT = P.sb([128, KC, 3], F32, "scT")
        for v in range(3):
            P.dma("sp", cT[:, :, v], cvec[v].rearrange("(k p) -> p k", p=128), wr=[cT], allow_slow_non_contiguous=True)
        P.op("act", lambda e: e.activation(out=scT[:], in_=cT[:], func=AF.Silu), rd=[cT], wr=[scT])
        wada_t = Rot([P.sb([128, KC, 512], F32, "wada") for _ in range(2)])
        bada_t = Rot([P.sb([3, 512], F32, "bada") for _ in range(2)])
        modrow = Rot([P.sb([3, 512], F32, "modrow") for _ in range(2)])
        for l in range(DEPTH):
            for g in range(12):
                wt, bt, mr = wada_t.next(), bada_t.next(), modrow.next()
                P.dma("sp", bt[:], b_ada[l:l + 1, g * 512:(g + 1) * 512].broadcast_to([3, 512]), wr=[bt])
                P.dma("sp", wt[:], w_ada[l, :, g * 512:(g + 1) * 512].rearrange("(k p) c -> p k c", p=128), wr=[wt])
                ps = psf.next()
                for k in range(KC):
                    P.op("pe", lambda e: e.matmul(ps[0:3, :], scT[:, k, :], wt[:, k, :], start=(k == 0), stop=(k == KC - 1)),
                         rd=[scT, wt], wr=[ps])
                P.op("dve", lambda e: e.tensor_tensor(mr[:], ps[0:3, :], bt[:], ALU.add), rd=[ps, bt], wr=[mr])
                P.dma("sp", MOD[l, :, g * 512:(g + 1) * 512], mr[:], rd=[mr], wr=[L_MOD])
        P.scope_end(mark)

    pc = P.sb([128, NPCOL], F32, "pcols")
    modc = P.sb([128, 48, 3], F32, "modc")
    g1c = P.sb([128, 3, KC], F32, "g1c")
    g2c = P.sb([128, 3, KC], F32, "g2c")
    mgrow = [[P.sb([128, D], F32, "mgrow") for w in range(2)] for v in range(3)]

    def pcol(nm, i=0, n=1):
        o, _ = PCOLS[nm]
        return pc[:, o + i:o + i + n]

    def layer_params(l):
        P.dma("sp", pc[:], pcols_d[l], wr=[pc])
        for v in range(3):
            P.dma("sp", modc[:, :, v], MOD[l, v].rearrange("(c p) -> p c", p=128), rd=[L_MOD], wr=[modc],
                  allow_slow_non_contiguous=True)
        for v in range(3):
            for w, mi in enumerate((2, 5)):
                P.dma("sp", mgrow[v][w][:], MOD[l, v:v + 1, mi * D:(mi + 1) * D].broadcast_to([128, D]),
                      rd=[L_MOD], wr=[mgrow[v][w]])
            for (gc, gname, mi) in ((g1c, "g1", 1), (g2c, "g2", 4)):
                P.op("dve", lambda e: e.scalar_tensor_tensor(gc[:, v, :], modc[:, mi * 8:(mi + 1) * 8, v], 1.0,
                                                              pcol(gname, 0, 8), ALU.add, ALU.mult),
                     rd=[modc, pc], wr=[gc])

    def x_src(l, s, e0, n):
        if e0 < CTX:
            return (ctx_in if l == 0 else xc)[s, e0:e0 + n, :]
        return (x_in if l == 0 else out)[s, e0 - CTX:e0 - CTX + n, :]

    def x_dst(s, e0, n):
        if e0 < CTX:
            return xc[s, e0:e0 + n, :]
        return out[s, e0 - CTX:e0 - CTX + n, :]

    ss_t = Rot([P.sb([128, 8], F32, "ss") for _ in range(2)])
    junk = P.sb([128, D], F32, "junk")

    epsc = P.sb([128, 4], F32, "epsc")
    P.op("dve", lambda e: e.memset(epsc[:, 0:1], NORM_EPS), wr=[epsc])
    P.op("dve", lambda e: e.memset(epsc[:, 1:2], LN_EPS), wr=[epsc])
    P.op("dve", lambda e: e.memset(epsc[:, 2:3], RW_GN_EPS), wr=[epsc])
    P.op("dve", lambda e: e.memset(epsc[:, 3:4], 1e-24), wr=[epsc])
    EPSI = {NORM_EPS: 0, LN_EPS: 1, RW_GN_EPS: 2, 1e-24: 3}

    def rsqrt(out_ap, in_ap, scale, eps, rd, wr):
        i = EPSI[eps]
        npart = out_ap.shape[0]
        P.op("act", lambda e: e.activation(out=out_ap, in_=in_ap, func=AF.Sqrt, scale=scale, bias=epsc[0:npart, i:i + 1]),
             rd=list(rd) + [epsc], wr=wr)
        P.op("dve", lambda e: e.reciprocal(out_ap, out_ap), rd=wr, wr=wr)

    def norm_T(xt, nsub, gcol, shcol, hT, xn):
        ss = ss_t.next()
        for sub in range(nsub):
            P.op("act", lambda e: e.activation(out=junk[:], in_=xt[:, sub, :], func=AF.Square,
                                               accum_out=ss[:, sub:sub + 1]), rd=[xt], wr=[junk, ss])
        rsqrt(ss[:, 4:4 + nsub], ss[:, 0:nsub], 1.0 / D, NORM_EPS, [ss], [ss])
        for sub in range(nsub):
            P.op("dve", lambda e: e.tensor_scalar(xn[:, sub, :], xt[:, sub, :], ss[:, 4 + sub:5 + sub], None, ALU.mult),
                 rd=[xt, ss], wr=[xn])
        for k in range(KC):
            pb = psb.next()
            for sub in range(nsub):
                P.op("pe", lambda e: e.transpose(pb[:, sub * 128:(sub + 1) * 128], xn[:, sub, k * 128:(k + 1) * 128],
                                                 ident_bf[:]), rd=[xn, ident_bf], wr=[pb])
            P.op("act", lambda e: e.activation(out=hT[:, k, 0:nsub * 128], in_=pb[:, 0:nsub * 128], func=AF.Identity,
                                               scale=gcol(k), bias=shcol(k)), rd=[pb, g1c, g2c, modc], wr=[hT])

    evac_i = [0]

    def evac(dst_ap, ps, src_ap, wr):
        evac_i[0] += 1
        if evac_i[0] % 2:
            P.op("act", lambda e: e.copy(dst_ap, src_ap), rd=[ps], wr=wr)
        else:
            P.op("dve", lambda e: e.tensor_copy(dst_ap, src_ap), rd=[ps], wr=wr)

    def phase_A(l):
        mark = P.scope_begin()
        wg_rot = Rot([P.sb([128, KC, 512], BF16, "wg") for _ in range(3)])
        ust_rot = Rot([P.sb([128, 4, 512], BF16, "ust") for _ in range(2)])
        xt_rot = Rot([P.sb([128, 4, D], F32, "xt") for _ in range(2)])
        xn = P.sb([128, 4, D], BF16, "xn")
        hT_rot = Rot([P.sb([128, KC, 512], BF16, "hT") for _ in range(2)])
        conv_tile = conv_setup()
        last = (l == DEPTH - 1)
        pending = []
        for s in range(NS):
            for ti, (e0, n) in enumerate(cfg.tiles):
                nsub = n // 128
                v = 2 if e0 < CTX else s
                xt = xt_rot.next()
                P.dma("sp", xt[:, 0:nsub, :], x_src(l, s, e0, n).rearrange("(a p) d -> p a d", p=128),
                      rd=[L_X[s][ti]], wr=[xt])
                cg = conv_tile(l, last, *pending.pop(0)) if pending else None
                pending.append((s, ti))
                hT = hT_rot.next()
                norm_T(xt, nsub, lambda k: g1c[:, v, k:k + 1], lambda k: modc[:, 0 * 8 + k, v:v + 1], hT, xn)
                for g in range(NCH // 4):
                    wg = wg_rot.next()
                    P.dma("sp" if g % 2 == 0 else "pool", wg[:],
                          WB_in[l, :, g * 512:(g + 1) * 512].rearrange("(k p) c -> p k c", p=128),
                          rd=[L_WB[l]], wr=[wg])
                    ust = ust_rot.next()
                    for c4 in range(4):
                        ps = psf.next()
                        for k in range(KC):
                            P.op("pe", lambda e: e.matmul(ps[:, 0:n], wg[:, k, c4 * 128:(c4 + 1) * 128], hT[:, k, 0:n],
                                                          start=(k == 0), stop=(k == KC - 1)), rd=[wg, hT], wr=[ps])
                        evac(ust[:, c4, 0:n], ps, ps[:, 0:n], [ust])
                    P.dma("sp", U[s, g * 4:(g + 1) * 4, :, e0:e0 + n].rearrange("c p t -> p c t"), ust[:, :, 0:n],
                          rd=[ust], wr=[L_U[s][ti]])
                    if cg is not None:
                        try:
                            next(cg)
                        except StopIteration:
                            cg = None
                if cg is not None:
                    for _ in cg:
                        pass
        while pending:
            for _ in conv_tile(l, last, *pending.pop(0)):
                pass
        P.scope_end(mark)

    def ln_stats_bcast(xb, sqb, lhsT_ones, nj, n, eps, tag):
        raise NotImplementedError

    def phase_C(l, last):
        mark = P.scope_begin()
        pw = P.sb([128, 3, 4, D], BF16, "pw")
        wo = P.sb([128, KC, D], BF16, "wo")
        for br in range(3):
            P.dma("sp", pw[:, br, :, :], WB_p[l, br].rearrange("(j p) d -> p j d", p=128), rd=[L_WB[l]], wr=[pw])
        P.dma("sp", wo[:], WB_out[l].rearrange("(k p) d -> p k d", p=128), rd=[L_WB[l]], wr=[wo])
        yt_rot = Rot([P.sb([128, 12, 256], BF16, "yt") for _ in range(1)])
        gt_rot = Rot([P.sb([128, 24, 256], BF16, "gt") for _ in range(1)])
        mT = P.sb([128, KC, 256], F32, "mT")
        mTb = P.sb([128, KC, 256], BF16, "mTb")
        tmpc = Rot([P.sb([128, 512], F32, "tmpc") for _ in range(2)])
        xt2 = Rot([P.sb([128, 2, D], F32, "xt2") for _ in range(1)])
        hid = P.sb([128, 32, 256], BF16, "hid")
        w1_rot = Rot([P.sb([128, KC, 512], BF16, "w1g") for _ in range(2)])
        w2_rot = Rot([P.sb([128, 8, D], BF16, "w2g") for _ in range(2)])
        xt_rot = Rot([P.sb([128, 2, D], F32, "xt") for _ in range(2)])
        xn = P.sb([128, 2, D], BF16, "xn")
        hT_rot = Rot([P.sb([128, KC, 256], BF16, "hT") for _ in range(1)])
        gfin = P.sb([128, D], F32, "gfin")
        if last:
            P.dma("sp", gfin[:], g_final.broadcast_to([128, D]), wr=[gfin])
        for s in range(NS):
            for ti, e0, n in [(ti, e0 + o, min(256, n - o)) for ti, (e0, n) in enumerate(cfg.tiles) for o in range(0, n, 256)]:
                if last and e0 < CTX:
                    continue
                nsub = n // 128
                v = 2 if e0 < CTX else s
                xt = xt_rot.next()
                P.dma("sp", xt[:, 0:nsub, :], x_src(l, s, e0, n).rearrange("(a p) d -> p a d", p=128),
                      rd=[L_X[s][ti]], wr=[xt])
                yt, gt = yt_rot.next(), gt_rot.next()
                P.dma("pool", yt[:, :, 0:n], YBR[s, :, :, :, e0:e0 + n].rearrange("b j p t -> p (b j) t"),
                      rd=L_Y[s][0][ti:ti + 1] + L_Y[s][1][ti:ti + 1] + L_Y[s][2][ti:ti + 1], wr=[yt])
                P.dma("pool", gt[:, :, 0:n], U[s, 40:64, :, e0:e0 + n].rearrange("c p t -> p c t"), rd=[L_U[s][ti]], wr=[gt])
                P.op("act", lambda e: e.activation(out=gt[:, :, 0:n], in_=gt[:, :, 0:n], func=AF.Sigmoid), rd=[gt], wr=[gt])
                for oc in range(KC):
                    for br in range(3):
                        ps = psf.next()
                        for j in range(4):
                            P.op("pe", lambda e: e.matmul(ps[:, 0:n], pw[:, br, j, oc * 128:(oc + 1) * 128], yt[:, br * 4 + j, 0:n],
                                                          start=(j == 0), stop=(j == 3)), rd=[pw, yt], wr=[ps])
                        if br == 0:
                            P.op("dve", lambda e: e.tensor_tensor(mT[:, oc, 0:n], ps[:, 0:n], gt[:, br * 8 + oc, 0:n], ALU.mult),
                                 rd=[ps, gt], wr=[mT])
                        else:
                            tc_ = tmpc.next()
                            P.op("dve", lambda e: e.tensor_tensor(tc_[:, 0:n], ps[:, 0:n], gt[:, br * 8 + oc, 0:n], ALU.mult),
                                 rd=[ps, gt], wr=[tc_])
                            P.op("pool", lambda e: e.tensor_tensor(mT[:, oc, 0:n], mT[:, oc, 0:n], tc_[:, 0:n], ALU.add),
                                 rd=[tc_, mT], wr=[mT])
                    P.op("act", lambda e: e.copy(mTb[:, oc, 0:n], mT[:, oc, 0:n]), rd=[mT], wr=[mTb])
                if cfg.c_stop <= 2:
                    continue
                x2 = xt2.next()
                for sub in range(nsub):
                    for half in range(2):
                        ps = psf.next()
                        for k in range(KC):
                            P.op("pe", lambda e: e.matmul(ps[:, :], mTb[:, k, sub * 128:(sub + 1) * 128],
                                                          wo[:, k, half * 512:(half + 1) * 512], start=(k == 0), stop=(k == KC - 1)),
                                 rd=[mTb, wo], wr=[ps])
                        tc_ = tmpc.next()
                        P.op("dve", lambda e: e.tensor_tensor(tc_[:], ps[:], mgrow[v][0][:, half * 512:(half + 1) * 512], ALU.mult),
                             rd=[ps, mgrow[v][0]], wr=[tc_])
                        P.op("pool", lambda e: e.tensor_tensor(x2[:, sub, half * 512:(half + 1) * 512], tc_[:],
                                                               xt[:, sub, half * 512:(half + 1) * 512], ALU.add),
                             rd=[tc_, xt], wr=[x2])
                if cfg.c_stop <= 3:
                    continue
                hT = hT_rot.next()
                norm_T(x2, nsub, lambda k: g2c[:, v, k:k + 1], lambda k: modc[:, 3 * 8 + k, v:v + 1], hT, xn)
                for g in range(8):
                    w1 = w1_rot.next()
                    P.dma("sp", w1[:], WB_1[l, :, g * 512:(g + 1) * 512].rearrange("(k p) c -> p k c", p=128),
                          rd=[L_WB[l]], wr=[w1])
                    for c4 in range(4):
                        hc = g * 4 + c4
                        ps = psf.next()
                        for k in range(KC):
                            P.op("pe", lambda e: e.matmul(ps[:, 0:n], w1[:, k, c4 * 128:(c4 + 1) * 128], hT[:, k, 0:n],
                                                          start=(k == 0), stop=(k == KC - 1)), rd=[w1, hT], wr=[ps])
                        tc_ = tmpc.next()
                        P.op("act", lambda e: e.activation(out=tc_[:, 0:n], in_=ps[:, 0:n], func=AF.Relu), rd=[ps], wr=[tc_])
                        P.op("dve" if hc % 2 else "pool", lambda e: e.tensor_tensor(hid[:, hc, 0:n], tc_[:, 0:n], tc_[:, 0:n], ALU.mult),
                             rd=[tc_], wr=[hid])
                if cfg.c_stop <= 4:
                    continue
                xo = xt_rot.next()
                pss = [[psf.next() for half in range(2)] for sub in range(2)]
                for sp in range(0, nsub, 2):
                    subs = list(range(sp, min(sp + 2, nsub)))
                    for g in range(4):
                        w2 = w2_rot.next()
                        P.dma("sp", w2[:], WB_2[l, g * 1024:(g + 1) * 1024, :].rearrange("(k p) d -> p k d", p=128),
                              rd=[L_WB[l]], wr=[w2])
                        for kk_ in range(8):
                            hc = g * 8 + kk_
                            for sub in subs:
                                for half in range(2):
                                    ps = pss[sub - sp][half]
                                    P.op("pe", lambda e: e.matmul(ps[:, :], hid[:, hc, sub * 128:(sub + 1) * 128],
                                                                  w2[:, kk_, half * 512:(half + 1) * 512],
                                                                  start=(hc == 0), stop=(hc == 31)), rd=[hid, w2], wr=[ps])
                    for sub in subs:
                        for half in range(2):
                            ps = pss[sub - sp][half]
                            tc_ = tmpc.next()
                            P.op("dve", lambda e: e.tensor_tensor(tc_[:], ps[:], mgrow[v][1][:, half * 512:(half + 1) * 512], ALU.mult),
                                 rd=[ps, mgrow[v][1]], wr=[tc_])
                            P.op("pool", lambda e: e.tensor_tensor(xo[:, sub, half * 512:(half + 1) * 512], tc_[:],
                                                                   x2[:, sub, half * 512:(half + 1) * 512], ALU.add),
                                 rd=[tc_, x2], wr=[xo])
                if cfg.c_stop <= 5:
                    continue
                if last:
                    ss = ss_t.next()
                    for sub in range(nsub):
                        P.op("act", lambda e: e.activation(out=junk[:], in_=xo[:, sub, :], func=AF.Square,
                                                           accum_out=ss[:, sub:sub + 1]), rd=[xo], wr=[junk, ss])
                    rsqrt(ss[:, 4:4 + nsub], ss[:, 0:nsub], 1.0 / D, NORM_EPS, [ss], [ss])
                    for sub in range(nsub):
                        P.op("dve", lambda e: e.scalar_tensor_tensor(xo[:, sub, :], xo[:, sub, :], ss[:, 4 + sub:5 + sub], gfin[:],
                                                                      ALU.mult, ALU.mult), rd=[xo, ss, gfin], wr=[xo])
                if cfg.c_stop <= 6:
                    continue
                for sub in range(nsub):
                    P.dma("sp", x_dst(s, e0 + sub * 128, 128), xo[:, sub, :], rd=[xo], wr=[L_X[s][ti]])
        P.scope_end(mark)

    def stats_rstd(x_f32, nj, n, ones_l, eps, inv_n, scr):
        lhs, full = ones_l
        xb, sq = scr["xb"], scr["sq"]
        for j in range(nj):
            P.op("act", lambda e: e.copy(xb[:, j, 0:n], x_f32(j)), rd=scr["rd"], wr=[xb])
            P.op("act", lambda e: e.activation(out=sq[:, j, 0:n], in_=x_f32(j), func=AF.Square), rd=scr["rd"], wr=[sq])
        groups = [list(range(nj))] if full else [[j] for j in range(nj)]
        for gi, g in enumerate(groups):
            ps1, ps2 = psf.next(), psf.next()
            for a, j in enumerate(g):
                P.op("pe", lambda e: e.matmul(ps1[:, 0:n], lhs[:], xb[:, j, 0:n], start=(a == 0), stop=(a == len(g) - 1)),
                     rd=[lhs, xb], wr=[ps1])
            for a, j in enumerate(g):
                P.op("pe", lambda e: e.matmul(ps2[:, 0:n], lhs[:], sq[:, j, 0:n], start=(a == 0), stop=(a == len(g) - 1)),
                     rd=[lhs, sq], wr=[ps2])
            mean, rstd = scr["mean"], scr["rstd"]
            P.op("act", lambda e: e.mul(mean[:, gi, 0:n], ps1[:, 0:n], inv_n), rd=[ps1], wr=[mean])
            P.op("dve", lambda e: e.tensor_tensor(rstd[:, gi, 0:n], mean[:, gi, 0:n], mean[:, gi, 0:n], ALU.mult),
                 rd=[mean], wr=[rstd])
            P.op("dve", lambda e: e.scalar_tensor_tensor(rstd[:, gi, 0:n], ps2[:, 0:n], inv_n, rstd[:, gi, 0:n],
                                                          ALU.mult, ALU.subtract), rd=[ps2, rstd], wr=[rstd])
            rsqrt(rstd[:, gi, 0:n], rstd[:, gi, 0:n], 1.0, eps, [rstd], [rstd])

    def conv_setup():
        ug = Rot([P.sb([128, 8, 512], BF16, "ug") for _ in range(2)])
        zp = P.sb([128, 4, 8 * 94], F32, "zp")
        zc = P.sb([128, 4, 542], F32, "zc")
        sg = P.sb([128, 4, 512], F32, "sg")
        acc = P.sb([128, 4, 512], F32, "acc")
        scr = {"xb": P.sb([128, 4, 512], BF16, "xb"), "sq": P.sb([128, 4, 512], BF16, "sq"),
               "mean": P.sb([128, 1, 512], F32, "mean"), "rstd": P.sb([128, 1, 512], F32, "rstd"), "rd": [acc]}
        yo = Rot([P.sb([128, 4, 512], BF16, "yo") for _ in range(2)])
        P.op("dve", lambda e: e.memset(zp[:], 0.0), wr=[zp])
        P.op("dve", lambda e: e.memset(zc[:], 0.0), wr=[zc])
        o_dw = PCOLS["cdw"][0]

        def conv_tile(l, last, s, ti):
                e0, n = cfg.tiles[ti]
                isctx = e0 < CTX
                if last and isctx:
                    return
                yield
                u = ug.next()
                P.dma("sp", u[:, :, 0:n], U[s, 15:23, :, e0:e0 + n].rearrange("c p t -> p c t"), rd=[L_U[s][ti]], wr=[u])
                P.op("act", lambda e: e.activation(out=sg[:, :, 0:n], in_=u[:, 4:8, 0:n], func=AF.Sigmoid), rd=[u], wr=[sg])
                if isctx:
                    assert CTX <= 512 and n == CTX
                    zbuf = zc
                    zin = zc[:, :, 15:15 + n]
                    P.op("dve", lambda e: e.tensor_tensor(zin, u[:, 0:4, 0:n], sg[:, :, 0:n], ALU.mult), rd=[u, sg], wr=[zc])
                    win = lambda j, tau: zc[:, j, tau:tau + n]
                    av = lambda j: acc[:, j, 0:n]
                else:
                    nr = n // 64
                    zbuf = zp
                    zin = zp[:, :, 0:nr * 94].rearrange("p j (r w) -> p j r w", w=94)[:, :, :, 15:79]
                    P.op("dve", lambda e: e.tensor_tensor(zin, u[:, 0:4, 0:n].rearrange("p j (r w) -> p j r w", w=64),
                                                           sg[:, :, 0:n].rearrange("p j (r w) -> p j r w", w=64), ALU.mult),
                         rd=[u, sg], wr=[zp])
                    win = lambda j, tau: zp[:, j, 0:nr * 94].rearrange("p (r w) -> p r w", w=94)[:, :, tau:tau + 64]
                    av = lambda j: acc[:, j, 0:n].rearrange("p (r w) -> p r w", w=64)
                for j in range(4):
                    eng = "dve"
                    P.op(eng, lambda e: e.tensor_scalar(av(j), win(j, 0), pc[:, o_dw + j * 31:o_dw + j * 31 + 1],
                                                        pcol("cdb", j), ALU.mult, ALU.add), rd=[zbuf, pc], wr=[acc])
                    for tau in range(1, 31):
                        P.op(eng, lambda e: e.scalar_tensor_tensor(av(j), win(j, tau),
                                                                   pc[:, o_dw + j * 31 + tau:o_dw + j * 31 + tau + 1],
                                                                   av(j), ALU.mult, ALU.add), rd=[zbuf, pc, acc], wr=[acc])
                        if tau % 9 == 0:
                            yield
                yield
                stats_rstd(lambda j: acc[:, j, 0:n], 4, n, (ones_bf, True), LN_EPS, 1.0 / 512, scr)
                yield
                y = yo.next()
                for j in range(4):
                    P.op("dve", lambda e: e.tensor_tensor(acc[:, j, 0:n], acc[:, j, 0:n], scr["mean"][:, 0, 0:n], ALU.subtract),
                         rd=[acc, scr["mean"]], wr=[acc])
                    P.op("dve", lambda e: e.tensor_tensor(acc[:, j, 0:n], acc[:, j, 0:n], scr["rstd"][:, 0, 0:n], ALU.mult),
                         rd=[acc, scr["rstd"]], wr=[acc])
                    P.op("act", lambda e: e.activation(out=y[:, j, 0:n], in_=acc[:, j, 0:n], func=AF.Silu,
                                                       scale=pcol("clnw", j), bias=pcol("clnb", j)), rd=[acc, pc], wr=[y])
                P.dma("pool", YBR[s, 1, :, :, e0:e0 + n].rearrange("j p t -> p j t"), y[:, :, 0:n], rd=[y], wr=[L_Y[s][1][ti]])
        return conv_tile

    HM = dscr("HM", [NS, 2, TC, 512], F32)
    L_HM = [[LT() for d in range(2)] for s in range(NS)]
    selh_d = din("selh", [4, 4 * 128])
    NCK = TC // 64
    DH5 = 128 ** -0.5

    def mlstm_branch(l, last):
        mark = P.scope_begin()
        selh = P.sb([4, 4 * 128], F32, "selh")
        P.dma("sp", selh[:], selh_d, wr=[selh])
        mb = P.sb([4, 4], F32, "mb")
        for d in range(2):
            P.dma("sp", mb[:, 2 * d:2 * d + 2], mlb[l, d * 4:(d + 1) * 4, :], wr=[mb])
        mk = P.sb([64, 2, 64], F32, "mk")
        P.dma("sp", mk[:, 0, :], masks_d[1], wr=[mk])
        P.dma("sp", mk[:, 1, :], masks_d[3], wr=[mk])
        P.op("dve", lambda e: e.tensor_scalar(mk[:], mk[:], DH5, None, ALU.mult), rd=[mk], wr=[mk])
        PAD = 64
        gb = P.sb([4, 2, TC], BF16, "gb")
        IG = P.sb([4, TC], F32, "IG")
        Bc = P.sb([4, TC], F32, "Bc")
        G = P.sb([4, TC], F32, "G")
        MU = P.sb([4, TC + 2 * PAD], F32, "MU")
        Q2_ = P.sb([4, TC], F32, "Q2")
        Q = [G, IG, Q2_, Bc]
        CAR = P.sb([4, NCK], F32, "CAR")
        carb = P.sb([128, 4, NCK], F32, "carb")
        qk_rot = Rot([P.sb([128, 8, 512], BF16, "qk") for _ in range(1)])
        vk_rot = Rot([P.sb([128, 4, 512], BF16, "vk") for _ in range(2)])
        vtm = P.sb([64, 8, 4, 130], BF16, "vtm")
        ktm = P.sb([64, 8, 4, 128], BF16, "ktm")
        cols = Rot([P.sb([64, 16], F32, "cols") for _ in range(2)])
        Sb = Rot([P.sb([64, 64], BF16, "Sb") for _ in range(2)])
        Vp = Rot([P.sb([64, 130], BF16, "Vp") for _ in range(2)])
        tmpn = Rot([P.sb([64, 130], F32, "tmpn") for _ in range(2)])
        ne = Rot([P.sb([64, 130], F32, "ne") for _ in range(2)])
        dn = Rot([P.sb([64, 2], F32, "dn") for _ in range(2)])
        hst = Rot([P.sb([64, 8, 512], F32, "hst") for _ in range(1)])
        Cf = P.sb([128, 4, 130], F32, "Cf")
        Cb = P.sb([128, 4, 130], BF16, "Cb")
        Sb4 = P.sb([64, 4, 64], BF16, "Sb4")
        tn4 = P.sb([64, 4, 130], F32, "tn4")
        nn4 = P.sb([64, 4, 130], F32, "nn4")
        dd4 = P.sb([64, 2, 4], F32, "dd4")
        vp4 = P.sb([64, 4, 130], BF16, "vp4")
        P.op("dve", lambda e: e.memset(vtm[:], 1.0), wr=[vtm])
        for s in range(NS):
            for d in range(2):
                def seg(e_lo, e_hi):
                    if d == 0:
                        return slice(e_lo, e_hi)
                    return slice(e_lo - CTX, e_hi - CTX) if e_lo >= CTX else slice(T + e_lo, T + e_hi)
                for (lo, hi) in ((0, CTX), (CTX, TC)):
                    for gi in range(2):
                        P.dma("sp", gb[:, gi, seg(lo, hi)], U[s, 39, d * 8 + gi * 4:d * 8 + gi * 4 + 4, lo:hi],
                              rd=[t_ for t_ in L_U[s]], wr=[gb])
                P.op("act", lambda e: e.activation(out=IG[:], in_=gb[:, 0, :], func=AF.Identity, bias=mb[:, 2 * d:2 * d + 1]),
                     rd=[gb, mb], wr=[IG])
                P.op("act", lambda e: e.activation(out=Bc[:], in_=gb[:, 1, :], func=AF.Sigmoid, bias=mb[:, 2 * d + 1:2 * d + 2]),
                     rd=[gb, mb], wr=[Bc])
                P.op("act", lambda e: e.activation(out=Bc[:], in_=Bc[:], func=AF.Ln), rd=[Bc], wr=[Bc])
                rv = (lambda ap: ap) if d == 0 else (lambda ap: ap[:, ::-1])
                P.op("dve", lambda e: e.memset(MU[:], 1.0), wr=[MU])
                P.op("dve", lambda e: e.tensor_tensor_scan(rv(G[:]), rv(MU[:, 0:TC]), rv(Bc[:]), 0.0, ALU.mult, ALU.add),
                     rd=[MU, Bc], wr=[G])
                P.op("dve", lambda e: e.memset(MU[:], 0.0), rd=[G], wr=[MU])
                P.op("dve", lambda e: e.tensor_copy(Bc[:], G[:]), rd=[G], wr=[Bc])
                P.op("dve", lambda e: e.tensor_tensor(G[:], IG[:], Bc[:], ALU.subtract), rd=[IG, Bc], wr=[G])
                mu = MU[:, PAD:PAD + TC]
                P.op("dve", lambda e: e.tensor_tensor_scan(rv(mu), rv(G[:]), rv(G[:]), 0.0, ALU.max, ALU.max),
                     rd=[G], wr=[MU])
                mu3 = mu.rearrange("p (c t) -> p c t", t=64)
                if d == 0:
                    mu_last = mu3[:, :, 63:64]
                    mu_ent = MU[:, PAD - 1:PAD - 1 + TC].rearrange("p (c t) -> p c t", t=64)[:, :, 0:1]
                else:
                    mu_last = mu3[:, :, 0:1]
                    mu_ent = MU[:, PAD + 64:PAD + 64 + TC].rearrange("p (c t) -> p c t", t=64)[:, :, 0:1]
                v3 = lambda b: b[:].rearrange("p (c t) -> p c t", t=64)
                bc3 = lambda ap: ap.broadcast_to([4, NCK, 64])
                P.op("dve", lambda e: e.tensor_tensor(v3(Q[0]), v3(G), bc3(mu_last), ALU.subtract), rd=[G, MU], wr=[Q[0]])
                P.op("act", lambda e: e.activation(out=Q[0][:], in_=Q[0][:], func=AF.Exp), rd=[Q[0]], wr=[Q[0]])
                P.op("dve", lambda e: e.tensor_tensor(v3(Q[1]), bc3(mu_last), mu3, ALU.subtract), rd=[MU], wr=[Q[1]])
                P.op("act", lambda e: e.activation(out=Q[1][:], in_=Q[1][:], func=AF.Exp), rd=[Q[1]], wr=[Q[1]])
                P.op("dve", lambda e: e.tensor_tensor(v3(Q[2]), bc3(mu_ent), mu3, ALU.subtract), rd=[MU], wr=[Q[2]])
                P.op("act", lambda e: e.activation(out=Q[2][:], in_=Q[2][:], func=AF.Exp), rd=[Q[2]], wr=[Q[2]])
                P.op("dve", lambda e: e.tensor_tensor(Q[3][:], Bc[:], mu, ALU.add), rd=[Bc, MU], wr=[Q[3]])
                P.op("act", lambda e: e.activation(out=Q[3][:], in_=Q[3][:], func=AF.Exp, scale=-1.0), rd=[Q[3]], wr=[Q[3]])
                P.op("dve", lambda e: e.tensor_tensor(CAR[:].unsqueeze(2), mu_ent, mu_last, ALU.subtract), rd=[MU], wr=[CAR])
                P.op("act", lambda e: e.activation(out=CAR[:], in_=CAR[:], func=AF.Exp), rd=[CAR], wr=[CAR])
                for h in range(4):
                    ps = psf.next()
                    P.op("pe", lambda e: e.matmul(ps[:, 0:NCK], selh[:, h * 128:(h + 1) * 128], CAR[:], start=True, stop=True),
                         rd=[selh, CAR], wr=[ps])
                    P.op("dve", lambda e: e.tensor_copy(carb[:, h, :], ps[:, 0:NCK]), rd=[ps], wr=[carb])
                P.op("dve", lambda e: e.memset(Cf[:], 0.0), wr=[Cf])
                P.op("dve", lambda e: e.memset(Cb[:], 0.0), wr=[Cb])
                order = list(range(len(cfg.tiles)))
                nctx_t = sum(1 for (e0, n) in cfg.tiles if e0 < CTX)
                if d == 1:
                    order = list(range(nctx_t - 1, -1, -1)) + list(range(len(cfg.tiles) - 1, nctx_t - 1, -1))
                for ti in order:
                    e0, n = cfg.tiles[ti]
                    nck = n // 64
                    qk, vk = qk_rot.next(), vk_rot.next()
                    P.dma("sp", qk[:, :, 0:n], U[s, 23:31, :, e0:e0 + n].rearrange("c p t -> p c t"), rd=[L_U[s][ti]], wr=[qk])
                    P.dma("pool", vk[:, 0:4, 0:n], U[s, 31:35, :, e0:e0 + n].rearrange("c p t -> p c t"), rd=[L_U[s][ti]], wr=[vk])
                    for ck in range(nck):
                        for (src, srcbuf, dst, w) in ((lambda h: vk[:, h, ck * 64:(ck + 1) * 64], vk, vtm, 130),
                                                      (lambda h: qk[:, 4 + h, ck * 64:(ck + 1) * 64], qk, ktm, 128)):
                            pb = psb.next()
                            for h in range(4):
                                P.op("pe", lambda e: e.transpose(pb[0:64, h * 128:(h + 1) * 128], src(h), ident_bf[:]),
                                     rd=[srcbuf, ident_bf], wr=[pb])
                            P.op("act", lambda e: e.copy(dst[:, ck, :, 0:128], pb[0:64, 0:512].rearrange("p (h c) -> p h c", c=128)),
                                 rd=[pb], wr=[dst])
                    hs = hst.next()
                    ckorder = range(nck) if d == 0 else range(nck - 1, -1, -1)
                    for ck in ckorder:
                        ec = (e0 + ck * 64)
                        mc = seg(ec, ec + 64)
                        cidx = mc.start // 64
                        cl = cols.next()
                        psq = psf.next()
                        for q in range(4):
                            P.op("pe", lambda e: e.matmul(psq[0:64, q * 4:(q + 1) * 4], Q[q][:, mc], ident_f[0:4, 0:4],
                                                          start=True, stop=True), rd=[Q[q], ident_f], wr=[psq])
                        P.op("dve", lambda e: e.tensor_copy(cl[:], psq[0:64, 0:16]), rd=[psq], wr=[cl])
                        colq = lambda q, h: cl[:, q * 4 + h:q * 4 + h + 1]
                        H4 = range(4)
                        reg = lambda pp, h: pp[h // 2][0:64, (h % 2) * 130:(h % 2) * 130 + 129]
                        for h in H4:
                            P.op("dve", lambda e: e.tensor_scalar(vp4[:, h, 0:129], vtm[:, ck, h, 0:129], colq(0, h), DH5, ALU.mult, ALU.mult),
                                 rd=[vtm, cl], wr=[vp4])
                        psS = psf.next()
                        for h in H4:
                            P.op("pe", lambda e: e.matmul(psS[0:64, h * 64:(h + 1) * 64], qk[:, 4 + h, ck * 64:(ck + 1) * 64],
                                                          qk[:, h, ck * 64:(ck + 1) * 64], start=True, stop=True), rd=[qk], wr=[psS])
                        psC = [psf.next(), psf.next()]
                        for h in H4:
                            P.op("pe", lambda e: e.matmul(reg(psC, h), qk[:, h, ck * 64:(ck + 1) * 64], Cb[:, h, 0:129],
                                                          start=True, stop=True), rd=[qk, Cb], wr=[psC[h // 2]])
                        psU = [psf.next(), psf.next()]
                        regU = lambda h: psU[h // 2][:, (h % 2) * 130:(h % 2) * 130 + 129]
                        for h in H4:
                            P.op("pe", lambda e: e.matmul(regU(h), ktm[:, ck, h, :], vp4[:, h, 0:129], start=True, stop=True),
                                 rd=[ktm, vp4], wr=[psU[h // 2]])
                        for h in H4:
                            P.op("dve", lambda e: e.scalar_tensor_tensor(Sb4[:, h, :], psS[0:64, h * 64:(h + 1) * 64], colq(0, h), mk[:, d, :],
                                                                          ALU.mult, ALU.mult), rd=[psS, cl, mk], wr=[Sb4])
                        for h in H4:
                            P.op("act", lambda e: e.activation(out=tn4[:, h, 0:129], in_=reg(psC, h), func=AF.Copy, scale=colq(2, h)),
                                 rd=[psC[h // 2], cl], wr=[tn4])
                        for h in H4:
                            P.op("dve", lambda e: e.scalar_tensor_tensor(Cf[:, h, 0:129], Cf[:, h, 0:129],
                                                                          carb[:, h, cidx:cidx + 1], regU(h),
                                                                          ALU.mult, ALU.add), rd=[Cf, carb, psU[h // 2]], wr=[Cf])
                        P.op("act", lambda e: e.copy(Cb[:], Cf[:]), rd=[Cf], wr=[Cb])
                        psI = [psf.next(), psf.next()]
                        for h in H4:
                            P.op("pe", lambda e: e.matmul(reg(psI, h), Sb4[:, h, :], vtm[:, ck, h, 0:129], start=True, stop=True),
                                 rd=[Sb4, vtm], wr=[psI[h // 2]])
                        for h in H4:
                            P.op("dve", lambda e: e.scalar_tensor_tensor(nn4[:, h, 0:129], reg(psI, h), colq(1, h), tn4[:, h, 0:129],
                                                                          ALU.mult, ALU.add), rd=[psI[h // 2], cl, tn4], wr=[nn4])
                        P.op("dve", lambda e: e.scalar_tensor_tensor(dd4[:, 0, :], nn4[:, :, 128], -1.0, nn4[:, :, 128],
                                                                      ALU.mult, ALU.max), rd=[nn4], wr=[dd4])
                        P.op("dve", lambda e: e.tensor_tensor(dd4[:, 0, :], dd4[:, 0, :], cl[:, 12:16], ALU.max), rd=[dd4, cl], wr=[dd4])
                        P.op("dve", lambda e: e.reciprocal(dd4[:, 1, :], dd4[:, 0, :]), rd=[dd4], wr=[dd4])
                        for h in H4:
                            P.op("act", lambda e: e.activation(out=hs[:, ck, h * 128:(h + 1) * 128], in_=nn4[:, h, 0:128],
                                                               func=AF.Copy, scale=dd4[:, 1, h:h + 1]), rd=[nn4, dd4], wr=[hs])
                    P.dma("sp", HM[s, d, e0:e0 + n, :].rearrange("(c t) f -> t c f", t=64), hs[:, 0:nck, :], rd=[hs], wr=[L_HM[s][d]])
        P.scope_end(mark)
        mark = P.scope_begin()
        hf = Rot([P.sb([128, 512], F32, "hf") for _ in range(2)])
        hb = Rot([P.sb([128, 512], F32, "hb") for _ in range(2)])
        hnb = Rot([P.sb([128, 512], BF16, "hnb") for _ in range(2)])
        st = Rot([P.sb([128, 16], F32, "st") for _ in range(2)])
        og = Rot([P.sb([128, 4, 512], BF16, "og") for _ in range(2)])
        yc = Rot([P.sb([128, 4, 512], BF16, "yc") for _ in range(2)])
        for s in range(NS):
            for ti, (e0, n) in enumerate(cfg.tiles):
                if last and e0 < CTX:
                    continue
                o_ = og.next()
                P.dma("sp", o_[:, :, 0:n], U[s, 35:39, :, e0:e0 + n].rearrange("c p t -> p c t"), rd=[L_U[s][ti]], wr=[o_])
                P.op("act", lambda e: e.activation(out=o_[:, :, 0:n], in_=o_[:, :, 0:n], func=AF.Sigmoid), rd=[o_], wr=[o_])
                y = yc.next()
                for sub in range(n // 128):
                    a, b, hn, t_ = hf.next(), hb.next(), hnb.next(), st.next()
                    r0 = e0 + sub * 128
                    P.dma("sp", a[:], HM[s, 0, r0:r0 + 128, :], rd=[L_HM[s][0]], wr=[a])
                    P.dma("pool", b[:], HM[s, 1, r0:r0 + 128, :], rd=[L_HM[s][1]], wr=[b])
                    P.op("dve", lambda e: e.tensor_tensor(a[:], a[:], b[:], ALU.add), rd=[a, b], wr=[a])
                    P.op("dve", lambda e: e.reduce_sum(t_[:, 0:4], a[:].rearrange("p (h c) -> p h c", c=128), AX.X), rd=[a], wr=[t_])
                    for h in range(4):
                        P.op("act", lambda e: e.activation(out=b[:, h * 128:(h + 1) * 128], in_=a[:, h * 128:(h + 1) * 128],
                                                           func=AF.Square, accum_out=t_[:, 4 + h:5 + h]), rd=[a], wr=[b, t_])
                    P.op("dve", lambda e: e.tensor_scalar(t_[:, 0:4], t_[:, 0:4], 1.0 / 128, None, ALU.mult), rd=[t_], wr=[t_])
                    P.op("dve", lambda e: e.tensor_tensor(t_[:, 8:12], t_[:, 0:4], t_[:, 0:4], ALU.mult), rd=[t_], wr=[t_])
                    P.op("dve", lambda e: e.scalar_tensor_tensor(t_[:, 8:12], t_[:, 4:8], 1.0 / 128, t_[:, 8:12], ALU.mult, ALU.subtract),
                         rd=[t_], wr=[t_])
                    rsqrt(t_[:, 12:16], t_[:, 8:12], 1.0, LN_EPS, [t_], [t_])
                    for h in range(4):
                        P.op("dve", lambda e: e.tensor_scalar(hn[:, h * 128:(h + 1) * 128], a[:, h * 128:(h + 1) * 128],
                                                              t_[:, h:h + 1], t_[:, 12 + h:13 + h], ALU.subtract, ALU.mult),
                             rd=[a, t_], wr=[hn])
                    pb = psb.next()
                    for h in range(4):
                        P.op("pe", lambda e: e.transpose(pb[:, h * 128:(h + 1) * 128], hn[:, h * 128:(h + 1) * 128], ident_bf[:]),
                             rd=[hn, ident_bf], wr=[pb])
                    for h in range(4):
                        P.op("dve", lambda e: e.scalar_tensor_tensor(y[:, h, sub * 128:(sub + 1) * 128], pb[:, h * 128:(h + 1) * 128],
                                                                      pcol("mnw", h), o_[:, h, sub * 128:(sub + 1) * 128],
                                                                      ALU.mult, ALU.mult), rd=[pb, pc, o_], wr=[y])
                P.dma("sp", YBR[s, 2, :, :, e0:e0 + n].rearrange("j p t -> p j t"), y[:, :, 0:n], rd=[y], wr=[L_Y[s][2][ti]])
        P.scope_end(mark)

    RF = dscr("RF", [NS, 2, 4, 128, NCK, 4, 64], BF16)
    PLs = dscr("PLs", [NS, 2, 128, 4, NCK], F32)
    VT = dscr("VT", [NS, TC, 512], BF16)
    GB = dscr("GB", [NS, 2, 4, 128, TC], BF16)
    YR = dscr("YR", [NS, 2, 4, 128, TC], F32)
    L_RF = [[[LT() for _ in cfg.tiles] for d in range(2)] for s in range(NS)]
    L_VT = [[LT() for _ in cfg.tiles] for s in range(NS)]
    L_GB = [[LT() for _ in cfg.tiles] for s in range(NS)]
    L_YR = [[[LT() for _ in cfg.tiles] for d in range(2)] for s in range(NS)]
    C0 = float(np.exp(-0.5))

    def rwkv_branch(l, last):
        mark = P.scope_begin()
        wtmp = P.sb([128, 512], F32, "wtmp")
        w2p = P.sb([128, 2, 512], BF16, "w2p")
        a2p = P.sb([128, 2, 512], BF16, "a2p")
        g2b = P.sb([128, 512], BF16, "g2b")
        P.op("dve", lambda e: e.memset(w2p[:], 0.0), wr=[w2p])
        P.op("dve", lambda e: e.memset(a2p[:], 0.0), wr=[a2p])
        for (src, dst) in ((rw_w2, w2p), (rw_a2, a2p)):
            P.dma("sp", wtmp[:], src[l], wr=[wtmp])
            for d in range(2):
                P.op("dve", lambda e: e.tensor_copy(dst[d * 64:(d + 1) * 64, d, :], wtmp[d * 64:(d + 1) * 64, :]), rd=[wtmp], wr=[dst])
        P.dma("sp", wtmp[:], rw_g2[l], wr=[wtmp])
        P.op("dve", lambda e: e.tensor_copy(g2b[:], wtmp[:]), rd=[wtmp], wr=[g2b])
        coef0 = P.sb([128, 15], F32, "coef0")
        omka = P.sb([128, 4], F32, "omka")
        P.op("dve", lambda e: e.tensor_tensor(coef0[:], pcol("mu_p", 0, 15), pcol("mu_n", 0, 15), ALU.add), rd=[pc], wr=[coef0])
        P.op("dve", lambda e: e.tensor_scalar(coef0[:], coef0[:], -1.0, 1.0, ALU.mult, ALU.add), rd=[coef0], wr=[coef0])
        P.op("dve", lambda e: e.tensor_scalar(omka[:], pcol("ka", 0, 4), -1.0, 1.0, ALU.mult, ALU.add), rd=[pc], wr=[omka])
        rmask = P.sb([128, 2, 512], F32, "rmask")
        P.op("dve", lambda e: e.memset(rmask[:], 1.0), wr=[rmask])
        P.op("dve", lambda e: e.memset(rmask[:, 0, :].rearrange("p (c t) -> p c t", t=64)[:, :, 0:1], 0.0), wr=[rmask])
        P.op("dve", lambda e: e.memset(rmask[:, 1, :].rearrange("p (c t) -> p c t", t=64)[:, :, 63:64], 0.0), wr=[rmask])

        mark1 = P.scope_begin()
        ush = P.sb([128, 15, 514], BF16, "ush")
        xs = P.sb([128, 15, 512], F32, "xs")
        tmp = P.sb([128, 15, 512], F32, "tmp")
        sig = [P.sb([128, 4, 512], F32, "sig")] * 2
        aa = [P.sb([128, 4, 512], F32, "aa")] * 2
        gate = P.sb([128, 4, 512], F32, "gate")
        kkn = P.sb([128, 4, 512], F32, "kkn")
        kd = P.sb([128, 4, 512], F32, "kd")
        rf = P.sb([128, 4, 8, 4, 64], BF16, "rf")
        lb = P.sb([128, 3, 512], BF16, "lb")
        b4 = P.sb([128, 4, 512], BF16, "b4")
        gbt = P.sb([128, 2, 4, 512], BF16, "gbt")
        vtm = P.sb([128, 4, 512], BF16, "vtmr")
        plt = P.sb([128, 4, 8], F32, "plt")

        def bc(col_ap, nj, n):
            return col_ap.unsqueeze(2).broadcast_to([128, nj, n])

        for s in range(NS):
            for ti, (e0, n) in enumerate(cfg.tiles):
                nck = n // 64
                lo_edge = (e0 == 0 or e0 == CTX)
                hi_edge = (e0 + n == CTX or e0 + n == TC)
                P.dma("sp", ush[:, :, 1:n + 1], U[s, 0:15, :, e0:e0 + n].rearrange("c p t -> p c t"), rd=[L_U[s][ti]], wr=[ush])
                if lo_edge:
                    P.op("dve", lambda e: e.memset(ush[:, :, 0:1], 0.0), wr=[ush])
                else:
                    P.dma("sp", ush[:, :, 0:1], U[s, 0:15, :, e0 - 1:e0].rearrange("c p t -> p c t"), rd=[L_U[s][ti - 1]], wr=[ush],
                          allow_slow_non_contiguous=True)
                if hi_edge:
                    P.op("dve", lambda e: e.memset(ush[:, :, n + 1:n + 2], 0.0), wr=[ush])
                else:
                    P.dma("sp", ush[:, :, n + 1:n + 2], U[s, 0:15, :, e0 + n:e0 + n + 1].rearrange("c p t -> p c t"),
                          rd=[L_U[s][ti + 1]], wr=[ush], allow_slow_non_contiguous=True)
                X, Tm = xs[:, :, 0:n], tmp[:, :, 0:n]
                P.op("dve", lambda e: e.tensor_tensor(X, ush[:, :, 1:n + 1], bc(coef0[:], 15, n), ALU.mult), rd=[ush, coef0], wr=[xs])
                P.op("dve", lambda e: e.tensor_tensor(Tm, ush[:, :, 0:n], bc(pcol("mu_p", 0, 15), 15, n), ALU.mult), rd=[ush, pc], wr=[tmp])
                P.op("dve", lambda e: e.tensor_tensor(X, X, Tm, ALU.add), rd=[xs, tmp], wr=[xs])
                P.op("dve", lambda e: e.tensor_tensor(Tm, ush[:, :, 2:n + 2], bc(pcol("mu_n", 0, 15), 15, n), ALU.mult), rd=[ush, pc], wr=[tmp])
                P.op("dve", lambda e: e.tensor_tensor(X, X, Tm, ALU.add), rd=[xs, tmp], wr=[xs])
                r_, k_, v_ = xs[:, 0:4, 0:n], xs[:, 4:8, 0:n], xs[:, 8:12, 0:n]
                T0, T1, T2 = tmp[:, 0:4, 0:n], tmp[:, 4:8, 0:n], tmp[:, 8:12, 0:n]
                P.op("act", lambda e: e.activation(out=lb[:, 0, 0:n], in_=xs[:, 12, 0:n], func=AF.Tanh), rd=[xs], wr=[lb])
                P.op("act", lambda e: e.copy(lb[:, 1, 0:n], xs[:, 13, 0:n]), rd=[xs], wr=[lb])
                P.op("act", lambda e: e.activation(out=lb[:, 2, 0:n], in_=xs[:, 14, 0:n], func=AF.Sigmoid), rd=[xs], wr=[lb])
                for j in range(4):
                    ps = psf.next()
                    P.op("pe", lambda e: e.matmul(ps[:, 0:n], g2b[:, j * 128:(j + 1) * 128], lb[:, 2, 0:n], start=True, stop=True),
                         rd=[g2b, lb], wr=[ps])
                    P.op("act", lambda e: e.copy(gate[:, j, 0:n], ps[:, 0:n]), rd=[ps], wr=[gate])
                P.op("dve", lambda e: e.tensor_tensor(T0, k_, bc(pcol("kk", 0, 4), 4, n), ALU.mult), rd=[xs, pc], wr=[tmp])
                P.op("act", lambda e: e.activation(out=b4[:, :, 0:n], in_=T0, func=AF.Square), rd=[tmp], wr=[b4])
                for j in range(4):
                    ps = psf.next()
                    P.op("pe", lambda e: e.matmul(ps[:, 0:n], blk_bf[:], b4[:, j, 0:n], start=True, stop=True), rd=[blk_bf, b4], wr=[ps])
                    rsqrt(tmp[:, 4 + j, 0:n], ps[:, 0:n], 1.0, 1e-24, [ps], [tmp])
                P.op("dve", lambda e: e.tensor_tensor(kkn[:, :, 0:n], T0, T1, ALU.mult), rd=[tmp], wr=[kkn])
                P.op("dve", lambda e: e.tensor_tensor(T0, r_, k_, ALU.mult), rd=[xs], wr=[tmp])
                P.op("dve", lambda e: e.tensor_tensor(b4[:, :, 0:n], T0, bc(pcol("rk", 0, 4), 4, n), ALU.mult), rd=[tmp, pc], wr=[b4])
                for j in range(4):
                    ps = psf.next()
                    P.op("pe", lambda e: e.matmul(ps[:, 0:n], blk_bf[:], b4[:, j, 0:n], start=True, stop=True), rd=[blk_bf, b4], wr=[ps])
                    P.op("dve", lambda e: e.tensor_tensor(tmp[:, 4 + j, 0:n], ps[:, 0:n], xs[:, 8 + j, 0:n], ALU.mult), rd=[ps, xs], wr=[tmp])
                P.op("dve", lambda e: e.tensor_tensor(gbt[:, 1, :, 0:n], T1, gate[:, :, 0:n], ALU.mult), rd=[tmp, gate], wr=[gbt])
                P.op("act", lambda e: e.copy(gbt[:, 0, :, 0:n], gate[:, :, 0:n]), rd=[gate], wr=[gbt])
                for w in range(2):
                    P.dma("sp", GB[s, w, :, :, e0:e0 + n].rearrange("j p t -> p j t"), gbt[:, w, :, 0:n], rd=[gbt], wr=[L_GB[s][ti]])
                P.op("act", lambda e: e.copy(b4[:, :, 0:n], v_), rd=[xs], wr=[b4])
                for sub in range(n // 128):
                    pb = psb.next()
                    for j in range(4):
                        P.op("pe", lambda e: e.transpose(pb[:, j * 128:(j + 1) * 128], b4[:, j, sub * 128:(sub + 1) * 128], ident_bf[:]),
                             rd=[b4, ident_bf], wr=[pb])
                    P.op("dve", lambda e: e.tensor_copy(vtm[:, sub, :], pb[:, 0:512]), rd=[pb], wr=[vtm])
                P.dma("sp", VT[s, e0:e0 + n, :].rearrange("(a p) c -> p a c", p=128), vtm[:, 0:n // 128, :], rd=[vtm], wr=[L_VT[s][ti]])
                for d in range(2):
                    rv = (lambda ap: ap) if d == 0 else (lambda ap: ap[:, ::-1])
                    rfv = lambda kind: rf[:, :, 0:nck, kind, :]
                    v4 = lambda ap: ap.rearrange("p j (c t) -> p j c t", t=64)
                    for (wp, li, dst, bn) in ((w2p, 0, sig[d], "w0"), (a2p, 1, aa[d], "a0")):
                        for j in range(4):
                            ps = psf.next()
                            P.op("pe", lambda e: e.matmul(ps[:, 0:n], wp[:, d, j * 128:(j + 1) * 128], lb[:, li, 0:n], start=True, stop=True),
                                 rd=[wp, lb], wr=[ps])
                            P.op("act", lambda e: e.activation(out=dst[:, j, 0:n], in_=ps[:, 0:n], func=AF.Sigmoid,
                                                               bias=pcol(bn, d * 4 + j)), rd=[ps, pc], wr=[dst])
                    P.op("dve", lambda e: e.tensor_tensor(T0, aa[d][:, :, 0:n], bc(pcol("ka", 0, 4), 4, n), ALU.mult), rd=[aa[d], pc], wr=[tmp])
                    P.op("dve", lambda e: e.tensor_tensor(T0, T0, bc(omka[:], 4, n), ALU.add), rd=[tmp, omka], wr=[tmp])
                    P.op("dve", lambda e: e.tensor_tensor(kd[:, :, 0:n], T0, k_, ALU.mult), rd=[tmp, xs], wr=[kd])
                    for j in range(4):
                        P.op("dve", lambda e: e.tensor_tensor_scan(rv(tmp[:, 8 + j, 0:n]), rv(rmask[:, d, 0:n]), rv(sig[d][:, j, 0:n]),
                                                                   0.0, ALU.mult, ALU.add), rd=[rmask, sig[d]], wr=[tmp])
                    P.op("dve", lambda e: e.tensor_tensor(T0, T2, sig[d][:, :, 0:n], ALU.subtract), rd=[tmp, sig[d]], wr=[tmp])
                    P.op("act", lambda e: e.activation(out=T0, in_=T0, func=AF.Exp, scale=-C0), rd=[tmp], wr=[tmp])
                    P.op("dve", lambda e: e.tensor_scalar(T0, T0, -1.0, None, ALU.mult), rd=[tmp], wr=[tmp])
                    P.op("dve", lambda e: e.tensor_tensor(rfv(0), v4(kkn[:, :, 0:n]), v4(T0), ALU.mult), rd=[kkn, tmp], wr=[rf])
                    P.op("act", lambda e: e.activation(out=T0, in_=T2, func=AF.Exp, scale=-C0), rd=[tmp], wr=[tmp])
                    P.op("dve", lambda e: e.tensor_tensor(rfv(1), v4(r_), v4(T0), ALU.mult), rd=[xs, tmp], wr=[rf])
                    lastpos = 63 if d == 0 else 0
                    P.op("dve", lambda e: e.tensor_copy(plt[:, :, 0:nck], v4(T0)[:, :, :, lastpos]), rd=[tmp], wr=[plt])
                    P.dma("sp", PLs[s, d, :, :, e0 // 64:e0 // 64 + nck], plt[:, :, 0:nck], rd=[plt], wr=[L_RF[s][d][ti]])
                    P.op("act", lambda e: e.activation(out=T0, in_=T2, func=AF.Exp, scale=C0), rd=[tmp], wr=[tmp])
                    P.op("dve", lambda e: e.tensor_tensor(T1, kkn[:, :, 0:n], aa[d][:, :, 0:n], ALU.mult), rd=[kkn, aa[d]], wr=[tmp])
                    P.op("dve", lambda e: e.tensor_tensor(rfv(2), v4(T1), v4(T0), ALU.mult), rd=[tmp], wr=[rf])
                    P.op("dve", lambda e: e.tensor_tensor(rfv(3), v4(kd[:, :, 0:n]), v4(T0), ALU.mult), rd=[kd, tmp], wr=[rf])
                    for j in range(4):
                        P.dma("sp" if j % 2 == 0 else "pool", RF[s, d, j, :, e0 // 64:e0 // 64 + nck, :, :], rf[:, j, 0:nck, :, :],
                              rd=[rf], wr=[L_RF[s][d][ti]])
        P.scope_end(mark1)
        if cfg.stop_after == "rwB1":
            P.scope_end(mark)
            return

        mark2 = P.scope_begin()
        mk2 = P.sb([64, 2, 128], F32, "mk2")
        mkT = P.sb([64, 2, 64], F32, "mkT")
        for d in range(2):
            P.dma("sp", mk2[:, d, 0:64], masks_d[2 * d], wr=[mk2])
            P.dma("sp", mk2[:, d, 64:128], masks_d[2 * d + 1], wr=[mk2])
            P.dma("sp", mkT[:, d, :], masks_d[2 - 2 * d], wr=[mkT])
        lvm_f = P.sb([64, 6, 64], F32, "lvm_f")
        lvm = P.sb([64, 6, 64], BF16, "lvm")
        P.dma("sp", lvm_f[:], lvmask_d.rearrange("l i t -> i l t"), wr=[lvm_f])
        P.op("dve", lambda e: e.tensor_copy(lvm[:], lvm_f[:]), rd=[lvm_f], wr=[lvm])
        NSLOT = 2

        def alloc_slot():
            return dict(rt=P.sb([64, 8, 4, 4, 64], BF16, "rft"), vt=P.sb([64, 4, 512], BF16, "vt2"),
                        pl=P.sb([64, 8, 4], F32, "plr"), ys=P.sb([64, 8, 256], F32, "ysb"),
                        ST=P.sb([64, 8, 64], F32, "ST"), STb=P.sb([64, 8, 64], BF16, "STb"),
                        AB=P.sb([64, 8, 128], BF16, "AB"), AK=P.sb([64, 8, 128], BF16, "AK"),
                        NM=P.sb([64, 2, 6, 8, 64], BF16, "NM"), TH=P.sb([64, 8, 64], BF16, "TH"),
                        TT=P.sb([64, 8, 64], BF16, "TT"), Zs=P.sb([64, 2, 8, 64], BF16, "Zs"),
                        Wb=P.sb([64, 8, 64], BF16, "Wb"), XW=P.sb([64, 8, 128], BF16, "XW"),
                        KB=P.sb([64, 8, 2, 64], BF16, "KB"))
        slots = [alloc_slot() for _ in range(NSLOT)]
        for i_, sl_ in enumerate(slots):
            sl_["ps"] = Rot(psf.bufs[i_ * 3:(i_ + 1) * 3])
        j3 = lambda ap, c: ap.rearrange("p (j c) -> p j c", c=c)
        HS = list(range(8))

        def scan_stream(s, d, B_):
                rt, vt, pl, ys = B_["rt"], B_["vt"], B_["pl"], B_["ys"]
                ST, STb, AB, AK, NM, TH, TT = B_["ST"], B_["STb"], B_["AB"], B_["AK"], B_["NM"], B_["TH"], B_["TT"]
                Zs, Wb, XW, KB = B_["Zs"], B_["Wb"], B_["XW"], B_["KB"]
                myps = B_["ps"]
                P.op("dve", lambda e: e.memset(ST[:], 0.0), wr=[ST])
                P.op("dve", lambda e: e.memset(STb[:], 0.0), wr=[STb])
                order = list(range(len(cfg.tiles)))
                nctx_t = sum(1 for (e0, n) in cfg.tiles if e0 < CTX)
                if d == 1:
                    order = list(range(nctx_t - 1, -1, -1)) + list(range(len(cfg.tiles) - 1, nctx_t - 1, -1))
                pieces = []
                for ti in order:
                    e0_, n_ = cfg.tiles[ti]
                    offs = list(range(0, n_, 256))
                    if d == 1:
                        offs = offs[::-1]
                    pieces += [(ti, e0_ + o, min(256, n_ - o)) for o in offs]
                for (ti, e0, n) in pieces:
                    nck = n // 64
                    c0 = e0 // 64
                    for h in HS:
                        j, hl = h // 2, h % 2
                        P.dma("sp" if h % 2 == 0 else "pool", rt[:, h, 0:nck, :, :], RF[s, d, j, hl * 64:(hl + 1) * 64, c0:c0 + nck, :, :],
                              rd=[L_RF[s][d][ti]], wr=[rt])
                        P.dma("sp", pl[:, h, 0:nck], PLs[s, d, hl * 64:(hl + 1) * 64, j, c0:c0 + nck], rd=[L_RF[s][d][ti]], wr=[pl])
                    P.dma("pool", vt[:, 0:nck, :], VT[s, e0:e0 + n, :].rearrange("(c t) f -> t c f", t=64), rd=[L_VT[s][ti]], wr=[vt])
                    for ck in (range(nck) if d == 0 else range(nck - 1, -1, -1)):
                        Vh = lambda h: vt[:, ck, h * 64:(h + 1) * 64]
                        G2 = ((0, slice(0, 4)), (1, slice(4, 8)))
                        psB, psT = [myps.next(), myps.next()], myps.next()
                        for h in HS:
                            g, c = h // 4, h % 4
                            P.op("pe", lambda e: e.matmul(psB[g][0:64, c * 128:(c + 1) * 128], rt[:, h, ck, 2, :], rt[:, h, ck, 0:2, :],
                                                          start=True, stop=True), rd=[rt], wr=[psB[g]])
                            P.op("pe", lambda e: e.matmul(psT[0:64, h * 64:(h + 1) * 64], rt[:, h, ck, 0, :], rt[:, h, ck, 2, :],
                                                          start=True, stop=True), rd=[rt], wr=[psT])
                        yield
                        m2 = mk2[:, d, :].unsqueeze(1).broadcast_to([64, 4, 128])
                        for g, hs in G2:
                            P.op("dve", lambda e: e.tensor_tensor(AB[:, hs, :], j3(psB[g][0:64, 0:512], 128), m2, ALU.mult), rd=[psB[g], mk2], wr=[AB])
                        P.op("dve", lambda e: e.tensor_tensor(XW[:, :, 0:64], j3(psT[0:64, 0:512], 64),
                                                               mkT[:, d, :].unsqueeze(1).broadcast_to([64, 8, 64]), ALU.mult),
                             rd=[psT, mkT], wr=[XW])
                        psK = [myps.next(), myps.next()]
                        for h in HS:
                            g, c = h // 4, h % 4
                            P.op("pe", lambda e: e.matmul(psK[g][0:64, c * 128:(c + 1) * 128], rt[:, h, ck, 3, :], rt[:, h, ck, 0:2, :],
                                                          start=True, stop=True), rd=[rt], wr=[psK[g]])
                        yield
                        for g, hs in G2:
                            P.op("dve", lambda e: e.tensor_tensor(AK[:, hs, :], j3(psK[g][0:64, 0:512], 128), m2, ALU.mult), rd=[psK[g], mk2], wr=[AK])
                        lv4 = lvm[:].unsqueeze(2).broadcast_to([64, 6, 8, 64])
                        P.op("dve", lambda e: e.tensor_tensor(NM[:, 0], AB[:, :, 0:64].unsqueeze(1).broadcast_to([64, 6, 8, 64]), lv4, ALU.mult),
                             rd=[AB, lvm], wr=[NM])
                        P.op("dve", lambda e: e.tensor_tensor(NM[:, 1], XW[:, :, 0:64].unsqueeze(1).broadcast_to([64, 6, 8, 64]), lv4, ALU.mult),
                             rd=[XW, lvm], wr=[NM])
                        idb = ident_bf[0:64, 0:64].unsqueeze(1).broadcast_to([64, 8, 64])
                        P.op("dve", lambda e: e.tensor_tensor(TH[:], NM[:, 0, 0], idb, ALU.add), rd=[NM, ident_bf], wr=[TH])
                        P.op("dve", lambda e: e.tensor_tensor(TT[:], NM[:, 1, 0], idb, ALU.add), rd=[NM, ident_bf], wr=[TT])
                        yield
                        psW = myps.next()
                        for h in HS:
                            o_ = psW[0:64, h * 64:(h + 1) * 64]
                            P.op("pe", lambda e: e.matmul(o_, rt[:, h, ck, 0, :], STb[:, h, :], start=True, stop=False), rd=[rt, STb], wr=[psW])
                            P.op("pe", lambda e: e.matmul(o_, AK[:, h, 0:64], Vh(h), start=False, stop=True), rd=[AK, vt], wr=[psW])
                        P.op("act", lambda e: e.copy(Wb[:], j3(psW[0:64, 0:512], 64)), rd=[psW], wr=[Wb])
                        yield
                        for lvl in range(1, 6):
                            psZ, psZp = myps.next(), myps.next()
                            for h in HS:
                                P.op("pe", lambda e: e.matmul(psZ[0:64, h * 64:(h + 1) * 64], NM[:, 1, lvl, h, :], TH[:, h, :],
                                                              start=True, stop=True), rd=[NM, TH], wr=[psZ])
                                P.op("pe", lambda e: e.matmul(psZp[0:64, h * 64:(h + 1) * 64], NM[:, 0, lvl, h, :], TT[:, h, :],
                                                              start=True, stop=True), rd=[NM, TT], wr=[psZp])
                            P.op("act", lambda e: e.copy(Zs[:, 0], j3(psZ[0:64, 0:512], 64)), rd=[psZ], wr=[Zs])
                            P.op("dve", lambda e: e.tensor_copy(Zs[:, 1], j3(psZp[0:64, 0:512], 64)), rd=[psZp], wr=[Zs])
                            yield
                            psA, psBt = myps.next(), myps.next()
                            for h in HS:
                                P.op("pe", lambda e: e.matmul(psA[0:64, h * 64:(h + 1) * 64], TT[:, h, :], Zs[:, 0, h, :],
                                                              start=True, stop=True), rd=[TT, Zs], wr=[psA])
                                P.op("pe", lambda e: e.matmul(psBt[0:64, h * 64:(h + 1) * 64], TH[:, h, :], Zs[:, 1, h, :],
                                                              start=True, stop=True), rd=[TH, Zs], wr=[psBt])
                            P.op("dve", lambda e: e.tensor_tensor(TH[:], TH[:], j3(psA[0:64, 0:512], 64), ALU.add), rd=[TH, psA], wr=[TH])
                            P.op("dve", lambda e: e.tensor_tensor(TT[:], TT[:], j3(psBt[0:64, 0:512], 64), ALU.add), rd=[TT, psBt], wr=[TT])
                            yield
                        psUu = myps.next()
                        for h in HS:
                            P.op("pe", lambda e: e.matmul(psUu[0:64, h * 64:(h + 1) * 64], TH[:, h, :], Wb[:, h, :], start=True, stop=True),
                                 rd=[TH, Wb], wr=[psUu])
                        P.op("act", lambda e: e.copy(XW[:, :, 64:128], j3(psUu[0:64, 0:512], 64)), rd=[psUu], wr=[XW])
                        yield
                        Ub = lambda h: XW[:, h, 64:128]
                        psY = myps.next()
                        for h in HS:
                            o_ = psY[0:64, h * 64:(h + 1) * 64]
                            P.op("pe", lambda e: e.matmul(o_, STb[:, h, :], rt[:, h, ck, 1, :], start=True, stop=False), rd=[STb, rt], wr=[psY])
                            P.op("pe", lambda e: e.matmul(o_, Vh(h), AK[:, h, 64:128], start=False, stop=False), rd=[vt, AK], wr=[psY])
                            P.op("pe", lambda e: e.matmul(o_, Ub(h), AB[:, h, 64:128], start=False, stop=True), rd=[XW, AB], wr=[psY])
                        P.op("act", lambda e: e.copy(ys[:, :, ck * 64:(ck + 1) * 64], j3(psY[0:64, 0:512], 64)), rd=[psY], wr=[ys])
                        yield
                        psX = [myps.next(), myps.next()]
                        for h in HS:
                            g, c = h // 4, h % 4
                            for w, kind in enumerate((3, 2)):
                                P.op("pe", lambda e: e.matmul(psX[g][0:64, (c * 2 + w) * 64:(c * 2 + w + 1) * 64], rt[:, h, ck, kind, :],
                                                              ident_bf[0:64, 0:64], start=True, stop=True), rd=[rt, ident_bf], wr=[psX[g]])
                        for g, hs in G2:
                            P.op("dve", lambda e: e.tensor_copy(KB[:, hs, :, :], psX[g][0:64, 0:512].rearrange("p (j w c) -> p j w c", w=2, c=64)),
                                 rd=[psX[g]], wr=[KB])
                        yield
                        psS = myps.next()
                        for h in HS:
                            o_ = psS[0:64, h * 64:(h + 1) * 64]
                            P.op("pe", lambda e: e.matmul(o_, KB[:, h, 0, :], Vh(h), start=True, stop=False), rd=[KB, vt], wr=[psS])
                            P.op("pe", lambda e: e.matmul(o_, KB[:, h, 1, :], Ub(h), start=False, stop=True), rd=[KB, XW], wr=[psS])
                        P.op("dve", lambda e: e.tensor_tensor(ST[:], ST[:], j3(psS[0:64, 0:512], 64), ALU.add), rd=[ST, psS], wr=[ST])
                        P.op("dve", lambda e: e.tensor_tensor(ST[:], ST[:], pl[:, :, ck:ck + 1].broadcast_to([64, 8, 64]), ALU.mult),
                             rd=[ST, pl], wr=[ST])
                        P.op("act", lambda e: e.copy(STb[:], ST[:]), rd=[ST], wr=[STb])
                        yield
                    for hl in range(2):
                        P.dma("sp", YR[s, d, :, hl * 64:(hl + 1) * 64, e0:e0 + n].rearrange("j p t -> p j t"), ys[:, hl::2, 0:n],
                              rd=[ys], wr=[L_YR[s][d][ti]])

        streams = [(s_, d_) for s_ in range(NS) for d_ in range(2)]
        for g0 in range(0, len(streams), NSLOT):
            gens = [scan_stream(s_, d_, slots[i]) for i, (s_, d_) in enumerate(streams[g0:g0 + NSLOT])]
            while gens:
                for g_ in list(gens):
                    try:
                        next(g_)
                    except StopIteration:
                        gens.remove(g_)
        P.scope_end(mark2)
        if cfg.stop_after == "rwB2":
            P.scope_end(mark)
            return

        mark3 = P.scope_begin()
        ya = Rot([P.sb([128, 4, 512], F32, "ya") for _ in range(2)])
        yb_ = Rot([P.sb([128, 4, 512], F32, "yb") for _ in range(2)])
        gbl = Rot([P.sb([128, 2, 4, 512], BF16, "gbl") for _ in range(2)])
        yo = Rot([P.sb([128, 4, 512], BF16, "yor") for _ in range(2)])
        scr = {"xb": P.sb([128, 4, 512], BF16, "xbr"), "sq": P.sb([128, 4, 512], BF16, "sqr"),
               "mean": P.sb([128, 4, 512], F32, "meanr"), "rstd": P.sb([128, 4, 512], F32, "rstdr"), "rd": []}
        for s in range(NS):
            for ti, (e0, n) in enumerate(cfg.tiles):
                if last and e0 < CTX:
                    continue
                a, b, g, y = ya.next(), yb_.next(), gbl.next(), yo.next()
                P.dma("sp", a[:, :, 0:n], YR[s, 0, :, :, e0:e0 + n].rearrange("j p t -> p j t"), rd=[L_YR[s][0][ti]], wr=[a])
                P.dma("pool", b[:, :, 0:n], YR[s, 1, :, :, e0:e0 + n].rearrange("j p t -> p j t"), rd=[L_YR[s][1][ti]], wr=[b])
                for w in range(2):
                    P.dma("sp", g[:, w, :, 0:n], GB[s, w, :, :, e0:e0 + n].rearrange("j p t -> p j t"), rd=[L_GB[s][ti]], wr=[g])
                P.op("dve", lambda e: e.tensor_tensor(a[:, :, 0:n], a[:, :, 0:n], b[:, :, 0:n], ALU.add), rd=[a, b], wr=[a])
                scr["rd"] = [a]
                stats_rstd(lambda j: a[:, j, 0:n], 4, n, (blk_bf, False), RW_GN_EPS, 1.0 / 64, scr)
                A_, B_ = a[:, :, 0:n], b[:, :, 0:n]
                P.op("dve", lambda e: e.tensor_tensor(A_, A_, scr["mean"][:, :, 0:n], ALU.subtract), rd=[a, scr["mean"]], wr=[a])
                P.op("dve", lambda e: e.tensor_tensor(A_, A_, scr["rstd"][:, :, 0:n], ALU.mult), rd=[a, scr["rstd"]], wr=[a])
                P.op("dve", lambda e: e.tensor_tensor(A_, A_, bc(pcol("rlnw", 0, 4), 4, n), ALU.mult), rd=[a, pc], wr=[a])
                P.op("dve", lambda e: e.tensor_tensor(A_, A_, bc(pcol("rlnb", 0, 4), 4, n), ALU.add), rd=[a, pc], wr=[a])
                P.op("dve", lambda e: e.tensor_tensor(A_, A_, g[:, 0, :, 0:n], ALU.mult), rd=[a, g], wr=[a])
                P.op("dve", lambda e: e.tensor_tensor(y[:, :, 0:n], A_, g[:, 1, :, 0:n], ALU.add), rd=[a, g], wr=[y])
                P.dma("sp", YBR[s, 0, :, :, e0:e0 + n].rearrange("j p t -> p j t"), y[:, :, 0:n], rd=[y], wr=[L_Y[s][0][ti]])
        P.scope_end(mark3)
        P.scope_end(mark)

    modulation()
    for l in range(DEPTH):
        cast_layer(l)
    for l in range(DEPTH):
        layer_params(l)
        last = (l == DEPTH - 1)
        if cfg.stop_after == "cast":
            break
        phase_A(l)
        if cfg.stop_after == "A":
            break
        if cfg.stop_after == "conv":
            break
        if cfg.stop_after not in ("rw", "rwB1", "rwB2"):
            mlstm_branch(l, last)
        if cfg.stop_after == "ml":
            break
        rwkv_branch(l, last)
        if cfg.stop_after in ("rw", "rwB1", "rwB2", "noC"):
            break
        phase_C(l, last)

    P.barrier()
    return nc, P


def host_maps(inp, cfg, n_cores):
    NS, DEPTH = cfg.NS, cfg.DEPTH
    f = lambda a: np.ascontiguousarray(np.asarray(a, np.float32))
    shared = {
        "w_ada": f(inp["w_ada"]), "b_ada": f(inp["b_ada"]),
        "w_in": pad_w_in(f(inp["w_in"])),
        "pcols": np.stack([pack_pcols(inp, l) for l in range(DEPTH)]),
        "rw_w2": f(inp["rw_w2"]).reshape(DEPTH, 128, 512),
        "rw_a2": f(inp["rw_a2"]).reshape(DEPTH, 128, 512),
        "rw_g2": f(inp["rw_g2"]),
        "mlb": np.ascontiguousarray(np.concatenate([np.stack([f(inp["ml_ib"]).reshape(DEPTH, 8), f(inp["ml_fb"]).reshape(DEPTH, 8)], -1)] * 2, 1)),
        "p_abc": np.ascontiguousarray(np.stack([f(inp["p_a"]), f(inp["p_b"]), f(inp["p_c"])], 1)),
        "w_out": f(inp["w_out"]), "w_mlp1": f(inp["w_mlp1"]), "w_mlp2": f(inp["w_mlp2"]),
        "g_final": f(inp["g_final"]).reshape(1, D),
        "ident": np.eye(128, dtype=np.float32),
        "selh": np.ascontiguousarray(np.kron(np.eye(4, dtype=np.float32), np.ones((1, 128), np.float32))),
        "blkones": np.kron(np.eye(2, dtype=np.float32), np.ones((64, 64), np.float32)),
    }
    idx = np.arange(64)
    shared["masks"] = np.stack([(idx[:, None] < idx[None, :]), (idx[:, None] <= idx[None, :]),
                                (idx[:, None] > idx[None, :]), (idx[:, None] >= idx[None, :])]).astype(np.float32)
    lv = lambda sz: ((idx[:, None] // (2 * sz) == idx[None, :] // (2 * sz)) & (idx[:, None] // sz != idx[None, :] // sz))
    shared["lvmask"] = np.stack([lv(sz) for sz in (1, 2, 4, 8, 16, 32)]).astype(np.float32)
    maps = []
    x, c, ctx, c_ctx = f(inp["x"]), f(inp["c"]), f(inp["ctx"]), f(inp["c_ctx"])
    for i in range(n_cores):
        m = dict(shared)
        m["x"] = x[i * NS:(i + 1) * NS]
        m["ctx"] = ctx[i * NS:(i + 1) * NS]
        cv = np.zeros((3, D), np.float32)
        cv[0:NS] = c[i * NS:(i + 1) * NS]
        cv[2] = c_ctx
        m["cvec"] = cv
        maps.append(m)
    return maps


_CACHE = {}


def kernel(**inputs):
    n_cores = 8
    B = inputs["x"].shape[0]
    cfg = Cfg(NS=B // n_cores, T=inputs["x"].shape[1], CTX=inputs["ctx"].shape[1], DEPTH=inputs["w_ada"].shape[0])
    key = (cfg.NS, cfg.T, cfg.CTX, cfg.DEPTH)
    if key not in _CACHE:
        _CACHE[key] = build(cfg)
    nc, _ = _CACHE[key]
    maps = host_maps(inputs, cfg, n_cores)
    res = run_bass_kernel_spmd(nc, maps, core_ids=list(range(n_cores)))
    return np.concatenate([np.asarray(r["out"], np.float32) for r in res.results], axis=0)
```
